# Optimizing a Trainium2 kernel written in Bass

```python
import math
import jax, jax.numpy as jnp
from jax import lax
import numpy as np

D_MODEL = 1024
BATCH = 2
SEQ = 16384
DEPTH = 2

EPS = 1e-6
CHUNK = 128

RET_HEADS = 4
RET_DK = 256
RET_DV = 512
RET_QK = RET_HEADS * RET_DK
RET_V = RET_HEADS * RET_DV
ROPE_BASE = 10000.0

SSD_INNER = 2 * D_MODEL
SSD_HEADDIM = 64
SSD_HEADS = SSD_INNER // SSD_HEADDIM
SSD_GROUPS = 4
SSD_HPG = SSD_HEADS // SSD_GROUPS
SSD_STATE = 128
SSD_CONV = 4
SSD_BC = SSD_GROUPS * SSD_STATE
SSD_CONV_CH = SSD_INNER + 2 * SSD_BC

D_FF = -(-8 * D_MODEL // (3 * 256)) * 256

IN_SIZES = (RET_QK, RET_QK, RET_V, RET_V, SSD_INNER, SSD_CONV_CH, SSD_HEADS, D_MODEL, D_MODEL)
D_IN = sum(IN_SIZES)

kernel_name = "hybrid_retention_ssd_gated_block"


def rmsnorm(x, w):
    xf = x.astype(jnp.float32)
    y = xf * lax.rsqrt(jnp.mean(xf * xf, axis=-1, keepdims=True) + EPS)
    return (y * w.astype(jnp.float32)).astype(x.dtype)


def rotary(t, pos):
    half = t.shape[-1] // 2
    inv = ROPE_BASE ** (-jnp.arange(half, dtype=jnp.float32) / half)
    ang = pos.astype(jnp.float32)[:, None] * inv[None, :]
    cos = jnp.cos(ang)[None, :, None, :]
    sin = jnp.sin(ang)[None, :, None, :]
    t1, t2 = t[..., :half], t[..., half:]
    return jnp.concatenate([t1 * cos - t2 * sin, t1 * sin + t2 * cos], axis=-1)


def retention(q, k, v):
    bsz, s = q.shape[0], q.shape[1]
    nc = s // CHUNK
    log_g = jnp.log(1.0 - 2.0 ** (-5.0 - jnp.arange(RET_HEADS, dtype=jnp.float32)))
    idx = jnp.arange(CHUNK, dtype=jnp.float32)
    diff = idx[:, None] - idx[None, :]
    causal = diff >= 0
    intra = jnp.where(causal[None], jnp.exp(log_g[:, None, None] * jnp.maximum(diff, 0.0)[None]), 0.0)
    q_decay = jnp.exp(log_g[:, None] * (idx + 1.0))[None, :, :, None]
    k_decay = jnp.exp(log_g[:, None] * (CHUNK - 1.0 - idx))[None, :, :, None]
    chunk_decay = jnp.exp(log_g * CHUNK)[None, :, None, None]

    def to_chunks(t):
        return t.reshape(bsz, nc, CHUNK, RET_HEADS, t.shape[-1]).transpose(1, 0, 3, 2, 4)

    def step(state, inp):
        qc, kc, vc = inp
        sc = jnp.einsum('bhid,bhjd->bhij', qc, kc) * intra[None]
        y = jnp.einsum('bhij,bhjv->bhiv', sc, vc) + jnp.einsum('bhid,bhdv->bhiv', qc, state) * q_decay
        state = state * chunk_decay + jnp.einsum('bhjd,bhjv->bhdv', kc * k_decay, vc)
        return state, y

    init = jnp.zeros((bsz, RET_HEADS, RET_DK, RET_DV), jnp.float32)
    _, ys = lax.scan(step, init, (to_chunks(q), to_chunks(k), to_chunks(v)))
    return ys.transpose(1, 0, 3, 2, 4).reshape(bsz, s, RET_HEADS, RET_DV)


def ssd_scan(xdt, la, bm, cm):
    bsz, s = xdt.shape[0], xdt.shape[1]
    nc = s // CHUNK
    mask = jnp.tril(jnp.ones((CHUNK, CHUNK), dtype=bool))[None, :, :, None, None]

    def to_chunks(t):
        return jnp.moveaxis(t.reshape((bsz, nc, CHUNK) + t.shape[2:]), 1, 0)

    def step(state, inp):
        xc, lac, bc, cc = inp
        acs = jnp.cumsum(lac, axis=1)
        seg = acs[:, :, None] - acs[:, None, :]
        lmat = jnp.where(mask, jnp.exp(jnp.where(mask, seg, 0.0)), 0.0)
        cb = jnp.einsum('bign,bjgn->bijg', cc, bc)
        y = jnp.einsum('bijgr,bjgrp->bigrp', cb[..., None] * lmat, xc)
        y = y + jnp.einsum('bign,bgrpn->bigrp', cc, state) * jnp.exp(acs)[..., None]
        last = acs[:, -1]
        state = state * jnp.exp(last)[..., None, None] + jnp.einsum(
            'bjgn,bjgrp->bgrpn', bc, xc * jnp.exp(last[:, None] - acs)[..., None])
        return state, y

    init = jnp.zeros((bsz, SSD_GROUPS, SSD_HPG, SSD_HEADDIM, SSD_STATE), jnp.float32)
    _, ys = lax.scan(step, init, (to_chunks(xdt), to_chunks(la), to_chunks(bm), to_chunks(cm)))
    return jnp.moveaxis(ys, 0, 1).reshape(xdt.shape)


def causal_dwconv(x, w, b):
    out = lax.conv_general_dilated(
        x, w[:, None, :], window_strides=(1,), padding=[(SSD_CONV - 1, 0)],
        dimension_numbers=('NWC', 'WIO', 'NWC'), feature_group_count=x.shape[-1])
    return out + b


def hybrid_mixer(h, w_in, ret_out, conv_w, conv_b, dt_bias, a_log, d_skip, ssd_norm_w, ssd_out, w_o):
    bsz, s, _ = h.shape
    u = h @ w_in
    q, k, v, g_ret, z, xbc, dt, gate_ret, gate_ssd = jnp.split(u, list(np.cumsum(IN_SIZES)[:-1]), axis=-1)
    pos = jnp.arange(s)

    qf = rotary(q.astype(jnp.float32).reshape(bsz, s, RET_HEADS, RET_DK), pos)
    kf = rotary(k.astype(jnp.float32).reshape(bsz, s, RET_HEADS, RET_DK), pos) * (RET_DK ** -0.5)
    vf = v.astype(jnp.float32).reshape(bsz, s, RET_HEADS, RET_DV)
    yr = retention(qf, kf, vf)
    yr = yr * lax.rsqrt(jnp.mean(yr * yr, axis=-1, keepdims=True) + EPS)
    yr = (jax.nn.silu(g_ret.astype(jnp.float32)) * yr.reshape(bsz, s, RET_V)).astype(h.dtype)
    o_ret = yr @ ret_out

    xbc = jax.nn.silu(causal_dwconv(xbc, conv_w, conv_b))
    xs, bm, cm = jnp.split(xbc, [SSD_INNER, SSD_INNER + SSD_BC], axis=-1)
    xs = xs.astype(jnp.float32).reshape(bsz, s, SSD_GROUPS, SSD_HPG, SSD_HEADDIM)
    bm = bm.astype(jnp.float32).reshape(bsz, s, SSD_GROUPS, SSD_STATE)
    cm = cm.astype(jnp.float32).reshape(bsz, s, SSD_GROUPS, SSD_STATE)
    dtp = jax.nn.softplus(dt.astype(jnp.float32) + dt_bias.astype(jnp.float32))
    dtp = dtp.reshape(bsz, s, SSD_GROUPS, SSD_HPG)
    a = -jnp.exp(a_log.astype(jnp.float32)).reshape(SSD_GROUPS, SSD_HPG)
    ys = ssd_scan(xs * dtp[..., None], dtp * a, bm, cm)
    ys = ys + d_skip.astype(jnp.float32).reshape(SSD_GROUPS, SSD_HPG)[:, :, None] * xs
    ys = ys.reshape(bsz, s, SSD_INNER) * jax.nn.silu(z.astype(jnp.float32))
    ysg = ys.reshape(bsz, s, SSD_GROUPS, SSD_INNER // SSD_GROUPS)
    ysg = ysg * lax.rsqrt(jnp.mean(ysg * ysg, axis=-1, keepdims=True) + EPS)
    ys = (ysg.reshape(bsz, s, SSD_INNER) * ssd_norm_w.astype(jnp.float32)).astype(h.dtype)
    o_ssd = ys @ ssd_out

    merged = jax.nn.sigmoid(gate_ret) * o_ret + jax.nn.sigmoid(gate_ssd) * o_ssd
    return merged @ w_o


def swiglu(h, w_gate_up, w_down):
    gu = h @ w_gate_up
    gate, up = jnp.split(gu, [D_FF], axis=-1)
    return (jax.nn.silu(gate) * up) @ w_down


def setup_inputs(seed: int = 0) -> dict:
    key = jax.random.key(seed)
    ks = jax.random.split(key, 20)
    f32 = jnp.float32

    def nrm(k, shape, fan_in):
        return jax.random.normal(k, shape, f32) * (fan_in ** -0.5)

    def gain(k, shape):
        return 1.0 + 0.02 * jax.random.normal(k, shape, f32)

    dt0 = jnp.exp(jax.random.uniform(ks[6], (DEPTH, SSD_HEADS), f32, math.log(1e-3), math.log(1e-1)))
    dt_bias = dt0 + jnp.log(-jnp.expm1(-dt0))
    a_log = jnp.log(jax.random.uniform(ks[7], (DEPTH, SSD_HEADS), f32, 1.0, 16.0))
    return {
        "x": jax.random.normal(ks[0], (BATCH, SEQ, D_MODEL), f32),
        "norm_mix_w": gain(ks[1], (DEPTH, D_MODEL)),
        "w_in": nrm(ks[2], (DEPTH, D_MODEL, D_IN), D_MODEL),
        "ret_out": nrm(ks[3], (DEPTH, RET_V, D_MODEL), RET_V),
        "conv_w": nrm(ks[4], (DEPTH, SSD_CONV, SSD_CONV_CH), SSD_CONV),
        "conv_b": 0.02 * jax.random.normal(ks[5], (DEPTH, SSD_CONV_CH), f32),
        "dt_bias": dt_bias,
        "a_log": a_log,
        "d_skip": gain(ks[8], (DEPTH, SSD_HEADS)),
        "ssd_norm_w": gain(ks[9], (DEPTH, SSD_INNER)),
        "ssd_out": nrm(ks[10], (DEPTH, SSD_INNER, D_MODEL), SSD_INNER),
        "w_o": nrm(ks[11], (DEPTH, D_MODEL, D_MODEL), D_MODEL),
        "norm_ffn_w": gain(ks[12], (DEPTH, D_MODEL)),
        "w_gate_up": nrm(ks[13], (DEPTH, D_MODEL, 2 * D_FF), D_MODEL),
        "w_down": nrm(ks[14], (DEPTH, D_FF, D_MODEL), D_FF),
        "final_norm_w": gain(ks[15], (D_MODEL,)),
    }


def reference(x, norm_mix_w, w_in, ret_out, conv_w, conv_b, dt_bias, a_log, d_skip, ssd_norm_w,
              ssd_out, w_o, norm_ffn_w, w_gate_up, w_down, final_norm_w):
    h = x
    for l in range(DEPTH):
        h = h + hybrid_mixer(rmsnorm(h, norm_mix_w[l]), w_in[l], ret_out[l], conv_w[l], conv_b[l],
                             dt_bias[l], a_log[l], d_skip[l], ssd_norm_w[l], ssd_out[l], w_o[l])
        h = h + swiglu(rmsnorm(h, norm_ffn_w[l]), w_gate_up[l], w_down[l])
    return rmsnorm(h, final_norm_w)
```

```python
import os
import numpy as np
from contextlib import ExitStack
import concourse.bass as bass
import concourse.mybir as mybir
from concourse.bass_utils import run_bass_kernel_spmd

F32, BF16 = mybir.dt.float32, mybir.dt.bfloat16
AF = mybir.ActivationFunctionType
ALU = mybir.AluOpType

D = 1024
NKB = 8
DEPTH = 2
EPS = 1e-6
NH = 4
DFF = 2816
NFF = 22
DIN = 13344
C_Q, C_K, C_V, C_G, C_Z, C_X, C_B, C_C, C_DT, C_GR, C_GS = 0, 1024, 2048, 4096, 6144, 8192, 10240, 10752, 11264, 11296, 12320
NUNIT = 55
U_BCB, U_BCC = 0, 1
def U_A(h): return 2 + 3 * h
def U_V(h): return 3 + 3 * h
def U_G(h): return 4 + 3 * h
def U_X(g): return 14 + 2 * g
def U_Z(g): return 15 + 2 * g
def U_RO(u): return 22 + 3 * u
def U_SO(u): return 23 + 3 * u
def U_GATE(u): return 24 + 3 * u
def U_WO(u): return 34 + u
def U_GU(u): return 36 + u
def U_WD(f): return 47 + f
GAMMA = [1.0 - 2.0 ** (-5.0 - h) for h in range(NH)]
NST = 13

ENG = dict(pe="tensor", act="scalar", dve="vector", pool="gpsimd", sp="sync")


def _prod(xs):
    r = 1
    for v in xs:
        r *= int(v)
    return r


def _iv(ap):
    t = ap.tensor
    dims = [(int(s), int(c)) for s, c in ap.ap]
    off = int(ap.offset)
    if "DRAM" in str(ap.space).upper():
        hi = off + sum(s * (c - 1) for s, c in dims if s > 0) + 1
        return (t.name, 0, 1, off, hi)
    if "PSUM" in str(ap.space).upper():
        return (t.name, 0, 128, 0, 1 << 30)
    rows = _prod(list(t.shape)[1:])
    p0, f0 = off // rows, off % rows
    fhi = f0 + sum(s * (c - 1) for s, c in dims[1:] if s > 0) + 1
    return (t.name, p0, p0 + dims[0][1], f0, fhi)


class _Stop(Exception):
    pass


class Prog:
    def __init__(self, nc, es):
        self.nc, self.es = nc, es
        self.ops = {e: [] for e in ENG}
        self.cnt = {e: 0 for e in ENG}
        self.sems = {}
        self.semval = {}
        self.waited = {e: {} for e in ENG}
        self.acc = {}
        self.nsem = 0
        self.log = [] if os.environ.get('KDUMP') else None

    def sem(self, key):
        if key not in self.sems:
            self.nsem += 1
            self.sems[key] = self.es.enter_context(self.nc.semaphore("s%d" % self.nsem))
        return self.sems[key]

    def _need(self, eng, reads, writes):
        need = {}

        def add(tok):
            k, v = tok
            if k == "pe" and eng == "pe":
                return
            if v > need.get(k, 0):
                need[k] = v

        for ap in reads:
            n, p0, p1, f0, f1 = _iv(ap)
            for r in self.acc.get(n, ()):
                if r[0] == "w" and r[1] < p1 and p0 < r[2] and r[3] < f1 and f0 < r[4]:
                    add(r[5])
        for ap in writes:
            n, p0, p1, f0, f1 = _iv(ap)
            for r in self.acc.get(n, ()):
                if r[1] < p1 and p0 < r[2] and r[3] < f1 and f0 < r[4]:
                    add(r[5])
        out = []
        for k, v in need.items():
            if k.startswith("d:"):
                v = max(v, self.semval[k])
            if self.waited[eng].get(k, 0) < v:
                self.waited[eng][k] = v
                out.append((k, v))
        return out

    def _rec(self, kind, ap, tok):
        n, p0, p1, f0, f1 = _iv(ap)
        L = self.acc.setdefault(n, [])
        if kind == "w":
            L[:] = [r for r in L if not (p0 <= r[1] and r[2] <= p1 and f0 <= r[3] and r[4] <= f1)]
        else:
            L[:] = [r for r in L if not (r[0] == "r" and r[5][0] == tok[0] and p0 <= r[1] and r[2] <= p1
                                         and f0 <= r[3] and r[4] <= f1)]
        L.append((kind, p0, p1, f0, f1, tok))

    def op(self, eng, fn, reads=(), writes=(), inc=True):
        self.sem(eng)
        psr = [a for a in reads if "PSUM" in str(a.space).upper()]
        if psr:
            reads = [a for a in reads if "PSUM" not in str(a.space).upper()]
            writes = list(writes) + psr
        waits = self._need(eng, reads, writes)
        idx = self.cnt[eng] + 1
        if inc:
            self.cnt[eng] = idx
        tok = (eng, idx)
        for ap in reads:
            self._rec("r", ap, tok)
        for ap in writes:
            self._rec("w", ap, tok)
        self.ops[eng].append((waits, fn, (eng, 1) if inc else None))
        if self.log is not None:
            self.log.append((eng, idx, inc, waits, [_iv(a) for a in reads], [_iv(a) for a in writes]))

    def dma(self, out, in_, key, eng="sp"):
        k = "d:" + key
        self.sem(k)
        waits = self._need(eng, [in_], [out])
        v = self.semval.get(k, 0) + 16
        self.semval[k] = v
        tok = (k, v)
        self._rec("r", in_, tok)
        self._rec("w", out, tok)
        self.ops[eng].append((waits, lambda e: e.dma_start(out=out, in_=in_), (k, 16)))
        if self.log is not None:
            self.log.append((eng, tok, True, waits, [_iv(in_)], [_iv(out)]))

    def finish(self):
        for k, v in self.semval.items():
            if self.waited["sp"].get(k, 0) < v:
                self.ops["sp"].append(([(k, v)], None, None))
        block = self.es.enter_context(self.nc.Block())
        for eng, attr in ENG.items():
            ops = self.ops[eng]
            if not ops:
                continue

            def body(e, ops=ops):
                for waits, fn, inc in ops:
                    for k, v in waits:
                        e.wait_ge(self.sems[k], v)
                    if fn is None:
                        continue
                    ins = fn(e)
                    if inc is not None:
                        ins.then_inc(self.sems[inc[0]], inc[1])

            getattr(block, attr)(body)


def mm(p, out, lhsT, rhs, start, stop, inc=None):
    if inc is None:
        inc = stop
    p.op("pe", lambda e: e.matmul(out, lhsT=lhsT, rhs=rhs, start=start, stop=stop), [lhsT, rhs], [out], inc=inc)


def tr(p, out, in_, ident, inc=True):
    p.op("pe", lambda e: e.transpose(out, in_, ident), [in_, ident], [out], inc=inc)


def act(p, out, in_, func, bias=None, scale=None, accum=None):
    kw = {}
    rd = [in_]
    if bias is not None:
        kw["bias"] = bias
        if not isinstance(bias, (int, float)):
            rd.append(bias)
    if scale is not None:
        kw["scale"] = scale
        if not isinstance(scale, (int, float)):
            rd.append(scale)
    wr = [out]
    if accum is not None:
        kw["accum_out"] = accum
        wr.append(accum)
    p.op("act", lambda e: e.activation(out=out, in_=in_, func=func, **kw), rd, wr)


def tt(p, eng, out, a, b, op):
    p.op(eng, lambda e: e.tensor_tensor(out=out, in0=a, in1=b, op=op), [a, b], [out])


def ts(p, eng, out, a, s1, op0, s2=None, op1=None):
    rd = [a] + [s for s in (s1, s2) if s is not None and not isinstance(s, (int, float))]
    if op1 is None:
        p.op(eng, lambda e: e.tensor_scalar(out=out, in0=a, scalar1=s1, scalar2=None, op0=op0), rd, [out])
    else:
        p.op(eng, lambda e: e.tensor_scalar(out=out, in0=a, scalar1=s1, scalar2=s2, op0=op0, op1=op1), rd, [out])


def stt(p, out, in0, scalar, in1, op0, op1):
    rd = [in0, in1] + ([] if isinstance(scalar, (int, float)) else [scalar])
    p.op("dve", lambda e: e.scalar_tensor_tensor(out=out, in0=in0, scalar=scalar, in1=in1, op0=op0, op1=op1), rd, [out])


def cp(p, eng, out, in_):
    if eng == "act":
        p.op("act", lambda e: e.copy(out=out, in_=in_), [in_], [out])
    else:
        p.op(eng, lambda e: e.tensor_copy(out=out, in_=in_), [in_], [out])


def bc(ap, axis, shape):
    return ap.unsqueeze(axis).to_broadcast(list(shape))


def build(NT, T, phases, fused=False):
    NCH = T // 128
    NTILE = NT // T
    assert T % 128 == 0 and NT % T == 0
    nc = bass.Bass("TRN2", target_bir_lowering=False)
    es = ExitStack()
    p = Prog(nc, es)
    layers = sorted(set(l for _, l in phases))

    def din(name, shape, dt=F32):
        return nc.dram_tensor(name, list(shape), dt, kind="ExternalInput").ap()

    def dout(name, shape, dt=F32):
        return nc.dram_tensor(name, list(shape), dt, kind="ExternalOutput").ap()

    def dint(name, shape, dt=F32):
        return nc.dram_tensor(name, list(shape), dt).ap()

    hin = din("hin", [NKB, 128, NT])
    halo_in = din("halo", [NKB, 128, 4])
    cst_d = din("cst", [128, 1540])
    cs_d = din("cs", [2, 128, NT])
    cf_d = din("cf", [128, 36])
    fnw_d = din("fnw", [128, 8])
    W = {}
    for l in layers:
        W[l] = dict(w_in=din("w_in%d" % l, [D, DIN]), nm=din("nm%d" % l, [128, 32]), cw=din("cw%d" % l, [128, 120]),
                    hp=din("hp%d" % l, [128, 96]), wu=dint("wu%d" % l, [NUNIT, 128, 4096], BF16))
        if any(k == "p2" and ll == l for k, ll in phases):
            W[l].update(ret_out=din("ret_out%d" % l, [2048, D]), ssd_out=din("ssd_out%d" % l, [2048, D]),
                        w_o=din("w_o%d" % l, [D, D]), w_gu=din("w_gu%d" % l, [D, 2 * DFF]),
                        w_down=din("w_down%d" % l, [DFF, D]))
    if not fused:
        (kind, lay), = phases
        if kind == "p1":
            stloc = dout("stloc", [NST, 128, 512])
        else:
            stall = din("stall", [4, NST, 128, 512])
            hout = dout("hout", [NKB, 128, NT])

    sb = lambda name, shape, dt=F32: es.enter_context(nc.sbuf_tensor(name, list(shape), dt))
    ps = lambda name, shape, dt=F32: es.enter_context(nc.psum_tensor(name, list(shape), dt))

    cst = sb("cst_sb", [128, 1540])
    identb = sb("identb", [128, 128], BF16)
    onesb = sb("onesb", [128, 128], BF16)
    triTb = sb("triTb", [128, 128], BF16)
    triUb = sb("triUb", [128, 128], BF16)
    lahl = sb("lahl", [128, NCH, 2, 32], BF16)
    rlab = sb("rlab", [128, 2, 8, 128], BF16)
    xD = sb("xD", [128, 512], BF16)
    cossin = sb("cossin", [128, 2, T])
    cf = sb("cf_sb", [128, 36])
    fnw = sb("fnw_sb", [128, 8])
    nm = {l: sb("nm_sb%d" % l, [128, 32]) for l in layers}
    cw = {l: sb("cw_sb%d" % l, [128, 24, 5]) for l in layers}
    hp = {l: sb("hp_sb%d" % l, [128, 96]) for l in layers}
    atile = {l: sb("atile%d" % l, [128, 32]) for l in layers}
    wdt = {l: sb("wdt%d" % l, [128, NKB, 32], BF16) for l in layers}
    hT = sb("hT", [128, NKB, T])
    hnT = sb("hnT", [128, NKB, T], BF16)
    rbc = sb("rbc", [128, T])
    lnv = sb("lnv", [128, T])
    hnTh = sb("hnTh", [128, NKB, 4], BF16)
    haloT = sb("haloT", [128, NKB, 4])
    small = sb("small", [128, 64])
    NW = 4
    wst = [sb("wst%d" % i, [128, 4096], BF16) for i in range(NW)]
    FS = sb("FS", [128, 3 * T + 4 + NCH * 256 + 128 + 1024 + 1024 + 3 * T + 4])
    BS = sb("BS", [128, 8 * T + 4 * T + NCH * 512 + 1536 + 128 + 2048 + 512 + 64], BF16)
    U = sb("U", [128, 24 * T], BF16)
    hist = sb("hist", [128, 24, 4])
    Rf = sb("Rf", [128, NH, 2, 512])
    Rbf = sb("Rbf", [128, NH, 2, 512], BF16)
    Sf = sb("Sf", [128, 4, 512])
    Sbf = sb("Sbf", [128, 4, 512], BF16)
    LAt = sb("LAt", [128, 32])
    pA = [ps("pA0", [128, 512]), ps("pA1", [128, 512])]
    pY = ps("pY", [128, 512])
    pS = [ps("pS0", [128, 512]), ps("pS1", [128, 512])]
    pM = ps("pM", [128, 512])
    pT = [ps("pT0", [128, 1024], BF16), ps("pT1", [128, 1024], BF16)]

    identf = cst[:, 0:128]
    triT = cst[:, 128:256]
    triU = cst[:, 256:384]
    onesf = cst[:, 384:512]
    intraT = cst[:, 512:1024].rearrange("p (h i) -> p h i", h=4)
    qdec = cst[:, 1024:1536].rearrange("p (h i) -> p h i", h=4)
    kdec = cst[:, 1536:1540]
    _o2 = 3 * T + 4 + NCH * 256 + 128 + 1024 + 1024
    _tmp = [(FS[:, 0:T], FS[:, T:2 * T], FS[:, 2 * T:3 * T + 4]),
            (FS[:, _o2:_o2 + T], FS[:, _o2 + T:_o2 + 2 * T], FS[:, _o2 + 2 * T:_o2 + 3 * T + 4])]
    _tsel = [0]

    def nxt():
        _tsel[0] ^= 1
        return _tmp[_tsel[0]]
    ra, rb, xraw = _tmp[0]
    o = 3 * T + 4
    dtb = FS[:, o:o + NCH * 256].rearrange("p (c f) -> p c f", c=NCH)
    o += NCH * 256
    cbm = FS[:, o:o + 128]
    o += 128
    rla = FS[:, o:o + 1024].rearrange("p (r i) -> p r i", r=8)
    o += 1024
    ubuf = FS[:, o:o + 512]
    vbuf = FS[:, o + 512:o + 1024]
    BT = BS[:, 0:4 * T].rearrange("p (g t) -> p g t", g=4)
    CT = BS[:, 4 * T:8 * T].rearrange("p (g t) -> p g t", g=4)
    o = 8 * T
    qT = BS[:, o:o + 2 * T].rearrange("p (b t) -> p b t", b=2)
    kT = BS[:, o + 2 * T:o + 4 * T].rearrange("p (b t) -> p b t", b=2)
    r2 = o + 4 * T
    vtok = BS[:, r2:r2 + NCH * 512].rearrange("p (c f) -> p c f", c=NCH)
    r2 += NCH * 512
    sT = BS[:, r2:r2 + 128]
    qTs = BS[:, r2 + 128:r2 + 384].rearrange("p (b i) -> p b i", b=2)
    yn = BS[:, r2 + 384:r2 + 896]
    ktok = BS[:, r2 + 896:r2 + 1152]
    gT = U[:, 16 * T:20 * T].rearrange("p (b t) -> p b t", b=4)
    xcT = BS[:, o:o + 4 * T].rearrange("p (b t) -> p b t", b=4)
    s2 = o + 4 * T
    sz = BS[:, s2:s2 + NCH * 512].rearrange("p (c f) -> p c f", c=NCH)
    s2 += NCH * 512
    xtok = BS[:, s2:s2 + 512]
    xdt = BS[:, s2 + 512:s2 + 1024]
    xdts = BS[:, s2 + 1024:s2 + 1536]
    s2 += 1536
    Btok = BS[:, s2:s2 + 128]
    s2 += 128
    esb = BS[:, s2:s2 + 1024].rearrange("p (r i) -> p r i", r=8)
    MT = BS[:, s2 + 1024:s2 + 2048].rearrange("p (r i) -> p r i", r=8)
    s2 += 2048
    ysn = BS[:, s2:s2 + 512]
    yrT = U[:, 0:16 * T].rearrange("p (b t) -> p b t", b=16)
    ysT = yrT
    mTb = U[:, 16 * T:24 * T].rearrange("p (b t) -> p b t", b=8)
    actT = U[:, 0:22 * T].rearrange("p (b t) -> p b t", b=22)
    stage = hT[:, :, :].rearrange("p a b -> p (a b)")
    assert NKB * T >= 4096

    p.dma(cst[:], cst_d[:, :], "cst")
    p.dma(cf[:], cf_d[:, :], "cf")
    p.dma(fnw[:], fnw_d[:, :], "fnw")
    cp(p, "dve", identb[:], identf)
    cp(p, "dve", onesb[:], onesf)
    cp(p, "dve", triTb[:], triT)
    cp(p, "dve", triUb[:], triU)
    for l in layers:
        p.dma(nm[l][:], W[l]["nm"][:, :], "nm%d" % l)
        p.dma(cw[l][:].rearrange("p a b -> p (a b)"), W[l]["cw"][:, :], "cw%d" % l)
        p.dma(hp[l][:], W[l]["hp"][:, :], "hp%d" % l)
        act(p, atile[l][:], hp[l][:, 32:64], AF.Exp)
        ts(p, "dve", atile[l][:], atile[l][:], -1.0, ALU.mult)

    KST = int(os.environ.get('KSTAGE', '99'))
    conv_rr = [0]

    def convert_piece(dst, src, scale, mul=None):
        e = ("dve", "act", "pool")[conv_rr[0] % 3]
        conv_rr[0] += 1
        if scale is None:
            cp(p, e, dst, src)
        elif e == "act":
            act(p, dst, src, AF.Copy, scale=scale)
            if mul is not None:
                ts(p, "dve", dst, dst, mul, ALU.mult)
        elif e == "dve":
            if mul is None:
                ts(p, "dve", dst, src, scale, ALU.mult)
            else:
                ts(p, "dve", dst, src, scale, ALU.mult, mul, ALU.mult)
        else:
            ts(p, "pool", dst, src, scale, ALU.mult, 1.0 if mul is None else mul, ALU.mult)

    def unit_pieces(l, u):
        Wl = W[l]
        nmx = lambda kb: nm[l][:, kb:kb + 1]
        nfx = lambda kb: nm[l][:, 8 + kb:9 + kb]
        nsx = lambda kb: nm[l][:, 16 + kb:17 + kb]
        out = []
        rows = lambda kb: slice(kb * 128, (kb + 1) * 128)
        if u in (U_BCB, U_BCC):
            c0 = C_B if u == U_BCB else C_C
            for kb in range(8):
                out.append((Wl["w_in"][rows(kb), c0:c0 + 512], kb * 512, 512, nmx(kb), None))
        elif 2 <= u < 14:
            h, k = divmod(u - 2, 3)
            for kb in range(8):
                if k == 0:
                    out.append((Wl["w_in"][rows(kb), C_Q + h * 256:C_Q + h * 256 + 256], kb * 512, 256, nmx(kb), None))
                    out.append((Wl["w_in"][rows(kb), C_K + h * 256:C_K + h * 256 + 256], kb * 512 + 256, 256, nmx(kb), 1.0 / 16))
                else:
                    c0 = (C_V if k == 1 else C_G) + h * 512
                    out.append((Wl["w_in"][rows(kb), c0:c0 + 512], kb * 512, 512, nmx(kb), None))
        elif 14 <= u < 22:
            g, k = divmod(u - 14, 2)
            c0 = (C_X if k == 0 else C_Z) + g * 512
            for kb in range(8):
                out.append((Wl["w_in"][rows(kb), c0:c0 + 512], kb * 512, 512, nmx(kb), None))
        elif 22 <= u < 34:
            uu, k = divmod(u - 22, 3)
            if k == 0:
                for kb in range(16):
                    out.append((Wl["ret_out"][rows(kb), uu * 256:uu * 256 + 256], kb * 256, 256, None, None))
            elif k == 1:
                for kb in range(16):
                    out.append((Wl["ssd_out"][rows(kb), uu * 256:uu * 256 + 256], kb * 256, 256, nsx(kb), None))
            else:
                for kb in range(8):
                    out.append((Wl["w_in"][rows(kb), C_GR + uu * 256:C_GR + uu * 256 + 256], kb * 512, 256, nmx(kb), None))
                    out.append((Wl["w_in"][rows(kb), C_GS + uu * 256:C_GS + uu * 256 + 256], kb * 512 + 256, 256, nmx(kb), None))
        elif u in (34, 35):
            uu = u - 34
            for kb in range(8):
                out.append((Wl["w_o"][rows(kb), uu * 512:uu * 512 + 512], kb * 512, 512, None, None))
        elif 36 <= u < 47:
            uu = u - 36
            for kb in range(8):
                out.append((Wl["w_gu"][rows(kb), uu * 256:uu * 256 + 256], kb * 512, 256, nfx(kb), None))
                out.append((Wl["w_gu"][rows(kb), DFF + uu * 256:DFF + uu * 256 + 256], kb * 512 + 256, 256, nfx(kb), None))
        else:
            f = u - 47
            for kb in range(NFF):
                out.append((Wl["w_down"][rows(kb), f * 128:f * 128 + 128], kb * 128, 128, None, None))
        return out

    stages = [(stage, "stage0"), (FS[:, 0:4096], "stage1"), (Rf[:].rearrange("p a b c -> p (a b c)"), "stage2")]

    def convert_layer(l, units):
        n = len(units)
        for i in range(n + 2):
            if i < n:
                st, key = stages[i % 3]
                for q, (src, dc, w, sc, mul) in enumerate(unit_pieces(l, units[i])):
                    p.dma(st[:, dc:dc + w], src, key, eng=("sp" if q % 2 == 0 else "act"))
            j = i - 2
            if 0 <= j < n:
                st, key = stages[j % 3]
                wb = wst[j % NW]
                pcs = unit_pieces(l, units[j])
                for src, dc, w, sc, mul in pcs:
                    convert_piece(wb[:, dc:dc + w], st[:, dc:dc + w], sc, mul)
                hi = max(dc + w for _, dc, w, _, _ in pcs)
                p.dma(W[l]["wu"][units[j], :, 0:hi], wb[:, 0:hi], "wst%d" % (j % NW))
        for kb in range(8):
            p.dma(stage[:, kb * 32:(kb + 1) * 32], W[l]["w_in"][kb * 128:(kb + 1) * 128, C_DT:C_DT + 32], "stage0")
        for kb in range(8):
            ts(p, "dve", wdt[l][:, kb, :], stage[:, kb * 32:(kb + 1) * 32], nm[l][:, kb:kb + 1], ALU.mult)

    P1_UNITS = [U_BCB] + [u for h in range(4) for u in (U_A(h), U_V(h))] + [U_X(g) for g in range(4)]
    P2_UNITS = ([U_BCB, U_BCC] + [u for h in range(4) for u in (U_A(h), U_V(h), U_G(h))]
                + [u for g in range(4) for u in (U_X(g), U_Z(g))]
                + [u for uu in range(4) for u in (U_RO(uu), U_SO(uu), U_GATE(uu))]
                + [U_WO(0), U_WO(1)] + [U_GU(i) for i in range(11)] + [U_WD(f) for f in range(8)])
    need_units = {}
    for kind, l in phases:
        s = need_units.setdefault(l, set())
        s.update(P1_UNITS if kind == "p1" else P2_UNITS)
    for l in layers:
        if KST >= 1:
            convert_layer(l, sorted(need_units[l])[:(2 if KST == 1 else 99)])

    class Stream:
        def __init__(self, l, seq):
            self.l, self.seq, self.i, self.slots = l, seq, 0, []

        def next(self, expect):
            while len(self.slots) < min(len(self.seq), self.i + NW - 1):
                u = self.seq[len(self.slots)]
                slot = wslot[0] % NW
                wslot[0] += 1
                p.dma(wst[slot][:], W[self.l]["wu"][u, :, :], "wst%d" % slot)
                self.slots.append(slot)
            assert self.seq[self.i] == expect, (self.seq[self.i], expect)
            w = wst[self.slots[self.i]]
            self.i += 1
            return w

    wslot = [0]

    def rmsnorm_to_hnT(l_unused=None):
        act(p, hnT[:], hT[:], AF.Square)
        for kb in range(NKB):
            mm(p, pM[:, 0:T], onesb[:], hnT[:, kb, :], kb == 0, kb == NKB - 1)
        act(p, lnv[:], pM[:, 0:T], AF.Ln, scale=1.0 / D, bias=EPS)
        act(p, rbc[:], lnv[:], AF.Exp, scale=-0.5)
        tt(p, "dve", hnT[:], hT[:], bc(rbc[:], 1, [128, NKB, T]), ALU.mult)

    def halo_norm(halo_src):
        p.dma(haloT[:], halo_src.rearrange("b p t -> p b t"), "haloT")
        act(p, hnTh[:], haloT[:], AF.Square)
        for kb in range(NKB):
            mm(p, pM[:, 256:260], onesb[:], hnTh[:, kb, :], kb == 0, kb == NKB - 1)
        act(p, small[:, 0:4], pM[:, 256:260], AF.Ln, scale=1.0 / D, bias=EPS)
        act(p, small[:, 4:8], small[:, 0:4], AF.Exp, scale=-0.5)
        tt(p, "dve", hnTh[:], haloT[:], bc(small[:, 4:8], 1, [128, NKB, 4]), ALU.mult)

    def proj_fm(w, c0, pa):
        for kb in range(NKB):
            mm(p, pa[:, 0:T], w[:, kb * 512 + c0:kb * 512 + c0 + 128], hnT[:, kb, :], kb == 0, kb == NKB - 1)

    def proj_tm(w, c, pa, width=512, c0=0):
        for kb in range(NKB):
            mm(p, pa[:, 0:width], hnT[:, kb, c * 128:(c + 1) * 128], w[:, kb * 512 + c0:kb * 512 + c0 + width],
               kb == 0, kb == NKB - 1)

    def conv_block(l, t, w, c0, blk, pa, out_bf):
        ra, rb, xraw = nxt()
        if t == 0:
            for kb in range(NKB):
                mm(p, pM[:, 260:264], w[:, kb * 512 + c0:kb * 512 + c0 + 128], hnTh[:, kb, :], kb == 0, kb == NKB - 1)
            cp(p, "act", xraw[:, 0:3], pM[:, 260:263])
        else:
            cp(p, "pool", xraw[:, 0:3], hist[:, blk, 0:3])
        cp(p, "act", xraw[:, 3:3 + T], pa[:, 0:T])
        cp(p, "pool", hist[:, blk, 0:3], xraw[:, T:T + 3])
        act(p, ra, pa[:, 0:T], AF.Identity, scale=cw[l][:, blk, 3:4], bias=cw[l][:, blk, 4:5])
        stt(p, rb, xraw[:, 0:T], cw[l][:, blk, 0:1], ra, ALU.mult, ALU.add)
        stt(p, ra, xraw[:, 1:1 + T], cw[l][:, blk, 1:2], rb, ALU.mult, ALU.add)
        stt(p, rb, xraw[:, 2:2 + T], cw[l][:, blk, 2:3], ra, ALU.mult, ALU.add)
        act(p, out_bf, rb, AF.Silu)

    def dt_prep(l, c, full):
        d = dtb[:, c, :]
        KD = int(os.environ.get('KDT', '63'))
        if KD & 1:
            for kb in range(NKB):
                mm(p, pM[:, 0:32], hnT[:, kb, c * 128:(c + 1) * 128], wdt[l][:, kb, :], kb == 0, kb == NKB - 1)
            tt(p, "dve", d[:, 0:32], pM[:, 0:32], hp[l][:, 0:32], ALU.add)
        if KD & 2:
            act(p, d[:, 32:64], d[:, 0:32], AF.Exp)
            act(p, d[:, 64:96], d[:, 32:64], AF.Ln, scale=1.0, bias=1.0)
            tt(p, "dve", d[:, 96:128], d[:, 64:96], atile[l][:], ALU.mult)
        if KD & 4:
            cp(p, "dve", lahl[:, c, 0, :], d[:, 96:128])
            tt(p, "dve", lahl[:, c, 1, :], d[:, 96:128], lahl[:, c, 0, :], ALU.subtract)
            for k, lt in enumerate((triTb, triUb, onesb)):
                mm(p, pM[:, 32 + 32 * k:64 + 32 * k], lt[:], lahl[:, c, 0, :], True, False)
                mm(p, pM[:, 32 + 32 * k:64 + 32 * k], lt[:], lahl[:, c, 1, :], False, True)
        if KD & 8:
            act(p, d[:, 128:224], pM[:, 32:128], AF.Exp)
        if KD & 32:
            tt(p, "dve", d[:, 224:256], d[:, 64:96], d[:, 160:192], ALU.mult)
        if (KD & 16) and not full:
            tt(p, "dve", LAt[:], LAt[:], pM[:, 96:128], ALU.add)

    def tile_body(l, t, full, src, dst, ws, last_layer):
        tsl = slice(t * T, (t + 1) * T)
        p.dma(hT[:], src[:, :, tsl].rearrange("b p t -> p b t"), "hT")
        p.dma(cossin[:], cs_d[:, :, tsl].rearrange("a p t -> p a t"), "cossin")
        rmsnorm_to_hnT()
        if KST < 5:
            raise _Stop()
        cosT, sinT = cossin[:, 0, :], cossin[:, 1, :]
        for c in range(NCH):
            dt_prep(l, c, full)
        if KST < 6:
            raise _Stop()
        w = ws.next(U_BCB)
        for g in range(4):
            proj_fm(w, g * 128, pA[g % 2])
            conv_block(l, t, w, g * 128, 16 + g, pA[g % 2], BT[:, g, :])
        if full:
            w = ws.next(U_BCC)
            for g in range(4):
                proj_fm(w, g * 128, pA[g % 2])
                conv_block(l, t, w, g * 128, 20 + g, pA[g % 2], CT[:, g, :])
        if KST < 7:
            raise _Stop()
        for h in range(NH):
            w = ws.next(U_A(h))
            for qk in ((0, 1) if full else (1,)):
                dstT = qT if qk == 0 else kT
                ra, rb, _ = nxt()
                proj_fm(w, qk * 256, pA[0])
                proj_fm(w, qk * 256 + 128, pA[1])
                tt(p, "dve", ra, pA[0][:, 0:T], cosT, ALU.mult)
                tt(p, "dve", rb, pA[1][:, 0:T], sinT, ALU.mult)
                tt(p, "pool", dstT[:, 0, :], ra, rb, ALU.subtract)
                tt(p, "dve", ra, pA[0][:, 0:T], sinT, ALU.mult)
                tt(p, "dve", rb, pA[1][:, 0:T], cosT, ALU.mult)
                tt(p, "pool", dstT[:, 1, :], ra, rb, ALU.add)
            w = ws.next(U_V(h))
            for c in range(NCH):
                proj_tm(w, c, pA[c % 2])
                cp(p, "act", vtok[:, c, :], pA[c % 2][:, :])
            if full:
                w = ws.next(U_G(h))
                for fb in range(4):
                    proj_fm(w, fb * 128, pA[fb % 2])
                    act(p, gT[:, fb, :], pA[fb % 2][:, 0:T], AF.Silu)
            for c in range(NCH):
                cs_ = slice(c * 128, (c + 1) * 128)
                if full:
                    for b in range(2):
                        mm(p, pM[:, 128:256], kT[:, b, cs_], qT[:, b, cs_], b == 0, b == 1)
                    tt(p, "dve", sT, pM[:, 128:256], intraT[:, h, :], ALU.mult)
                    tt(p, "pool", qTs, qT[:, :, cs_], bc(qdec[:, h, :], 1, [128, 2, 128]), ALU.mult)
                    mm(p, pY[:, :], sT, vtok[:, c, :], True, False)
                    mm(p, pY[:, :], qTs[:, 0, :], Rbf[:, h, 0, :], False, False)
                    mm(p, pY[:, :], qTs[:, 1, :], Rbf[:, h, 1, :], False, True)
                    act(p, ysn, pY[:, :], AF.Square, accum=small[:, 8:9])
                    act(p, small[:, 9:10], small[:, 8:9], AF.Ln, scale=1.0 / 512, bias=EPS)
                    act(p, small[:, 10:11], small[:, 9:10], AF.Exp, scale=-0.5)
                    ts(p, "dve", yn, pY[:, :], small[:, 10:11], ALU.mult)
                    for fb in range(4):
                        tr(p, pT[0][:, fb * 128:(fb + 1) * 128], yn[:, fb * 128:(fb + 1) * 128], identb[:], inc=(fb == 3))
                    tt(p, "dve", yrT[:, h * 4:(h + 1) * 4, cs_], pT[0][:, 0:512].rearrange("p (a b) -> p a b", a=4),
                       gT[:, :, cs_], ALU.mult)
                for b in range(2):
                    tr(p, pT[1][:, b * 128:(b + 1) * 128], kT[:, b, cs_], identb[:], inc=(b == 1))
                ts(p, "dve", ktok, pT[1][:, 0:256], kdec[:, h:h + 1], ALU.mult)
                for b in range(2):
                    mm(p, pS[b][:, :], ktok[:, b * 128:(b + 1) * 128], vtok[:, c, :], True, True)
                for b in range(2):
                    stt(p, Rf[:, h, b, :], Rf[:, h, b, :], GAMMA[h] ** 128, pS[b][:, :], ALU.mult, ALU.add)
                    if full:
                        cp(p, "act", Rbf[:, h, b, :], Rf[:, h, b, :])
        if KST < 8:
            raise _Stop()
        if full:
            out_proj(l, ws, first=True)
        if KST < 9:
            raise _Stop()
        for g in range(4):
            w = ws.next(U_X(g))
            for fb in range(4):
                proj_fm(w, fb * 128, pA[fb % 2])
                conv_block(l, t, w, fb * 128, g * 4 + fb, pA[fb % 2], xcT[:, fb, :])
            if full:
                w = ws.next(U_Z(g))
                for c in range(NCH):
                    proj_tm(w, c, pA[c % 2])
                    act(p, sz[:, c, :], pA[c % 2][:, :], AF.Silu)
            hs = slice(8 * g, 8 * g + 8)
            for c in range(NCH):
                cs_ = slice(c * 128, (c + 1) * 128)
                d = dtb[:, c, :]
                for fb in range(4):
                    tr(p, pT[0][:, fb * 128:(fb + 1) * 128], xcT[:, fb, cs_], identb[:], inc=(fb == 3))
                xps = pT[0][:, 0:512].rearrange("p (r q) -> p r q", r=8)
                r3 = lambda ap: ap.rearrange("p (r q) -> p r q", r=8)
                tt(p, "dve", r3(xdts), xps, bc(d[:, 224 + 8 * g:232 + 8 * g], 2, [128, 8, 64]), ALU.mult)
                tr(p, pT[1][:, 256:384], BT[:, g, cs_], identb[:])
                cp(p, "act", Btok, pT[1][:, 256:384])
                if full:
                    cp(p, "act", xtok, pT[0][:, 0:512])
                    tt(p, "dve", r3(xdt), xps, bc(d[:, 64 + 8 * g:72 + 8 * g], 2, [128, 8, 64]), ALU.mult)
                    mm(p, pM[:, 256:384], BT[:, g, cs_], CT[:, g, cs_], True, True)
                    tt(p, "dve", cbm, pM[:, 256:384], triT, ALU.mult)
                    for k in range(2):
                        tt(p, "pool", rlab[:, k, :, :], bc(triTb[:], 1, [128, 8, 128]),
                           bc(lahl[:, c, k, 8 * g:8 * g + 8], 2, [128, 8, 128]), ALU.mult)
                    for hf in range(2):
                        for k in range(2):
                            mm(p, pA[hf][:, :], triUb[:], rlab[:, k, hf * 4:(hf + 1) * 4, :].rearrange("p r i -> p (r i)"), k == 0, k == 1)
                        act(p, esb[:, hf * 4:(hf + 1) * 4, :].rearrange("p r i -> p (r i)"), pA[hf][:, :], AF.Exp)
                    tt(p, "dve", MT, esb, bc(cbm, 1, [128, 8, 128]), ALU.mult)
                    tt(p, "pool", r3(xD[:]), r3(xtok), bc(hp[l][:, 64 + 8 * g:72 + 8 * g], 2, [128, 8, 64]), ALU.mult)
                    mm(p, pY[:, :], identb[:], xD[:], True, False)
                    for r in range(8):
                        mm(p, pY[:, r * 64:(r + 1) * 64], MT[:, r, :], xdt[:, r * 64:(r + 1) * 64], False, r == 7)
                    mm(p, pS[1][:, :], CT[:, g, cs_], Sbf[:, g, :], True, True)
                    tt(p, "dve", r3(ubuf), r3(pS[1][:, :]), bc(d[:, 128 + 8 * g:136 + 8 * g], 2, [128, 8, 64]), ALU.mult)
                    tt(p, "dve", ubuf, ubuf, pY[:, :], ALU.add)
                    tt(p, "dve", ubuf, ubuf, sz[:, c, :], ALU.mult)
                    act(p, ysn, ubuf, AF.Square, accum=small[:, 12:13])
                    act(p, small[:, 13:14], small[:, 12:13], AF.Ln, scale=1.0 / 512, bias=EPS)
                    act(p, small[:, 14:15], small[:, 13:14], AF.Exp, scale=-0.5)
                    ts(p, "dve", ysn, ubuf, small[:, 14:15], ALU.mult)
                    for fb in range(4):
                        tr(p, pT[1][:, 512 + fb * 128:512 + (fb + 1) * 128], ysn[:, fb * 128:(fb + 1) * 128], identb[:], inc=(fb == 3))
                    cp(p, "act", ysT[:, g * 4:(g + 1) * 4, cs_], pT[1][:, 512:1024].rearrange("p (a b) -> p a b", a=4))
                mm(p, pS[0][:, :], Btok, xdts, True, True)
                tt(p, "dve", r3(Sf[:, g, :]), r3(Sf[:, g, :]), bc(d[:, 192 + 8 * g:200 + 8 * g], 2, [128, 8, 64]), ALU.mult)
                tt(p, "dve", Sf[:, g, :], Sf[:, g, :], pS[0][:, :], ALU.add)
                if full:
                    cp(p, "act", Sbf[:, g, :], Sf[:, g, :])
        if not full:
            return
        if KST < 10:
            raise _Stop()
        out_proj(l, ws, first=False)
        if KST < 11:
            raise _Stop()
        for uu in range(2):
            w = ws.next(U_WO(uu))
            for f in range(4):
                fb = uu * 4 + f
                pa = pA[f % 2]
                for kb in range(NKB):
                    mm(p, pa[:, 0:T], w[:, kb * 512 + f * 128:kb * 512 + (f + 1) * 128], mTb[:, kb, :], kb == 0, kb == NKB - 1)
                tt(p, "dve", hT[:, fb, :], hT[:, fb, :], pa[:, 0:T], ALU.add)
        if KST < 12:
            raise _Stop()
        rmsnorm_to_hnT()
        for uu in range(11):
            w = ws.next(U_GU(uu))
            for j in range(2):
                proj_fm(w, j * 128, pA[0])
                proj_fm(w, 256 + j * 128, pA[1])
                ra, _, _ = nxt()
                act(p, ra, pA[0][:, 0:T], AF.Silu)
                tt(p, "dve", actT[:, 2 * uu + j, :], ra, pA[1][:, 0:T], ALU.mult)
        for fb in range(8):
            w = ws.next(U_WD(fb))
            pa = pA[fb % 2]
            for kb in range(NFF):
                mm(p, pa[:, 0:T], w[:, kb * 128:(kb + 1) * 128], actT[:, kb, :], kb == 0, kb == NFF - 1)
            tt(p, "dve", hT[:, fb, :], hT[:, fb, :], pa[:, 0:T], ALU.add)
        if KST < 13:
            raise _Stop()
        if last_layer:
            act(p, hnT[:], hT[:], AF.Square)
            for kb in range(NKB):
                mm(p, pM[:, 0:T], onesb[:], hnT[:, kb, :], kb == 0, kb == NKB - 1)
            act(p, lnv[:], pM[:, 0:T], AF.Ln, scale=1.0 / D, bias=EPS)
            act(p, rbc[:], lnv[:], AF.Exp, scale=-0.5)
            for kb in range(NKB):
                stt(p, hT[:, kb, :], hT[:, kb, :], fnw[:, kb:kb + 1], rbc[:], ALU.mult, ALU.mult)
        p.dma(dst[:, :, tsl].rearrange("b p t -> p b t"), hT[:], "hT")

    def out_proj(l, ws_, first):
        for uu in range(4):
            w = ws_.next(U_RO(uu) if first else U_SO(uu))
            wg = ws_.next(U_GATE(uu))
            for j in range(2):
                fb = uu * 2 + j
                for kb in range(16):
                    mm(p, pY[:, 0:T], w[:, kb * 256 + j * 128:kb * 256 + (j + 1) * 128], yrT[:, kb, :], kb == 0, kb == 15)
                proj_fm(wg, (0 if first else 256) + j * 128, pA[j])
                ra, rb, _ = nxt()
                act(p, ra, pA[j][:, 0:T], AF.Sigmoid)
                if first:
                    tt(p, "dve", mTb[:, fb, :], ra, pY[:, 0:T], ALU.mult)
                else:
                    tt(p, "dve", rb, ra, pY[:, 0:T], ALU.mult)
                    tt(p, "pool", mTb[:, fb, :], mTb[:, fb, :], rb, ALU.add)

    def p2_seq():
        s = [U_BCB, U_BCC]
        for h in range(4):
            s += [U_A(h), U_V(h), U_G(h)]
        for uu in range(4):
            s += [U_RO(uu), U_GATE(uu)]
        for g in range(4):
            s += [U_X(g), U_Z(g)]
        for uu in range(4):
            s += [U_SO(uu), U_GATE(uu)]
        s += [U_WO(0), U_WO(1)] + [U_GU(i) for i in range(11)] + [U_WD(f) for f in range(8)]
        return s

    def p1_seq():
        s = [U_BCB]
        for h in range(4):
            s += [U_A(h), U_V(h)]
        s += [U_X(g) for g in range(4)]
        return s

    try:
        for kind, l in phases:
            src = hin
            if KST < 3:
                break
            halo_norm(halo_in)
            if KST < 4:
                break
            if kind == "p1":
                p.op("pool", lambda e: e.memset(Rf[:].rearrange("p a b c -> p (a b c)"), 0.0), [], [Rf[:]])
                p.op("pool", lambda e: e.memset(Sf[:].rearrange("p a b -> p (a b)"), 0.0), [], [Sf[:]])
                p.op("pool", lambda e: e.memset(LAt[:], 0.0), [], [LAt[:]])
                ws = Stream(l, p1_seq() * NTILE)
                for t in range(NTILE):
                    tile_body(l, t, False, src, None, ws, False)
                for h in range(NH):
                    for b in range(2):
                        p.dma(stloc[h * 2 + b, :, :], Rf[:, h, b, :], "Rf")
                for g in range(4):
                    p.dma(stloc[8 + g, :, :], Sf[:, g, :], "Sf")
                p.op("pool", lambda e: e.memset(rbc[:, 0:512], 0.0), [], [rbc[:, 0:512]])
                cp(p, "dve", rbc[:, 0:32], LAt[:])
                p.dma(stloc[12, :, :], rbc[:, 0:512], "rbc")
            else:
                for blk in range(8):
                    h, b = divmod(blk, 2)
                    p.dma(FS[:, 0:2048].rearrange("p (j f) -> p j f", j=4), stall[:, blk, :, :].rearrange("j p f -> p j f"), "FS")
                    ts(p, "dve", Rf[:, h, b, :], FS[:, 0:512], cf[:, h * 4:h * 4 + 1], ALU.mult)
                    for j in range(1, 4):
                        stt(p, Rf[:, h, b, :], FS[:, j * 512:(j + 1) * 512], cf[:, h * 4 + j:h * 4 + j + 1], Rf[:, h, b, :], ALU.mult, ALU.add)
                    cp(p, "act", Rbf[:, h, b, :], Rf[:, h, b, :])
                LAa = FS[:, 2048:2048 + 128].rearrange("p (j f) -> p j f", j=4)
                p.dma(LAa, stall[:, 12, :, 0:32].rearrange("j p f -> p j f"), "FSla")
                Ej = FS[:, 2176:2176 + 128].rearrange("p (j f) -> p j f", j=4)
                for j in range(4):
                    ts(p, "dve", Ej[:, j, :], LAa[:, 0, :], cf[:, 20 + j * 4:21 + j * 4], ALU.mult)
                    for m in range(1, 4):
                        stt(p, Ej[:, j, :], LAa[:, m, :], cf[:, 20 + j * 4 + m:21 + j * 4 + m], Ej[:, j, :], ALU.mult, ALU.add)
                    act(p, Ej[:, j, :], Ej[:, j, :], AF.Exp)
                    ts(p, "dve", Ej[:, j, :], Ej[:, j, :], cf[:, 16 + j:17 + j], ALU.mult)
                for g in range(4):
                    p.dma(FS[:, 0:2048].rearrange("p (j f) -> p j f", j=4), stall[:, 8 + g, :, :].rearrange("j p f -> p j f"), "FS")
                    r3 = lambda ap: ap.rearrange("p (r q) -> p r q", r=8)
                    tt(p, "dve", r3(Sf[:, g, :]), r3(FS[:, 0:512]), bc(Ej[:, 0, 8 * g:8 * g + 8], 2, [128, 8, 64]), ALU.mult)
                    for j in range(1, 4):
                        tt(p, "dve", r3(FS[:, j * 512:(j + 1) * 512]), r3(FS[:, j * 512:(j + 1) * 512]),
                           bc(Ej[:, j, 8 * g:8 * g + 8], 2, [128, 8, 64]), ALU.mult)
                        tt(p, "dve", Sf[:, g, :], Sf[:, g, :], FS[:, j * 512:(j + 1) * 512], ALU.add)
                    cp(p, "act", Sbf[:, g, :], Sf[:, g, :])
                ws = Stream(l, p2_seq() * NTILE)
                for t in range(NTILE):
                    tile_body(l, t, True, src, hout, ws, (l == DEPTH - 1) and not globals().get('_NOFINAL', False))
    except _Stop:
        pass
    if p.log is not None:
        with open(os.environ['KDUMP'], 'w') as f:
            for r in p.log:
                f.write(repr(r) + '\n')
    p.finish()
    es.close()
    return nc


def _consts():
    j = np.arange(128)
    cst = np.zeros((128, 1540), np.float32)
    cst[:, 0:128] = np.eye(128)
    cst[:, 128:256] = (j[:, None] <= j[None, :])
    cst[:, 256:384] = (j[:, None] > j[None, :])
    cst[:, 384:512] = 1.0
    for h in range(NH):
        g = GAMMA[h]
        diff = j[None, :] - j[:, None]
        cst[:, 512 + h * 128:512 + (h + 1) * 128] = np.where(diff >= 0, g ** np.maximum(diff, 0).astype(np.float64), 0.0)
        cst[:, 1024 + h * 128:1024 + (h + 1) * 128] = (g ** (j + 1.0))[None, :]
        cst[:, 1536 + h] = g ** (127.0 - j)
    return cst


def _rope(pos):
    inv = np.float32(10000.0) ** (-(np.arange(128, dtype=np.float32) / np.float32(128)))
    ang = pos.astype(np.float32)[:, None] * inv[None, :].astype(np.float32)
    return np.stack([np.cos(ang).T, np.sin(ang).T]).astype(np.float32)


def _coef(s, NT):
    cf = np.zeros((128, 36), np.float32)
    for h in range(NH):
        for j in range(4):
            if j < s:
                cf[:, h * 4 + j] = GAMMA[h] ** (float(NT) * (s - 1 - j))
    for j in range(4):
        cf[:, 16 + j] = 1.0 if j < s else 0.0
        for m in range(4):
            cf[:, 20 + j * 4 + m] = 1.0 if (j < m < s) else 0.0
    return cf


def _fm(a):
    return np.ascontiguousarray(a.T.reshape(NKB, 128, a.shape[0]))


def _layer_inputs(l, P):
    nmv = np.concatenate([P["norm_mix_w"][l].reshape(8, 128).T, P["norm_ffn_w"][l].reshape(8, 128).T,
                          P["ssd_norm_w"][l].reshape(16, 128).T], axis=1)
    cwv = np.zeros((128, 24, 5), np.float32)
    cwv[:, :, 0:4] = P["conv_w"][l].reshape(4, 24, 128).transpose(2, 1, 0)
    cwv[:, :, 4] = P["conv_b"][l].reshape(24, 128).T
    hpv = np.concatenate([np.broadcast_to(P[k][l][None, :], (128, 32)) for k in ("dt_bias", "a_log", "d_skip")], axis=1)
    return {
        "w_in%d" % l: np.ascontiguousarray(P["w_in"][l]), "ret_out%d" % l: np.ascontiguousarray(P["ret_out"][l]),
        "ssd_out%d" % l: np.ascontiguousarray(P["ssd_out"][l]), "w_o%d" % l: np.ascontiguousarray(P["w_o"][l]),
        "w_gu%d" % l: np.ascontiguousarray(P["w_gate_up"][l]), "w_down%d" % l: np.ascontiguousarray(P["w_down"][l]),
        "nm%d" % l: np.ascontiguousarray(nmv, np.float32), "cw%d" % l: np.ascontiguousarray(cwv.reshape(128, 120)),
        "hp%d" % l: np.ascontiguousarray(hpv, np.float32),
    }


_T_DEFAULT = 512


def kernel(x, norm_mix_w, w_in, ret_out, conv_w, conv_b, dt_bias, a_log, d_skip, ssd_norm_w,
           ssd_out, w_o, norm_ffn_w, w_gate_up, w_down, final_norm_w, _T=None):
    P = dict(norm_mix_w=norm_mix_w, w_in=w_in, ret_out=ret_out, conv_w=conv_w, conv_b=conv_b, dt_bias=dt_bias,
             a_log=a_log, d_skip=d_skip, ssd_norm_w=ssd_norm_w, ssd_out=ssd_out, w_o=w_o, norm_ffn_w=norm_ffn_w,
             w_gate_up=w_gate_up, w_down=w_down)
    P = {k: np.asarray(v, np.float32) for k, v in P.items()}
    x = np.asarray(x, np.float32)
    B, S, _ = x.shape
    NSEG = 4
    NT = S // NSEG
    T = _T or min(_T_DEFAULT, NT)
    ncore = B * NSEG
    assert ncore == 8
    cst = _consts()
    fnw = np.ascontiguousarray(np.asarray(final_norm_w, np.float32).reshape(8, 128).T)
    common = []
    for c in range(ncore):
        b, s = divmod(c, NSEG)
        common.append({"cst": cst, "cs": _rope(np.arange(s * NT, (s + 1) * NT)), "cf": _coef(s, NT), "fnw": fnw})

    def halo_of(hfm_prev):
        h = np.zeros((NKB, 128, 4), np.float32)
        if hfm_prev is not None:
            h[:, :, 0:3] = hfm_prev[:, :, -3:]
        return h

    hcur = [_fm(x[c // NSEG, (c % NSEG) * NT:((c % NSEG) + 1) * NT, :]) for c in range(ncore)]
    progs = {}
    for l in range(DEPTH):
        li = _layer_inputs(l, P)
        halos = [halo_of(hcur[c - 1] if c % NSEG else None) for c in range(ncore)]
        nc1 = build(NT, T, [("p1", l)])
        li1 = {k: v for k, v in li.items() if k.split("%d" % l)[0] in ("w_in", "nm", "cw", "hp")}
        maps = [dict(common[c], hin=hcur[c], halo=halos[c], **li1) for c in range(ncore)]
        r1 = run_bass_kernel_spmd(nc1, maps, core_ids=list(range(ncore)))
        st = [np.asarray(r["stloc"]) for r in r1.results]
        nc2 = build(NT, T, [("p2", l)])
        maps = []
        for c in range(ncore):
            b = c // NSEG
            stall = np.stack([st[b * NSEG + j] for j in range(NSEG)])
            maps.append(dict(common[c], hin=hcur[c], halo=halos[c], stall=stall, **li))
        r2 = run_bass_kernel_spmd(nc2, maps, core_ids=list(range(ncore)))
        hcur = [np.asarray(r["hout"]) for r in r2.results]
    out = np.empty((B, S, D), np.float32)
    for c in range(ncore):
        b, s = divmod(c, NSEG)
        out[b, s * NT:(s + 1) * NT, :] = hcur[c].reshape(D, NT).T
    return out
```

```python
import os
import numpy as np
from contextlib import ExitStack
import concourse.bass as bass
import concourse.mybir as mybir
from concourse.bass_utils import run_bass_kernel_spmd

F32, BF16 = mybir.dt.float32, mybir.dt.bfloat16
AF = mybir.ActivationFunctionType
ALU = mybir.AluOpType

D = 1024
NKB = 8
DEPTH = 2
EPS = 1e-6
NH = 4
DFF = 2816
NFF = 22
DIN = 13344
C_Q, C_K, C_V, C_G, C_Z, C_X, C_B, C_C, C_DT, C_GR, C_GS = 0, 1024, 2048, 4096, 6144, 8192, 10240, 10752, 11264, 11296, 12320
NUNIT = 55
U_BCB, U_BCC = 0, 1
def U_A(h): return 2 + 3 * h
def U_V(h): return 3 + 3 * h
def U_G(h): return 4 + 3 * h
def U_X(g): return 14 + 2 * g
def U_Z(g): return 15 + 2 * g
def U_RO(u): return 22 + 3 * u
def U_SO(u): return 23 + 3 * u
def U_GATE(u): return 24 + 3 * u
def U_WO(u): return 34 + u
def U_GU(u): return 36 + u
def U_WD(f): return 47 + f
GAMMA = [1.0 - 2.0 ** (-5.0 - h) for h in range(NH)]
NST = 13

ENG = dict(pe="tensor", act="scalar", dve="vector", pool="gpsimd", sp="sync")


def _prod(xs):
    r = 1
    for v in xs:
        r *= int(v)
    return r


def _iv(ap):
    t = ap.tensor
    dims = [(int(s), int(c)) for s, c in ap.ap]
    off = int(ap.offset)
    if "DRAM" in str(ap.space).upper():
        hi = off + sum(s * (c - 1) for s, c in dims if s > 0) + 1
        return (t.name, 0, 1, off, hi)
    if "PSUM" in str(ap.space).upper():
        return (t.name, 0, 128, 0, 1 << 30)
    rows = _prod(list(t.shape)[1:])
    p0, f0 = off // rows, off % rows
    fhi = f0 + sum(s * (c - 1) for s, c in dims[1:] if s > 0) + 1
    return (t.name, p0, p0 + dims[0][1], f0, fhi)


class _Stop(Exception):
    pass


class Prog:
    def __init__(self, nc, es):
        self.nc, self.es = nc, es
        self.ops = {e: [] for e in ENG}
        self.cnt = {e: 0 for e in ENG}
        self.sems = {}
        self.semval = {}
        self.waited = {e: {} for e in ENG}
        self.acc = {}
        self.nsem = 0
        self.log = [] if os.environ.get('KDUMP') else None

    def sem(self, key):
        if key not in self.sems:
            self.nsem += 1
            self.sems[key] = self.es.enter_context(self.nc.semaphore("s%d" % self.nsem))
        return self.sems[key]

    def _need(self, eng, reads, writes):
        need = {}

        def add(tok):
            k, v = tok
            if k == "pe" and eng == "pe":
                return
            if v > need.get(k, 0):
                need[k] = v

        for ap in reads:
            n, p0, p1, f0, f1 = _iv(ap)
            for r in self.acc.get(n, ()):
                if r[0] == "w" and r[1] < p1 and p0 < r[2] and r[3] < f1 and f0 < r[4]:
                    add(r[5])
        for ap in writes:
            n, p0, p1, f0, f1 = _iv(ap)
            for r in self.acc.get(n, ()):
                if r[1] < p1 and p0 < r[2] and r[3] < f1 and f0 < r[4]:
                    add(r[5])
        out = []
        for k, v in need.items():
            if k.startswith("d:"):
                v = max(v, self.semval[k])
            if self.waited[eng].get(k, 0) < v:
                self.waited[eng][k] = v
                out.append((k, v))
        return out

    def _rec(self, kind, ap, tok):
        n, p0, p1, f0, f1 = _iv(ap)
        L = self.acc.setdefault(n, [])
        if kind == "w":
            L[:] = [r for r in L if not (p0 <= r[1] and r[2] <= p1 and f0 <= r[3] and r[4] <= f1)]
        else:
            L[:] = [r for r in L if not (r[0] == "r" and r[5][0] == tok[0] and p0 <= r[1] and r[2] <= p1
                                         and f0 <= r[3] and r[4] <= f1)]
        L.append((kind, p0, p1, f0, f1, tok))

    def op(self, eng, fn, reads=(), writes=(), inc=True):
        self.sem(eng)
        psr = [a for a in reads if "PSUM" in str(a.space).upper()]
        if psr:
            reads = [a for a in reads if "PSUM" not in str(a.space).upper()]
            writes = list(writes) + psr
        waits = self._need(eng, reads, writes)
        idx = self.cnt[eng] + 1
        if inc:
            self.cnt[eng] = idx
        tok = (eng, idx)
        for ap in reads:
            self._rec("r", ap, tok)
        for ap in writes:
            self._rec("w", ap, tok)
        self.ops[eng].append((waits, fn, (eng, 1) if inc else None))
        if self.log is not None:
            self.log.append((eng, idx, inc, waits, [_iv(a) for a in reads], [_iv(a) for a in writes]))

    def dma(self, out, in_, key, eng="sp"):
        k = "d:" + key
        self.sem(k)
        waits = self._need(eng, [in_], [out])
        v = self.semval.get(k, 0) + 16
        self.semval[k] = v
        tok = (k, v)
        self._rec("r", in_, tok)
        self._rec("w", out, tok)
        self.ops[eng].append((waits, lambda e: e.dma_start(out=out, in_=in_), (k, 16)))
        if self.log is not None:
            self.log.append((eng, tok, True, waits, [_iv(in_)], [_iv(out)]))

    def finish(self):
        for k, v in self.semval.items():
            if self.waited["sp"].get(k, 0) < v:
                self.ops["sp"].append(([(k, v)], None, None))
        block = self.es.enter_context(self.nc.Block())
        for eng, attr in ENG.items():
            ops = self.ops[eng]
            if not ops:
                continue

            def body(e, ops=ops):
                for waits, fn, inc in ops:
                    for k, v in waits:
                        e.wait_ge(self.sems[k], v)
                    if fn is None:
                        continue
                    ins = fn(e)
                    if inc is not None:
                        ins.then_inc(self.sems[inc[0]], inc[1])

            getattr(block, attr)(body)


def mm(p, out, lhsT, rhs, start, stop, inc=None):
    if inc is None:
        inc = stop
    p.op("pe", lambda e: e.matmul(out, lhsT=lhsT, rhs=rhs, start=start, stop=stop), [lhsT, rhs], [out], inc=inc)


def tr(p, out, in_, ident, inc=True):
    p.op("pe", lambda e: e.transpose(out, in_, ident), [in_, ident], [out], inc=inc)


def act(p, out, in_, func, bias=None, scale=None, accum=None):
    kw = {}
    rd = [in_]
    if bias is not None:
        kw["bias"] = bias
        if not isinstance(bias, (int, float)):
            rd.append(bias)
    if scale is not None:
        kw["scale"] = scale
        if not isinstance(scale, (int, float)):
            rd.append(scale)
    wr = [out]
    if accum is not None:
        kw["accum_out"] = accum
        wr.append(accum)
    p.op("act", lambda e: e.activation(out=out, in_=in_, func=func, **kw), rd, wr)


def tt(p, eng, out, a, b, op):
    p.op(eng, lambda e: e.tensor_tensor(out=out, in0=a, in1=b, op=op), [a, b], [out])


def ts(p, eng, out, a, s1, op0, s2=None, op1=None):
    rd = [a] + [s for s in (s1, s2) if s is not None and not isinstance(s, (int, float))]
    if op1 is None:
        p.op(eng, lambda e: e.tensor_scalar(out=out, in0=a, scalar1=s1, scalar2=None, op0=op0), rd, [out])
    else:
        p.op(eng, lambda e: e.tensor_scalar(out=out, in0=a, scalar1=s1, scalar2=s2, op0=op0, op1=op1), rd, [out])


def stt(p, out, in0, scalar, in1, op0, op1):
    rd = [in0, in1] + ([] if isinstance(scalar, (int, float)) else [scalar])
    p.op("dve", lambda e: e.scalar_tensor_tensor(out=out, in0=in0, scalar=scalar, in1=in1, op0=op0, op1=op1), rd, [out])


def cp(p, eng, out, in_):
    if eng == "act":
        p.op("act", lambda e: e.copy(out=out, in_=in_), [in_], [out])
    else:
        p.op(eng, lambda e: e.tensor_copy(out=out, in_=in_), [in_], [out])


def bc(ap, axis, shape):
    return ap.unsqueeze(axis).to_broadcast(list(shape))


def build(NT, T, phases, fused=False):
    NCH = T // 128
    NTILE = NT // T
    assert T % 128 == 0 and NT % T == 0
    nc = bass.Bass("TRN2", target_bir_lowering=False)
    es = ExitStack()
    p = Prog(nc, es)
    layers = sorted(set(l for _, l in phases))

    def din(name, shape, dt=F32):
        return nc.dram_tensor(name, list(shape), dt, kind="ExternalInput").ap()

    def dout(name, shape, dt=F32):
        return nc.dram_tensor(name, list(shape), dt, kind="ExternalOutput").ap()

    def dint(name, shape, dt=F32):
        return nc.dram_tensor(name, list(shape), dt).ap()

    hin = din("hin", [NKB, 128, NT])
    halo_in = din("halo", [NKB, 128, 4])
    cst_d = din("cst", [128, 1540])
    cs_d = din("cs", [2, 128, NT])
    cf_d = din("cf", [128, 36])
    fnw_d = din("fnw", [128, 8])
    W = {}
    for l in layers:
        W[l] = dict(nm=din("nm%d" % l, [128, 32]), cw=din("cw%d" % l, [128, 120]), hp=din("hp%d" % l, [128, 96]),
                    wu=din("wu%d" % l, [NUNIT, 128, 4096], BF16), wdts=din("wdts%d" % l, [128, 256]))
    if not fused:
        (kind, lay), = phases
        if kind == "p1":
            stloc = dout("stloc", [NST, 128, 512])
        else:
            stall = din("stall", [4, NST, 128, 512])
            hout = dout("hout", [NKB, 128, NT])

    sb = lambda name, shape, dt=F32: es.enter_context(nc.sbuf_tensor(name, list(shape), dt))
    ps = lambda name, shape, dt=F32: es.enter_context(nc.psum_tensor(name, list(shape), dt))

    cst = sb("cst_sb", [128, 1540])
    identb = sb("identb", [128, 128], BF16)
    onesb = sb("onesb", [128, 128], BF16)
    triTb = sb("triTb", [128, 128], BF16)
    triUb = sb("triUb", [128, 128], BF16)
    lahl = sb("lahl", [128, NCH, 2, 32], BF16)
    rlab = sb("rlab", [128, 2, 8, 128], BF16)
    xD = sb("xD", [128, 512], BF16)
    cossin = sb("cossin", [128, 2, T])
    cf = sb("cf_sb", [128, 36])
    fnw = sb("fnw_sb", [128, 8])
    nm = {l: sb("nm_sb%d" % l, [128, 32]) for l in layers}
    cw = {l: sb("cw_sb%d" % l, [128, 24, 5]) for l in layers}
    hp = {l: sb("hp_sb%d" % l, [128, 96]) for l in layers}
    atile = {l: sb("atile%d" % l, [128, 32]) for l in layers}
    wdt = {l: sb("wdt%d" % l, [128, NKB, 32], BF16) for l in layers}
    hT = sb("hT", [128, NKB, T])
    hnT = sb("hnT", [128, NKB, T], BF16)
    rbc = sb("rbc", [128, T])
    lnv = sb("lnv", [128, T])
    hnTh = sb("hnTh", [128, NKB, 4], BF16)
    haloT = sb("haloT", [128, NKB, 4])
    small = sb("small", [128, 64])
    NW = 4
    wst = [sb("wst%d" % i, [128, 4096], BF16) for i in range(NW)]
    FS = sb("FS", [128, 3 * T + 4 + NCH * 256 + 128 + 1024 + 1024 + 3 * T + 4])
    BS = sb("BS", [128, 8 * T + 4 * T + NCH * 512 + 1536 + 128 + 2048 + 512 + 64], BF16)
    U = sb("U", [128, 24 * T], BF16)
    hist = sb("hist", [128, 24, 4])
    Rf = sb("Rf", [128, NH, 2, 512])
    Rbf = sb("Rbf", [128, NH, 2, 512], BF16)
    Sf = sb("Sf", [128, 4, 512])
    Sbf = sb("Sbf", [128, 4, 512], BF16)
    LAt = sb("LAt", [128, 32])
    pA = [ps("pA0", [128, 512]), ps("pA1", [128, 512])]
    pY = ps("pY", [128, 512])
    pS = [ps("pS0", [128, 512]), ps("pS1", [128, 512])]
    pM = ps("pM", [128, 512])
    pT = [ps("pT0", [128, 1024], BF16), ps("pT1", [128, 1024], BF16)]

    identf = cst[:, 0:128]
    triT = cst[:, 128:256]
    triU = cst[:, 256:384]
    onesf = cst[:, 384:512]
    intraT = cst[:, 512:1024].rearrange("p (h i) -> p h i", h=4)
    qdec = cst[:, 1024:1536].rearrange("p (h i) -> p h i", h=4)
    kdec = cst[:, 1536:1540]
    _o2 = 3 * T + 4 + NCH * 256 + 128 + 1024 + 1024
    _tmp = [(FS[:, 0:T], FS[:, T:2 * T], FS[:, 2 * T:3 * T + 4]),
            (FS[:, _o2:_o2 + T], FS[:, _o2 + T:_o2 + 2 * T], FS[:, _o2 + 2 * T:_o2 + 3 * T + 4])]
    _tsel = [0]

    def nxt():
        _tsel[0] ^= 1
        return _tmp[_tsel[0]]
    ra, rb, xraw = _tmp[0]
    o = 3 * T + 4
    dtb = FS[:, o:o + NCH * 256].rearrange("p (c f) -> p c f", c=NCH)
    o += NCH * 256
    cbm = FS[:, o:o + 128]
    o += 128
    rla = FS[:, o:o + 1024].rearrange("p (r i) -> p r i", r=8)
    o += 1024
    ubuf = FS[:, o:o + 512]
    vbuf = FS[:, o + 512:o + 1024]
    BT = BS[:, 0:4 * T].rearrange("p (g t) -> p g t", g=4)
    CT = BS[:, 4 * T:8 * T].rearrange("p (g t) -> p g t", g=4)
    o = 8 * T
    qT = BS[:, o:o + 2 * T].rearrange("p (b t) -> p b t", b=2)
    kT = BS[:, o + 2 * T:o + 4 * T].rearrange("p (b t) -> p b t", b=2)
    r2 = o + 4 * T
    vtok = BS[:, r2:r2 + NCH * 512].rearrange("p (c f) -> p c f", c=NCH)
    r2 += NCH * 512
    sT = BS[:, r2:r2 + 128]
    qTs = BS[:, r2 + 128:r2 + 384].rearrange("p (b i) -> p b i", b=2)
    yn = BS[:, r2 + 384:r2 + 896]
    ktok = BS[:, r2 + 896:r2 + 1152]
    gT = U[:, 16 * T:20 * T].rearrange("p (b t) -> p b t", b=4)
    xcT = BS[:, o:o + 4 * T].rearrange("p (b t) -> p b t", b=4)
    s2 = o + 4 * T
    sz = BS[:, s2:s2 + NCH * 512].rearrange("p (c f) -> p c f", c=NCH)
    s2 += NCH * 512
    xtok = BS[:, s2:s2 + 512]
    xdt = BS[:, s2 + 512:s2 + 1024]
    xdts = BS[:, s2 + 1024:s2 + 1536]
    s2 += 1536
    Btok = BS[:, s2:s2 + 128]
    s2 += 128
    esb = BS[:, s2:s2 + 1024].rearrange("p (r i) -> p r i", r=8)
    MT = BS[:, s2 + 1024:s2 + 2048].rearrange("p (r i) -> p r i", r=8)
    s2 += 2048
    ysn = BS[:, s2:s2 + 512]
    yrT = U[:, 0:16 * T].rearrange("p (b t) -> p b t", b=16)
    ysT = yrT
    mTb = U[:, 16 * T:24 * T].rearrange("p (b t) -> p b t", b=8)
    actT = U[:, 0:22 * T].rearrange("p (b t) -> p b t", b=22)
    stage = hT[:, :, :].rearrange("p a b -> p (a b)")
    assert NKB * T >= 4096

    p.dma(cst[:], cst_d[:, :], "cst")
    p.dma(cf[:], cf_d[:, :], "cf")
    p.dma(fnw[:], fnw_d[:, :], "fnw")
    cp(p, "dve", identb[:], identf)
    cp(p, "dve", onesb[:], onesf)
    cp(p, "dve", triTb[:], triT)
    cp(p, "dve", triUb[:], triU)
    for l in layers:
        p.dma(nm[l][:], W[l]["nm"][:, :], "nm%d" % l)
        p.dma(cw[l][:].rearrange("p a b -> p (a b)"), W[l]["cw"][:, :], "cw%d" % l)
        p.dma(hp[l][:], W[l]["hp"][:, :], "hp%d" % l)
        act(p, atile[l][:], hp[l][:, 32:64], AF.Exp)
        ts(p, "dve", atile[l][:], atile[l][:], -1.0, ALU.mult)

    KST = int(os.environ.get('KSTAGE', '99'))
    conv_rr = [0]

    def convert_piece(dst, src, scale, mul=None):
        e = ("dve", "act", "pool")[conv_rr[0] % 3]
        conv_rr[0] += 1
        if scale is None:
            cp(p, e, dst, src)
        elif e == "act":
            act(p, dst, src, AF.Copy, scale=scale)
            if mul is not None:
                ts(p, "dve", dst, dst, mul, ALU.mult)
        elif e == "dve":
            if mul is None:
                ts(p, "dve", dst, src, scale, ALU.mult)
            else:
                ts(p, "dve", dst, src, scale, ALU.mult, mul, ALU.mult)
        else:
            ts(p, "pool", dst, src, scale, ALU.mult, 1.0 if mul is None else mul, ALU.mult)

    def unit_pieces(l, u):
        Wl = W[l]
        nmx = lambda kb: nm[l][:, kb:kb + 1]
        nfx = lambda kb: nm[l][:, 8 + kb:9 + kb]
        nsx = lambda kb: nm[l][:, 16 + kb:17 + kb]
        out = []
        rows = lambda kb: slice(kb * 128, (kb + 1) * 128)
        if u in (U_BCB, U_BCC):
            c0 = C_B if u == U_BCB else C_C
            for kb in range(8):
                out.append((Wl["w_in"][rows(kb), c0:c0 + 512], kb * 512, 512, nmx(kb), None))
        elif 2 <= u < 14:
            h, k = divmod(u - 2, 3)
            for kb in range(8):
                if k == 0:
                    out.append((Wl["w_in"][rows(kb), C_Q + h * 256:C_Q + h * 256 + 256], kb * 512, 256, nmx(kb), None))
                    out.append((Wl["w_in"][rows(kb), C_K + h * 256:C_K + h * 256 + 256], kb * 512 + 256, 256, nmx(kb), 1.0 / 16))
                else:
                    c0 = (C_V if k == 1 else C_G) + h * 512
                    out.append((Wl["w_in"][rows(kb), c0:c0 + 512], kb * 512, 512, nmx(kb), None))
        elif 14 <= u < 22:
            g, k = divmod(u - 14, 2)
            c0 = (C_X if k == 0 else C_Z) + g * 512
            for kb in range(8):
                out.append((Wl["w_in"][rows(kb), c0:c0 + 512], kb * 512, 512, nmx(kb), None))
        elif 22 <= u < 34:
            uu, k = divmod(u - 22, 3)
            if k == 0:
                for kb in range(16):
                    out.append((Wl["ret_out"][rows(kb), uu * 256:uu * 256 + 256], kb * 256, 256, None, None))
            elif k == 1:
                for kb in range(16):
                    out.append((Wl["ssd_out"][rows(kb), uu * 256:uu * 256 + 256], kb * 256, 256, nsx(kb), None))
            else:
                for kb in range(8):
                    out.append((Wl["w_in"][rows(kb), C_GR + uu * 256:C_GR + uu * 256 + 256], kb * 512, 256, nmx(kb), None))
                    out.append((Wl["w_in"][rows(kb), C_GS + uu * 256:C_GS + uu * 256 + 256], kb * 512 + 256, 256, nmx(kb), None))
        elif u in (34, 35):
            uu = u - 34
            for kb in range(8):
                out.append((Wl["w_o"][rows(kb), uu * 512:uu * 512 + 512], kb * 512, 512, None, None))
        elif 36 <= u < 47:
            uu = u - 36
            for kb in range(8):
                out.append((Wl["w_gu"][rows(kb), uu * 256:uu * 256 + 256], kb * 512, 256, nfx(kb), None))
                out.append((Wl["w_gu"][rows(kb), DFF + uu * 256:DFF + uu * 256 + 256], kb * 512 + 256, 256, nfx(kb), None))
        else:
            f = u - 47
            for kb in range(NFF):
                out.append((Wl["w_down"][rows(kb), f * 128:f * 128 + 128], kb * 128, 128, None, None))
        return out

    stages = [(stage, "stage0"), (FS[:, 0:4096], "stage1"), (Rf[:].rearrange("p a b c -> p (a b c)"), "stage2")]

    def convert_layer(l, units):
        n = len(units)
        for i in range(n + 2):
            if i < n:
                st, key = stages[i % 3]
                for q, (src, dc, w, sc, mul) in enumerate(unit_pieces(l, units[i])):
                    p.dma(st[:, dc:dc + w], src, key, eng=("sp" if q % 2 == 0 else "act"))
            j = i - 2
            if 0 <= j < n:
                st, key = stages[j % 3]
                wb = wst[j % NW]
                pcs = unit_pieces(l, units[j])
                for src, dc, w, sc, mul in pcs:
                    convert_piece(wb[:, dc:dc + w], st[:, dc:dc + w], sc, mul)
                hi = max(dc + w for _, dc, w, _, _ in pcs)
                p.dma(W[l]["wu"][units[j], :, 0:hi], wb[:, 0:hi], "wst%d" % (j % NW))
        for kb in range(8):
            p.dma(stage[:, kb * 32:(kb + 1) * 32], W[l]["w_in"][kb * 128:(kb + 1) * 128, C_DT:C_DT + 32], "stage0")
        for kb in range(8):
            ts(p, "dve", wdt[l][:, kb, :], stage[:, kb * 32:(kb + 1) * 32], nm[l][:, kb:kb + 1], ALU.mult)

    P1_UNITS = [U_BCB] + [u for h in range(4) for u in (U_A(h), U_V(h))] + [U_X(g) for g in range(4)]
    P2_UNITS = ([U_BCB, U_BCC] + [u for h in range(4) for u in (U_A(h), U_V(h), U_G(h))]
                + [u for g in range(4) for u in (U_X(g), U_Z(g))]
                + [u for uu in range(4) for u in (U_RO(uu), U_SO(uu), U_GATE(uu))]
                + [U_WO(0), U_WO(1)] + [U_GU(i) for i in range(11)] + [U_WD(f) for f in range(8)])
    for l in layers:
        p.dma(stage[:, 0:256], W[l]["wdts"][:, :], "stage0")
        for kb in range(8):
            ts(p, "dve", wdt[l][:, kb, :], stage[:, kb * 32:(kb + 1) * 32], nm[l][:, kb:kb + 1], ALU.mult)

    class Stream:
        def __init__(self, l, seq):
            self.l, self.seq, self.i, self.slots = l, seq, 0, []

        def next(self, expect):
            while len(self.slots) < min(len(self.seq), self.i + NW - 1):
                u = self.seq[len(self.slots)]
                slot = wslot[0] % NW
                wslot[0] += 1
                p.dma(wst[slot][:], W[self.l]["wu"][u, :, :], "wst%d" % slot)
                self.slots.append(slot)
            assert self.seq[self.i] == expect, (self.seq[self.i], expect)
            w = wst[self.slots[self.i]]
            self.i += 1
            return w

    wslot = [0]

    def rmsnorm_to_hnT(l_unused=None):
        act(p, hnT[:], hT[:], AF.Square)
        for kb in range(NKB):
            mm(p, pM[:, 0:T], onesb[:], hnT[:, kb, :], kb == 0, kb == NKB - 1)
        act(p, lnv[:], pM[:, 0:T], AF.Ln, scale=1.0 / D, bias=EPS)
        act(p, rbc[:], lnv[:], AF.Exp, scale=-0.5)
        tt(p, "dve", hnT[:], hT[:], bc(rbc[:], 1, [128, NKB, T]), ALU.mult)

    def halo_norm(halo_src):
        p.dma(haloT[:], halo_src.rearrange("b p t -> p b t"), "haloT")
        act(p, hnTh[:], haloT[:], AF.Square)
        for kb in range(NKB):
            mm(p, pM[:, 256:260], onesb[:], hnTh[:, kb, :], kb == 0, kb == NKB - 1)
        act(p, small[:, 0:4], pM[:, 256:260], AF.Ln, scale=1.0 / D, bias=EPS)
        act(p, small[:, 4:8], small[:, 0:4], AF.Exp, scale=-0.5)
        tt(p, "dve", hnTh[:], haloT[:], bc(small[:, 4:8], 1, [128, NKB, 4]), ALU.mult)

    def proj_fm(w, c0, pa):
        for kb in range(NKB):
            mm(p, pa[:, 0:T], w[:, kb * 512 + c0:kb * 512 + c0 + 128], hnT[:, kb, :], kb == 0, kb == NKB - 1)

    def proj_tm(w, c, pa, width=512, c0=0):
        for kb in range(NKB):
            mm(p, pa[:, 0:width], hnT[:, kb, c * 128:(c + 1) * 128], w[:, kb * 512 + c0:kb * 512 + c0 + width],
               kb == 0, kb == NKB - 1)

    def conv_block(l, t, w, c0, blk, pa, out_bf):
        ra, rb, xraw = nxt()
        if t == 0:
            for kb in range(NKB):
                mm(p, pM[:, 260:264], w[:, kb * 512 + c0:kb * 512 + c0 + 128], hnTh[:, kb, :], kb == 0, kb == NKB - 1)
            cp(p, "act", xraw[:, 0:3], pM[:, 260:263])
        else:
            cp(p, "pool", xraw[:, 0:3], hist[:, blk, 0:3])
        cp(p, "act", xraw[:, 3:3 + T], pa[:, 0:T])
        cp(p, "pool", hist[:, blk, 0:3], xraw[:, T:T + 3])
        act(p, ra, pa[:, 0:T], AF.Identity, scale=cw[l][:, blk, 3:4], bias=cw[l][:, blk, 4:5])
        stt(p, rb, xraw[:, 0:T], cw[l][:, blk, 0:1], ra, ALU.mult, ALU.add)
        stt(p, ra, xraw[:, 1:1 + T], cw[l][:, blk, 1:2], rb, ALU.mult, ALU.add)
        stt(p, rb, xraw[:, 2:2 + T], cw[l][:, blk, 2:3], ra, ALU.mult, ALU.add)
        act(p, out_bf, rb, AF.Silu)

    def dt_prep(l, c, full):
        d = dtb[:, c, :]
        KD = int(os.environ.get('KDT', '63'))
        if KD & 1:
            for kb in range(NKB):
                mm(p, pM[:, 0:32], hnT[:, kb, c * 128:(c + 1) * 128], wdt[l][:, kb, :], kb == 0, kb == NKB - 1)
            tt(p, "dve", d[:, 0:32], pM[:, 0:32], hp[l][:, 0:32], ALU.add)
        if KD & 2:
            act(p, d[:, 32:64], d[:, 0:32], AF.Exp)
            act(p, d[:, 64:96], d[:, 32:64], AF.Ln, scale=1.0, bias=1.0)
            tt(p, "dve", d[:, 96:128], d[:, 64:96], atile[l][:], ALU.mult)
        if KD & 4:
            cp(p, "dve", lahl[:, c, 0, :], d[:, 96:128])
            tt(p, "dve", lahl[:, c, 1, :], d[:, 96:128], lahl[:, c, 0, :], ALU.subtract)
            for k, lt in enumerate((triTb, triUb, onesb)):
                mm(p, pM[:, 32 + 32 * k:64 + 32 * k], lt[:], lahl[:, c, 0, :], True, False)
                mm(p, pM[:, 32 + 32 * k:64 + 32 * k], lt[:], lahl[:, c, 1, :], False, True)
        if KD & 8:
            act(p, d[:, 128:224], pM[:, 32:128], AF.Exp)
        if KD & 32:
            tt(p, "dve", d[:, 224:256], d[:, 64:96], d[:, 160:192], ALU.mult)
        if (KD & 16) and not full:
            tt(p, "dve", LAt[:], LAt[:], pM[:, 96:128], ALU.add)

    def tile_body(l, t, full, src, dst, ws, last_layer):
        tsl = slice(t * T, (t + 1) * T)
        p.dma(hT[:], src[:, :, tsl].rearrange("b p t -> p b t"), "hT")
        p.dma(cossin[:], cs_d[:, :, tsl].rearrange("a p t -> p a t"), "cossin")
        rmsnorm_to_hnT()
        if KST < 5:
            raise _Stop()
        cosT, sinT = cossin[:, 0, :], cossin[:, 1, :]
        for c in range(NCH):
            dt_prep(l, c, full)
        if KST < 6:
            raise _Stop()
        w = ws.next(U_BCB)
        for g in range(4):
            proj_fm(w, g * 128, pA[g % 2])
            conv_block(l, t, w, g * 128, 16 + g, pA[g % 2], BT[:, g, :])
        if full:
            w = ws.next(U_BCC)
            for g in range(4):
                proj_fm(w, g * 128, pA[g % 2])
                conv_block(l, t, w, g * 128, 20 + g, pA[g % 2], CT[:, g, :])
        if KST < 7:
            raise _Stop()
        for h in range(NH):
            w = ws.next(U_A(h))
            for qk in ((0, 1) if full else (1,)):
                dstT = qT if qk == 0 else kT
                ra, rb, _ = nxt()
                proj_fm(w, qk * 256, pA[0])
                proj_fm(w, qk * 256 + 128, pA[1])
                tt(p, "dve", ra, pA[0][:, 0:T], cosT, ALU.mult)
                tt(p, "dve", rb, pA[1][:, 0:T], sinT, ALU.mult)
                tt(p, "pool", dstT[:, 0, :], ra, rb, ALU.subtract)
                tt(p, "dve", ra, pA[0][:, 0:T], sinT, ALU.mult)
                tt(p, "dve", rb, pA[1][:, 0:T], cosT, ALU.mult)
                tt(p, "pool", dstT[:, 1, :], ra, rb, ALU.add)
            w = ws.next(U_V(h))
            for c in range(NCH):
                proj_tm(w, c, pA[c % 2])
                cp(p, "act", vtok[:, c, :], pA[c % 2][:, :])
            if full:
                w = ws.next(U_G(h))
                for fb in range(4):
                    proj_fm(w, fb * 128, pA[fb % 2])
                    act(p, gT[:, fb, :], pA[fb % 2][:, 0:T], AF.Silu)
            for c in range(NCH):
                cs_ = slice(c * 128, (c + 1) * 128)
                if full:
                    for b in range(2):
                        mm(p, pM[:, 128:256], kT[:, b, cs_], qT[:, b, cs_], b == 0, b == 1)
                    tt(p, "dve", sT, pM[:, 128:256], intraT[:, h, :], ALU.mult)
                    tt(p, "pool", qTs, qT[:, :, cs_], bc(qdec[:, h, :], 1, [128, 2, 128]), ALU.mult)
                    mm(p, pY[:, :], sT, vtok[:, c, :], True, False)
                    mm(p, pY[:, :], qTs[:, 0, :], Rbf[:, h, 0, :], False, False)
                    mm(p, pY[:, :], qTs[:, 1, :], Rbf[:, h, 1, :], False, True)
                    act(p, ysn, pY[:, :], AF.Square, accum=small[:, 8:9])
                    act(p, small[:, 9:10], small[:, 8:9], AF.Ln, scale=1.0 / 512, bias=EPS)
                    act(p, small[:, 10:11], small[:, 9:10], AF.Exp, scale=-0.5)
                    ts(p, "dve", yn, pY[:, :], small[:, 10:11], ALU.mult)
                    for fb in range(4):
                        tr(p, pT[0][:, fb * 128:(fb + 1) * 128], yn[:, fb * 128:(fb + 1) * 128], identb[:], inc=(fb == 3))
                    tt(p, "dve", yrT[:, h * 4:(h + 1) * 4, cs_], pT[0][:, 0:512].rearrange("p (a b) -> p a b", a=4),
                       gT[:, :, cs_], ALU.mult)
                for b in range(2):
                    tr(p, pT[1][:, b * 128:(b + 1) * 128], kT[:, b, cs_], identb[:], inc=(b == 1))
                ts(p, "dve", ktok, pT[1][:, 0:256], kdec[:, h:h + 1], ALU.mult)
                for b in range(2):
                    mm(p, pS[b][:, :], ktok[:, b * 128:(b + 1) * 128], vtok[:, c, :], True, True)
                for b in range(2):
                    stt(p, Rf[:, h, b, :], Rf[:, h, b, :], GAMMA[h] ** 128, pS[b][:, :], ALU.mult, ALU.add)
                    if full:
                        cp(p, "act", Rbf[:, h, b, :], Rf[:, h, b, :])
        if KST < 8:
            raise _Stop()
        if full:
            out_proj(l, ws, first=True)
        if KST < 9:
            raise _Stop()
        for g in range(4):
            w = ws.next(U_X(g))
            for fb in range(4):
                proj_fm(w, fb * 128, pA[fb % 2])
                conv_block(l, t, w, fb * 128, g * 4 + fb, pA[fb % 2], xcT[:, fb, :])
            if full:
                w = ws.next(U_Z(g))
                for c in range(NCH):
                    proj_tm(w, c, pA[c % 2])
                    act(p, sz[:, c, :], pA[c % 2][:, :], AF.Silu)
            hs = slice(8 * g, 8 * g + 8)
            for c in range(NCH):
                cs_ = slice(c * 128, (c + 1) * 128)
                d = dtb[:, c, :]
                for fb in range(4):
                    tr(p, pT[0][:, fb * 128:(fb + 1) * 128], xcT[:, fb, cs_], identb[:], inc=(fb == 3))
                xps = pT[0][:, 0:512].rearrange("p (r q) -> p r q", r=8)
                r3 = lambda ap: ap.rearrange("p (r q) -> p r q", r=8)
                tt(p, "dve", r3(xdts), xps, bc(d[:, 224 + 8 * g:232 + 8 * g], 2, [128, 8, 64]), ALU.mult)
                tr(p, pT[1][:, 256:384], BT[:, g, cs_], identb[:])
                cp(p, "act", Btok, pT[1][:, 256:384])
                if full:
                    cp(p, "act", xtok, pT[0][:, 0:512])
                    tt(p, "dve", r3(xdt), xps, bc(d[:, 64 + 8 * g:72 + 8 * g], 2, [128, 8, 64]), ALU.mult)
                    mm(p, pM[:, 256:384], BT[:, g, cs_], CT[:, g, cs_], True, True)
                    tt(p, "dve", cbm, pM[:, 256:384], triT, ALU.mult)
                    for k in range(2):
                        tt(p, "pool", rlab[:, k, :, :], bc(triTb[:], 1, [128, 8, 128]),
                           bc(lahl[:, c, k, 8 * g:8 * g + 8], 2, [128, 8, 128]), ALU.mult)
                    for hf in range(2):
                        for k in range(2):
                            mm(p, pA[hf][:, :], triUb[:], rlab[:, k, hf * 4:(hf + 1) * 4, :].rearrange("p r i -> p (r i)"), k == 0, k == 1)
                        act(p, esb[:, hf * 4:(hf + 1) * 4, :].rearrange("p r i -> p (r i)"), pA[hf][:, :], AF.Exp)
                    tt(p, "dve", MT, esb, bc(cbm, 1, [128, 8, 128]), ALU.mult)
                    tt(p, "pool", r3(xD[:]), r3(xtok), bc(hp[l][:, 64 + 8 * g:72 + 8 * g], 2, [128, 8, 64]), ALU.mult)
                    mm(p, pY[:, :], identb[:], xD[:], True, False)
                    for r in range(8):
                        mm(p, pY[:, r * 64:(r + 1) * 64], MT[:, r, :], xdt[:, r * 64:(r + 1) * 64], False, r == 7)
                    mm(p, pS[1][:, :], CT[:, g, cs_], Sbf[:, g, :], True, True)
                    tt(p, "dve", r3(ubuf), r3(pS[1][:, :]), bc(d[:, 128 + 8 * g:136 + 8 * g], 2, [128, 8, 64]), ALU.mult)
                    tt(p, "dve", ubuf, ubuf, pY[:, :], ALU.add)
                    tt(p, "dve", ubuf, ubuf, sz[:, c, :], ALU.mult)
                    act(p, ysn, ubuf, AF.Square, accum=small[:, 12:13])
                    act(p, small[:, 13:14], small[:, 12:13], AF.Ln, scale=1.0 / 512, bias=EPS)
                    act(p, small[:, 14:15], small[:, 13:14], AF.Exp, scale=-0.5)
                    ts(p, "dve", ysn, ubuf, small[:, 14:15], ALU.mult)
                    for fb in range(4):
                        tr(p, pT[1][:, 512 + fb * 128:512 + (fb + 1) * 128], ysn[:, fb * 128:(fb + 1) * 128], identb[:], inc=(fb == 3))
                    cp(p, "act", ysT[:, g * 4:(g + 1) * 4, cs_], pT[1][:, 512:1024].rearrange("p (a b) -> p a b", a=4))
                mm(p, pS[0][:, :], Btok, xdts, True, True)
                tt(p, "dve", r3(Sf[:, g, :]), r3(Sf[:, g, :]), bc(d[:, 192 + 8 * g:200 + 8 * g], 2, [128, 8, 64]), ALU.mult)
                tt(p, "dve", Sf[:, g, :], Sf[:, g, :], pS[0][:, :], ALU.add)
                if full:
                    cp(p, "act", Sbf[:, g, :], Sf[:, g, :])
        if not full:
            return
        if KST < 10:
            raise _Stop()
        out_proj(l, ws, first=False)
        if KST < 11:
            raise _Stop()
        for uu in range(2):
            w = ws.next(U_WO(uu))
            for f in range(4):
                fb = uu * 4 + f
                pa = pA[f % 2]
                for kb in range(NKB):
                    mm(p, pa[:, 0:T], w[:, kb * 512 + f * 128:kb * 512 + (f + 1) * 128], mTb[:, kb, :], kb == 0, kb == NKB - 1)
                tt(p, "dve", hT[:, fb, :], hT[:, fb, :], pa[:, 0:T], ALU.add)
        if KST < 12:
            raise _Stop()
        rmsnorm_to_hnT()
        for uu in range(11):
            w = ws.next(U_GU(uu))
            for j in range(2):
                proj_fm(w, j * 128, pA[0])
                proj_fm(w, 256 + j * 128, pA[1])
                ra, _, _ = nxt()
                act(p, ra, pA[0][:, 0:T], AF.Silu)
                tt(p, "dve", actT[:, 2 * uu + j, :], ra, pA[1][:, 0:T], ALU.mult)
        for fb in range(8):
            w = ws.next(U_WD(fb))
            pa = pA[fb % 2]
            for kb in range(NFF):
                mm(p, pa[:, 0:T], w[:, kb * 128:(kb + 1) * 128], actT[:, kb, :], kb == 0, kb == NFF - 1)
            tt(p, "dve", hT[:, fb, :], hT[:, fb, :], pa[:, 0:T], ALU.add)
        if KST < 13:
            raise _Stop()
        if last_layer:
            act(p, hnT[:], hT[:], AF.Square)
            for kb in range(NKB):
                mm(p, pM[:, 0:T], onesb[:], hnT[:, kb, :], kb == 0, kb == NKB - 1)
            act(p, lnv[:], pM[:, 0:T], AF.Ln, scale=1.0 / D, bias=EPS)
            act(p, rbc[:], lnv[:], AF.Exp, scale=-0.5)
            for kb in range(NKB):
                stt(p, hT[:, kb, :], hT[:, kb, :], fnw[:, kb:kb + 1], rbc[:], ALU.mult, ALU.mult)
        p.dma(dst[:, :, tsl].rearrange("b p t -> p b t"), hT[:], "hT")

    def out_proj(l, ws_, first):
        for uu in range(4):
            w = ws_.next(U_RO(uu) if first else U_SO(uu))
            wg = ws_.next(U_GATE(uu))
            for j in range(2):
                fb = uu * 2 + j
                for kb in range(16):
                    mm(p, pY[:, 0:T], w[:, kb * 256 + j * 128:kb * 256 + (j + 1) * 128], yrT[:, kb, :], kb == 0, kb == 15)
                proj_fm(wg, (0 if first else 256) + j * 128, pA[j])
                ra, rb, _ = nxt()
                act(p, ra, pA[j][:, 0:T], AF.Sigmoid)
                if first:
                    tt(p, "dve", mTb[:, fb, :], ra, pY[:, 0:T], ALU.mult)
                else:
                    tt(p, "dve", rb, ra, pY[:, 0:T], ALU.mult)
                    tt(p, "pool", mTb[:, fb, :], mTb[:, fb, :], rb, ALU.add)

    def p2_seq():
        s = [U_BCB, U_BCC]
        for h in range(4):
            s += [U_A(h), U_V(h), U_G(h)]
        for uu in range(4):
            s += [U_RO(uu), U_GATE(uu)]
        for g in range(4):
            s += [U_X(g), U_Z(g)]
        for uu in range(4):
            s += [U_SO(uu), U_GATE(uu)]
        s += [U_WO(0), U_WO(1)] + [U_GU(i) for i in range(11)] + [U_WD(f) for f in range(8)]
        return s

    def p1_seq():
        s = [U_BCB]
        for h in range(4):
            s += [U_A(h), U_V(h)]
        s += [U_X(g) for g in range(4)]
        return s

    try:
        for kind, l in phases:
            src = hin
            if KST < 3:
                break
            halo_norm(halo_in)
            if KST < 4:
                break
            if kind == "p1":
                p.op("pool", lambda e: e.memset(Rf[:].rearrange("p a b c -> p (a b c)"), 0.0), [], [Rf[:]])
                p.op("pool", lambda e: e.memset(Sf[:].rearrange("p a b -> p (a b)"), 0.0), [], [Sf[:]])
                p.op("pool", lambda e: e.memset(LAt[:], 0.0), [], [LAt[:]])
                ws = Stream(l, p1_seq() * NTILE)
                for t in range(NTILE):
                    tile_body(l, t, False, src, None, ws, False)
                for h in range(NH):
                    for b in range(2):
                        p.dma(stloc[h * 2 + b, :, :], Rf[:, h, b, :], "Rf")
                for g in range(4):
                    p.dma(stloc[8 + g, :, :], Sf[:, g, :], "Sf")
                p.op("pool", lambda e: e.memset(rbc[:, 0:512], 0.0), [], [rbc[:, 0:512]])
                cp(p, "dve", rbc[:, 0:32], LAt[:])
                p.dma(stloc[12, :, :], rbc[:, 0:512], "rbc")
            else:
                for blk in range(8):
                    h, b = divmod(blk, 2)
                    p.dma(FS[:, 0:2048].rearrange("p (j f) -> p j f", j=4), stall[:, blk, :, :].rearrange("j p f -> p j f"), "FS")
                    ts(p, "dve", Rf[:, h, b, :], FS[:, 0:512], cf[:, h * 4:h * 4 + 1], ALU.mult)
                    for j in range(1, 4):
                        stt(p, Rf[:, h, b, :], FS[:, j * 512:(j + 1) * 512], cf[:, h * 4 + j:h * 4 + j + 1], Rf[:, h, b, :], ALU.mult, ALU.add)
                    cp(p, "act", Rbf[:, h, b, :], Rf[:, h, b, :])
                LAa = FS[:, 2048:2048 + 128].rearrange("p (j f) -> p j f", j=4)
                p.dma(LAa, stall[:, 12, :, 0:32].rearrange("j p f -> p j f"), "FSla")
                Ej = FS[:, 2176:2176 + 128].rearrange("p (j f) -> p j f", j=4)
                for j in range(4):
                    ts(p, "dve", Ej[:, j, :], LAa[:, 0, :], cf[:, 20 + j * 4:21 + j * 4], ALU.mult)
                    for m in range(1, 4):
                        stt(p, Ej[:, j, :], LAa[:, m, :], cf[:, 20 + j * 4 + m:21 + j * 4 + m], Ej[:, j, :], ALU.mult, ALU.add)
                    act(p, Ej[:, j, :], Ej[:, j, :], AF.Exp)
                    ts(p, "dve", Ej[:, j, :], Ej[:, j, :], cf[:, 16 + j:17 + j], ALU.mult)
                for g in range(4):
                    p.dma(FS[:, 0:2048].rearrange("p (j f) -> p j f", j=4), stall[:, 8 + g, :, :].rearrange("j p f -> p j f"), "FS")
                    r3 = lambda ap: ap.rearrange("p (r q) -> p r q", r=8)
                    tt(p, "dve", r3(Sf[:, g, :]), r3(FS[:, 0:512]), bc(Ej[:, 0, 8 * g:8 * g + 8], 2, [128, 8, 64]), ALU.mult)
                    for j in range(1, 4):
                        tt(p, "dve", r3(FS[:, j * 512:(j + 1) * 512]), r3(FS[:, j * 512:(j + 1) * 512]),
                           bc(Ej[:, j, 8 * g:8 * g + 8], 2, [128, 8, 64]), ALU.mult)
                        tt(p, "dve", Sf[:, g, :], Sf[:, g, :], FS[:, j * 512:(j + 1) * 512], ALU.add)
                    cp(p, "act", Sbf[:, g, :], Sf[:, g, :])
                ws = Stream(l, p2_seq() * NTILE)
                for t in range(NTILE):
                    tile_body(l, t, True, src, hout, ws, (l == DEPTH - 1) and not globals().get('_NOFINAL', False))
    except _Stop:
        pass
    if p.log is not None:
        with open(os.environ['KDUMP'], 'w') as f:
            for r in p.log:
                f.write(repr(r) + '\n')
    p.finish()
    es.close()
    return nc


NSLOT = 14


def build_convert():
    nc = bass.Bass("TRN2", target_bir_lowering=False)
    es = ExitStack()
    p = Prog(nc, es)
    src = nc.dram_tensor("usrc", [NSLOT, 128, 4096], F32, kind="ExternalInput").ap()
    scd = nc.dram_tensor("usc", [128, NSLOT * 32], F32, kind="ExternalInput").ap()
    dst = nc.dram_tensor("wpart", [NSLOT, 128, 4096], BF16, kind="ExternalOutput").ap()
    sc = es.enter_context(nc.sbuf_tensor("sc_sb", [128, NSLOT, 32], F32))
    stg = [es.enter_context(nc.sbuf_tensor("stg%d" % i, [128, 32, 128], F32)) for i in range(3)]
    wb = [es.enter_context(nc.sbuf_tensor("wb%d" % i, [128, 32, 128], BF16)) for i in range(3)]
    p.dma(sc[:].rearrange("p a b -> p (a b)"), scd[:, :], "sc")
    for i in range(NSLOT + 2):
        if i < NSLOT:
            p.dma(stg[i % 3][:].rearrange("p a b -> p (a b)"), src[i, :, :], "stg%d" % (i % 3), eng=("sp" if i % 2 == 0 else "act"))
        j = i - 2
        if 0 <= j < NSLOT:
            st, o = stg[j % 3], wb[j % 3]
            tt(p, "dve", o[:, 0:16, :], st[:, 0:16, :], bc(sc[:, j, 0:16], 2, [128, 16, 128]), ALU.mult)
            tt(p, "pool", o[:, 16:32, :], st[:, 16:32, :], bc(sc[:, j, 16:32], 2, [128, 16, 128]), ALU.mult)
            if j == 0:
                kv = o[:].rearrange("p (a b) c -> p a b c", b=4)[:, :, 2:4, :]
                ts(p, "dve", kv, kv, 1.0 / 16, ALU.mult)
            p.dma(dst[j, :, :], o[:].rearrange("p a b -> p (a b)"), "wb%d" % (j % 3))
    p.finish()
    es.close()
    return nc


def _unit_src(l, u, P):
    data = np.zeros((128, 4096), np.float32)
    sc = np.ones((128, 32), np.float32)
    nmx = P["norm_mix_w"][l].reshape(8, 128).T
    nfx = P["norm_ffn_w"][l].reshape(8, 128).T
    nsx = P["ssd_norm_w"][l].reshape(16, 128).T

    def put(mat, kb, c0, w, dc, scv):
        data[:, dc:dc + w] = mat[kb * 128:(kb + 1) * 128, c0:c0 + w]
        if scv is not None:
            sc[:, dc // 128:(dc + w) // 128] = scv[:, kb:kb + 1]

    w_in = P["w_in"][l]
    if u in (U_BCB, U_BCC):
        c0 = C_B if u == U_BCB else C_C
        for kb in range(8):
            put(w_in, kb, c0, 512, kb * 512, nmx)
    elif 2 <= u < 14:
        h, k = divmod(u - 2, 3)
        for kb in range(8):
            if k == 0:
                put(w_in, kb, C_Q + h * 256, 256, kb * 512, nmx)
                put(w_in, kb, C_K + h * 256, 256, kb * 512 + 256, nmx)
            else:
                put(w_in, kb, (C_V if k == 1 else C_G) + h * 512, 512, kb * 512, nmx)
    elif 14 <= u < 22:
        g, k = divmod(u - 14, 2)
        for kb in range(8):
            put(w_in, kb, (C_X if k == 0 else C_Z) + g * 512, 512, kb * 512, nmx)
    elif 22 <= u < 34:
        uu, k = divmod(u - 22, 3)
        if k == 0:
            for kb in range(16):
                put(P["ret_out"][l], kb, uu * 256, 256, kb * 256, None)
        elif k == 1:
            for kb in range(16):
                put(P["ssd_out"][l], kb, uu * 256, 256, kb * 256, nsx)
        else:
            for kb in range(8):
                put(w_in, kb, C_GR + uu * 256, 256, kb * 512, nmx)
                put(w_in, kb, C_GS + uu * 256, 256, kb * 512 + 256, nmx)
    elif u in (34, 35):
        for kb in range(8):
            put(P["w_o"][l], kb, (u - 34) * 512, 512, kb * 512, None)
    elif 36 <= u < 47:
        uu = u - 36
        for kb in range(8):
            put(P["w_gate_up"][l], kb, uu * 256, 256, kb * 512, nfx)
            put(P["w_gate_up"][l], kb, DFF + uu * 256, 256, kb * 512 + 256, nfx)
    else:
        f = u - 47
        for kb in range(NFF):
            put(P["w_down"][l], kb, f * 128, 128, kb * 128, None)
    return data, sc


def _convert_weights(P):
    a_units = [(l, U_A(h)) for l in range(DEPTH) for h in range(NH)]
    rest = [(l, u) for l in range(DEPTH) for u in range(NUNIT) if (l, u) not in a_units]
    assign = []
    for c in range(8):
        mine = [a_units[c]] + rest[c * (NSLOT - 1):(c + 1) * (NSLOT - 1)]
        assign.append(mine)
    maps = []
    for c in range(8):
        usrc = np.zeros((NSLOT, 128, 4096), np.float32)
        usc = np.ones((128, NSLOT, 32), np.float32)
        for i, (l, u) in enumerate(assign[c]):
            usrc[i], usc[:, i, :] = _unit_src(l, u, P)
        maps.append({"usrc": usrc, "usc": np.ascontiguousarray(usc.reshape(128, NSLOT * 32))})
    res = run_bass_kernel_spmd(build_convert(), maps, core_ids=list(range(8)))
    first = np.asarray(res.results[0]["wpart"])
    wu = [np.zeros((NUNIT, 128, 4096), first.dtype) for _ in range(DEPTH)]
    for c in range(8):
        part = np.asarray(res.results[c]["wpart"])
        for i, (l, u) in enumerate(assign[c]):
            wu[l][u] = part[i]
    return wu


def _consts():
    j = np.arange(128)
    cst = np.zeros((128, 1540), np.float32)
    cst[:, 0:128] = np.eye(128)
    cst[:, 128:256] = (j[:, None] <= j[None, :])
    cst[:, 256:384] = (j[:, None] > j[None, :])
    cst[:, 384:512] = 1.0
    for h in range(NH):
        g = GAMMA[h]
        diff = j[None, :] - j[:, None]
        cst[:, 512 + h * 128:512 + (h + 1) * 128] = np.where(diff >= 0, g ** np.maximum(diff, 0).astype(np.float64), 0.0)
        cst[:, 1024 + h * 128:1024 + (h + 1) * 128] = (g ** (j + 1.0))[None, :]
        cst[:, 1536 + h] = g ** (127.0 - j)
    return cst


def _rope(pos):
    inv = np.float32(10000.0) ** (-(np.arange(128, dtype=np.float32) / np.float32(128)))
    ang = pos.astype(np.float32)[:, None] * inv[None, :].astype(np.float32)
    return np.stack([np.cos(ang).T, np.sin(ang).T]).astype(np.float32)


def _coef(s, NT):
    cf = np.zeros((128, 36), np.float32)
    for h in range(NH):
        for j in range(4):
            if j < s:
                cf[:, h * 4 + j] = GAMMA[h] ** (float(NT) * (s - 1 - j))
    for j in range(4):
        cf[:, 16 + j] = 1.0 if j < s else 0.0
        for m in range(4):
            cf[:, 20 + j * 4 + m] = 1.0 if (j < m < s) else 0.0
    return cf


def _fm(a):
    return np.ascontiguousarray(a.T.reshape(NKB, 128, a.shape[0]))


def _layer_inputs(l, P):
    nmv = np.concatenate([P["norm_mix_w"][l].reshape(8, 128).T, P["norm_ffn_w"][l].reshape(8, 128).T,
                          P["ssd_norm_w"][l].reshape(16, 128).T], axis=1)
    cwv = np.zeros((128, 24, 5), np.float32)
    cwv[:, :, 0:4] = P["conv_w"][l].reshape(4, 24, 128).transpose(2, 1, 0)
    cwv[:, :, 4] = P["conv_b"][l].reshape(24, 128).T
    hpv = np.concatenate([np.broadcast_to(P[k][l][None, :], (128, 32)) for k in ("dt_bias", "a_log", "d_skip")], axis=1)
    wdts = P["w_in"][l][:, C_DT:C_DT + 32].reshape(8, 128, 32).transpose(1, 0, 2).reshape(128, 256)
    return {
        "nm%d" % l: np.ascontiguousarray(nmv, np.float32), "cw%d" % l: np.ascontiguousarray(cwv.reshape(128, 120)),
        "hp%d" % l: np.ascontiguousarray(hpv, np.float32), "wdts%d" % l: np.ascontiguousarray(wdts, np.float32),
    }


_T_DEFAULT = 512


def kernel(x, norm_mix_w, w_in, ret_out, conv_w, conv_b, dt_bias, a_log, d_skip, ssd_norm_w,
           ssd_out, w_o, norm_ffn_w, w_gate_up, w_down, final_norm_w, _T=None):
    P = dict(norm_mix_w=norm_mix_w, w_in=w_in, ret_out=ret_out, conv_w=conv_w, conv_b=conv_b, dt_bias=dt_bias,
             a_log=a_log, d_skip=d_skip, ssd_norm_w=ssd_norm_w, ssd_out=ssd_out, w_o=w_o, norm_ffn_w=norm_ffn_w,
             w_gate_up=w_gate_up, w_down=w_down)
    P = {k: np.asarray(v, np.float32) for k, v in P.items()}
    x = np.asarray(x, np.float32)
    B, S, _ = x.shape
    NSEG = 4
    NT = S // NSEG
    T = _T or min(_T_DEFAULT, NT)
    ncore = B * NSEG
    assert ncore == 8
    cst = _consts()
    fnw = np.ascontiguousarray(np.asarray(final_norm_w, np.float32).reshape(8, 128).T)
    common = []
    for c in range(ncore):
        b, s = divmod(c, NSEG)
        common.append({"cst": cst, "cs": _rope(np.arange(s * NT, (s + 1) * NT)), "cf": _coef(s, NT), "fnw": fnw})

    def halo_of(hfm_prev):
        h = np.zeros((NKB, 128, 4), np.float32)
        if hfm_prev is not None:
            h[:, :, 0:3] = hfm_prev[:, :, -3:]
        return h

    hcur = [_fm(x[c // NSEG, (c % NSEG) * NT:((c % NSEG) + 1) * NT, :]) for c in range(ncore)]
    wu = _convert_weights(P)
    for l in range(DEPTH):
        li = _layer_inputs(l, P)
        li["wu%d" % l] = wu[l]
        halos = [halo_of(hcur[c - 1] if c % NSEG else None) for c in range(ncore)]
        nc1 = build(NT, T, [("p1", l)])
        maps = [dict(common[c], hin=hcur[c], halo=halos[c], **li) for c in range(ncore)]
        r1 = run_bass_kernel_spmd(nc1, maps, core_ids=list(range(ncore)))
        st = [np.asarray(r["stloc"]) for r in r1.results]
        nc2 = build(NT, T, [("p2", l)])
        maps = []
        for c in range(ncore):
            b = c // NSEG
            stall = np.stack([st[b * NSEG + j] for j in range(NSEG)])
            maps.append(dict(common[c], hin=hcur[c], halo=halos[c], stall=stall, **li))
        r2 = run_bass_kernel_spmd(nc2, maps, core_ids=list(range(ncore)))
        hcur = [np.asarray(r["hout"]) for r in r2.results]
    out = np.empty((B, S, D), np.float32)
    for c in range(ncore):
        b, s = divmod(c, NSEG)
        out[b, s * NT:(s + 1) * NT, :] = hcur[c].reshape(D, NT).T
    return out
```

```python
import os
import numpy as np
from contextlib import ExitStack
import concourse.bass as bass
import concourse.mybir as mybir
from concourse.bass_utils import run_bass_kernel_spmd

F32, BF16 = mybir.dt.float32, mybir.dt.bfloat16
AF = mybir.ActivationFunctionType
ALU = mybir.AluOpType

D = 1024
NKB = 8
DEPTH = 2
EPS = 1e-6
NH = 4
DFF = 2816
NFF = 22
DIN = 13344
C_Q, C_K, C_V, C_G, C_Z, C_X, C_B, C_C, C_DT, C_GR, C_GS = 0, 1024, 2048, 4096, 6144, 8192, 10240, 10752, 11264, 11296, 12320
NUNIT = 55
U_BCB, U_BCC = 0, 1
def U_A(h): return 2 + 3 * h
def U_V(h): return 3 + 3 * h
def U_G(h): return 4 + 3 * h
def U_X(g): return 14 + 2 * g
def U_Z(g): return 15 + 2 * g
def U_RO(u): return 22 + 3 * u
def U_SO(u): return 23 + 3 * u
def U_GATE(u): return 24 + 3 * u
def U_WO(u): return 34 + u
def U_GU(u): return 36 + u
def U_WD(f): return 47 + f
GAMMA = [1.0 - 2.0 ** (-5.0 - h) for h in range(NH)]
NST = 13

ENG = dict(pe="tensor", act="scalar", dve="vector", pool="gpsimd", sp="sync")


def _prod(xs):
    r = 1
    for v in xs:
        r *= int(v)
    return r


def _iv(ap):
    t = ap.tensor
    dims = [(int(s), int(c)) for s, c in ap.ap]
    off = int(ap.offset)
    if "DRAM" in str(ap.space).upper():
        hi = off + sum(s * (c - 1) for s, c in dims if s > 0) + 1
        return (t.name, 0, 1, off, hi)
    if "PSUM" in str(ap.space).upper():
        return (t.name, 0, 128, 0, 1 << 30)
    rows = _prod(list(t.shape)[1:])
    p0, f0 = off // rows, off % rows
    fhi = f0 + sum(s * (c - 1) for s, c in dims[1:] if s > 0) + 1
    return (t.name, p0, p0 + dims[0][1], f0, fhi)


class _Stop(Exception):
    pass


class Prog:
    def __init__(self, nc, es):
        self.nc, self.es = nc, es
        self.ops = {e: [] for e in ENG}
        self.cnt = {e: 0 for e in ENG}
        self.sems = {}
        self.semval = {}
        self.waited = {e: {} for e in ENG}
        self.acc = {}
        self.nsem = 0
        self.log = [] if os.environ.get('KDUMP') else None

    def sem(self, key):
        if key not in self.sems:
            self.nsem += 1
            self.sems[key] = self.es.enter_context(self.nc.semaphore("s%d" % self.nsem))
        return self.sems[key]

    def _need(self, eng, reads, writes):
        need = {}

        def add(tok):
            k, v = tok
            if k == "pe" and eng == "pe":
                return
            if v > need.get(k, 0):
                need[k] = v

        for ap in reads:
            n, p0, p1, f0, f1 = _iv(ap)
            for r in self.acc.get(n, ()):
                if r[0] == "w" and r[1] < p1 and p0 < r[2] and r[3] < f1 and f0 < r[4]:
                    add(r[5])
        for ap in writes:
            n, p0, p1, f0, f1 = _iv(ap)
            for r in self.acc.get(n, ()):
                if r[1] < p1 and p0 < r[2] and r[3] < f1 and f0 < r[4]:
                    add(r[5])
        out = []
        for k, v in need.items():
            if k.startswith("d:"):
                v = max(v, self.semval[k])
            if self.waited[eng].get(k, 0) < v:
                self.waited[eng][k] = v
                out.append((k, v))
        return out

    def _rec(self, kind, ap, tok):
        n, p0, p1, f0, f1 = _iv(ap)
        L = self.acc.setdefault(n, [])
        if kind == "w":
            L[:] = [r for r in L if not (p0 <= r[1] and r[2] <= p1 and f0 <= r[3] and r[4] <= f1)]
        else:
            L[:] = [r for r in L if not (r[0] == "r" and r[5][0] == tok[0] and p0 <= r[1] and r[2] <= p1
                                         and f0 <= r[3] and r[4] <= f1)]
        L.append((kind, p0, p1, f0, f1, tok))

    def op(self, eng, fn, reads=(), writes=(), inc=True):
        self.sem(eng)
        psr = [a for a in reads if "PSUM" in str(a.space).upper()]
        if psr:
            reads = [a for a in reads if "PSUM" not in str(a.space).upper()]
            writes = list(writes) + psr
        waits = self._need(eng, reads, writes)
        idx = self.cnt[eng] + 1
        if inc:
            self.cnt[eng] = idx
        tok = (eng, idx)
        for ap in reads:
            self._rec("r", ap, tok)
        for ap in writes:
            self._rec("w", ap, tok)
        self.ops[eng].append((waits, fn, (eng, 1) if inc else None))
        if self.log is not None:
            self.log.append((eng, idx, inc, waits, [_iv(a) for a in reads], [_iv(a) for a in writes]))

    def dma(self, out, in_, key, eng="sp"):
        k = "d:" + key
        self.sem(k)
        waits = self._need(eng, [in_], [out])
        v = self.semval.get(k, 0) + 16
        self.semval[k] = v
        tok = (k, v)
        self._rec("r", in_, tok)
        self._rec("w", out, tok)
        self.ops[eng].append((waits, lambda e: e.dma_start(out=out, in_=in_), (k, 16)))
        if self.log is not None:
            self.log.append((eng, tok, True, waits, [_iv(in_)], [_iv(out)]))

    def finish(self):
        for k, v in self.semval.items():
            if self.waited["sp"].get(k, 0) < v:
                self.ops["sp"].append(([(k, v)], None, None))
        block = self.es.enter_context(self.nc.Block())
        for eng, attr in ENG.items():
            ops = self.ops[eng]
            if not ops:
                continue

            def body(e, ops=ops):
                for waits, fn, inc in ops:
                    for k, v in waits:
                        e.wait_ge(self.sems[k], v)
                    if fn is None:
                        continue
                    ins = fn(e)
                    if inc is not None:
                        ins.then_inc(self.sems[inc[0]], inc[1])

            getattr(block, attr)(body)


def mm(p, out, lhsT, rhs, start, stop, inc=None):
    if inc is None:
        inc = stop
    p.op("pe", lambda e: e.matmul(out, lhsT=lhsT, rhs=rhs, start=start, stop=stop), [lhsT, rhs], [out], inc=inc)


def tr(p, out, in_, ident, inc=True):
    p.op("pe", lambda e: e.transpose(out, in_, ident), [in_, ident], [out], inc=inc)


def act(p, out, in_, func, bias=None, scale=None, accum=None):
    kw = {}
    rd = [in_]
    if bias is not None:
        kw["bias"] = bias
        if not isinstance(bias, (int, float)):
            rd.append(bias)
    if scale is not None:
        kw["scale"] = scale
        if not isinstance(scale, (int, float)):
            rd.append(scale)
    wr = [out]
    if accum is not None:
        kw["accum_out"] = accum
        wr.append(accum)
    p.op("act", lambda e: e.activation(out=out, in_=in_, func=func, **kw), rd, wr)


def tt(p, eng, out, a, b, op):
    p.op(eng, lambda e: e.tensor_tensor(out=out, in0=a, in1=b, op=op), [a, b], [out])


def ts(p, eng, out, a, s1, op0, s2=None, op1=None):
    rd = [a] + [s for s in (s1, s2) if s is not None and not isinstance(s, (int, float))]
    if op1 is None:
        p.op(eng, lambda e: e.tensor_scalar(out=out, in0=a, scalar1=s1, scalar2=None, op0=op0), rd, [out])
    else:
        p.op(eng, lambda e: e.tensor_scalar(out=out, in0=a, scalar1=s1, scalar2=s2, op0=op0, op1=op1), rd, [out])


def stt(p, out, in0, scalar, in1, op0, op1):
    rd = [in0, in1] + ([] if isinstance(scalar, (int, float)) else [scalar])
    p.op("dve", lambda e: e.scalar_tensor_tensor(out=out, in0=in0, scalar=scalar, in1=in1, op0=op0, op1=op1), rd, [out])


def cp(p, eng, out, in_):
    if eng == "act":
        p.op("act", lambda e: e.copy(out=out, in_=in_), [in_], [out])
    else:
        p.op(eng, lambda e: e.tensor_copy(out=out, in_=in_), [in_], [out])


def bc(ap, axis, shape):
    return ap.unsqueeze(axis).to_broadcast(list(shape))


def build(NT, T, phases, fused=False):
    NCH = T // 128
    NTILE = NT // T
    assert T % 128 == 0 and NT % T == 0
    nc = bass.Bass("TRN2", target_bir_lowering=False)
    es = ExitStack()
    p = Prog(nc, es)
    layers = sorted(set(l for _, l in phases))

    def din(name, shape, dt=F32):
        return nc.dram_tensor(name, list(shape), dt, kind="ExternalInput").ap()

    def dout(name, shape, dt=F32):
        return nc.dram_tensor(name, list(shape), dt, kind="ExternalOutput").ap()

    def dint(name, shape, dt=F32):
        return nc.dram_tensor(name, list(shape), dt).ap()

    hin = din("hin", [NKB, 128, NT])
    halo_in = din("halo", [NKB, 128, 4])
    cst_d = din("cst", [128, 1540])
    cs_d = din("cs", [2, 128, NT])
    cf_d = din("cf", [128, 36])
    fnw_d = din("fnw", [128, 8])
    W = {}
    for l in layers:
        W[l] = dict(nm=din("nm%d" % l, [128, 32]), cw=din("cw%d" % l, [128, 120]), hp=din("hp%d" % l, [128, 96]),
                    wu=din("wu%d" % l, [NUNIT, 128, 4096], BF16), wdts=din("wdts%d" % l, [128, 256]))
    if not fused:
        (kind, lay), = phases
        if kind == "p1":
            stloc = dout("stloc", [NST, 128, 512])
        else:
            stall = din("stall", [4, NST, 128, 512])
            hout = dout("hout", [NKB, 128, NT])

    sb = lambda name, shape, dt=F32: es.enter_context(nc.sbuf_tensor(name, list(shape), dt))
    ps = lambda name, shape, dt=F32: es.enter_context(nc.psum_tensor(name, list(shape), dt))

    cst = sb("cst_sb", [128, 1540])
    identb = sb("identb", [128, 128], BF16)
    onesb = sb("onesb", [128, 128], BF16)
    triTb = sb("triTb", [128, 128], BF16)
    triUb = sb("triUb", [128, 128], BF16)
    lahl = sb("lahl", [128, NCH, 2, 32], BF16)
    rlab = sb("rlab", [128, 2, 8, 128], BF16)
    xD = sb("xD", [128, 512], BF16)
    cossin = sb("cossin", [128, 2, T])
    cf = sb("cf_sb", [128, 36])
    fnw = sb("fnw_sb", [128, 8])
    nm = {l: sb("nm_sb%d" % l, [128, 32]) for l in layers}
    cw = {l: sb("cw_sb%d" % l, [128, 24, 5]) for l in layers}
    hp = {l: sb("hp_sb%d" % l, [128, 96]) for l in layers}
    atile = {l: sb("atile%d" % l, [128, 32]) for l in layers}
    wdt = {l: sb("wdt%d" % l, [128, NKB, 32], BF16) for l in layers}
    hT = sb("hT", [128, NKB, T])
    hnT = sb("hnT", [128, NKB, T], BF16)
    rbc = sb("rbc", [128, T])
    lnv = sb("lnv", [128, T])
    hnTh = sb("hnTh", [128, NKB, 4], BF16)
    haloT = sb("haloT", [128, NKB, 4])
    small = sb("small", [128, 64])
    NW = 4
    wst = [sb("wst%d" % i, [128, 4096], BF16) for i in range(NW)]
    FS = sb("FS", [128, 3 * T + 4 + NCH * 256 + 128 + 1024 + 1024 + 3 * T + 4])
    BS = sb("BS", [128, 8 * T + 4 * T + NCH * 512 + 1536 + 128 + 2048 + 512 + 64], BF16)
    U = sb("U", [128, 24 * T], BF16)
    hist = sb("hist", [128, 24, 4])
    Rf = sb("Rf", [128, NH, 2, 512])
    Rbf = sb("Rbf", [128, NH, 2, 512], BF16)
    Sf = sb("Sf", [128, 4, 512])
    Sbf = sb("Sbf", [128, 4, 512], BF16)
    LAt = sb("LAt", [128, 32])
    pA = [ps("pA0", [128, 512]), ps("pA1", [128, 512])]
    pY = ps("pY", [128, 512])
    pS = [ps("pS0", [128, 512]), ps("pS1", [128, 512])]
    pM = ps("pM", [128, 512])
    pT = [ps("pT0", [128, 1024], BF16), ps("pT1", [128, 1024], BF16)]

    identf = cst[:, 0:128]
    triT = cst[:, 128:256]
    triU = cst[:, 256:384]
    onesf = cst[:, 384:512]
    intraT = cst[:, 512:1024].rearrange("p (h i) -> p h i", h=4)
    qdec = cst[:, 1024:1536].rearrange("p (h i) -> p h i", h=4)
    kdec = cst[:, 1536:1540]
    _o2 = 3 * T + 4 + NCH * 256 + 128 + 1024 + 1024
    _tmp = [(FS[:, 0:T], FS[:, T:2 * T], FS[:, 2 * T:3 * T + 4]),
            (FS[:, _o2:_o2 + T], FS[:, _o2 + T:_o2 + 2 * T], FS[:, _o2 + 2 * T:_o2 + 3 * T + 4])]
    _tsel = [0]

    def nxt():
        _tsel[0] ^= 1
        return _tmp[_tsel[0]]
    ra, rb, xraw = _tmp[0]
    o = 3 * T + 4
    dtb = FS[:, o:o + NCH * 256].rearrange("p (c f) -> p c f", c=NCH)
    o += NCH * 256
    cbm = FS[:, o:o + 128]
    o += 128
    rla = FS[:, o:o + 1024].rearrange("p (r i) -> p r i", r=8)
    o += 1024
    ubuf = FS[:, o:o + 512]
    vbuf = FS[:, o + 512:o + 1024]
    BT = BS[:, 0:4 * T].rearrange("p (g t) -> p g t", g=4)
    CT = BS[:, 4 * T:8 * T].rearrange("p (g t) -> p g t", g=4)
    o = 8 * T
    qT = BS[:, o:o + 2 * T].rearrange("p (b t) -> p b t", b=2)
    kT = BS[:, o + 2 * T:o + 4 * T].rearrange("p (b t) -> p b t", b=2)
    r2 = o + 4 * T
    vtok = BS[:, r2:r2 + NCH * 512].rearrange("p (c f) -> p c f", c=NCH)
    r2 += NCH * 512
    sT = BS[:, r2:r2 + 128]
    qTs = BS[:, r2 + 128:r2 + 384].rearrange("p (b i) -> p b i", b=2)
    yn = BS[:, r2 + 384:r2 + 896]
    ktok = BS[:, r2 + 896:r2 + 1152]
    gT = U[:, 16 * T:20 * T].rearrange("p (b t) -> p b t", b=4)
    xcT = BS[:, o:o + 4 * T].rearrange("p (b t) -> p b t", b=4)
    s2 = o + 4 * T
    sz = BS[:, s2:s2 + NCH * 512].rearrange("p (c f) -> p c f", c=NCH)
    s2 += NCH * 512
    xtok = BS[:, s2:s2 + 512]
    xdt = BS[:, s2 + 512:s2 + 1024]
    xdts = BS[:, s2 + 1024:s2 + 1536]
    s2 += 1536
    Btok = BS[:, s2:s2 + 128]
    s2 += 128
    esb = BS[:, s2:s2 + 1024].rearrange("p (r i) -> p r i", r=8)
    MT = BS[:, s2 + 1024:s2 + 2048].rearrange("p (r i) -> p r i", r=8)
    s2 += 2048
    ysn = BS[:, s2:s2 + 512]
    yrT = U[:, 0:16 * T].rearrange("p (b t) -> p b t", b=16)
    ysT = yrT
    mTb = U[:, 16 * T:24 * T].rearrange("p (b t) -> p b t", b=8)
    actT = U[:, 0:22 * T].rearrange("p (b t) -> p b t", b=22)
    stage = hT[:, :, :].rearrange("p a b -> p (a b)")
    assert NKB * T >= 4096

    p.dma(cst[:], cst_d[:, :], "cst")
    p.dma(cf[:], cf_d[:, :], "cf")
    p.dma(fnw[:], fnw_d[:, :], "fnw")
    cp(p, "dve", identb[:], identf)
    cp(p, "dve", onesb[:], onesf)
    cp(p, "dve", triTb[:], triT)
    cp(p, "dve", triUb[:], triU)
    for l in layers:
        p.dma(nm[l][:], W[l]["nm"][:, :], "nm%d" % l)
        p.dma(cw[l][:].rearrange("p a b -> p (a b)"), W[l]["cw"][:, :], "cw%d" % l)
        p.dma(hp[l][:], W[l]["hp"][:, :], "hp%d" % l)
        act(p, atile[l][:], hp[l][:, 32:64], AF.Exp)
        ts(p, "dve", atile[l][:], atile[l][:], -1.0, ALU.mult)

    KST = int(os.environ.get('KSTAGE', '99'))
    conv_rr = [0]

    def convert_piece(dst, src, scale, mul=None):
        e = ("dve", "act", "pool")[conv_rr[0] % 3]
        conv_rr[0] += 1
        if scale is None:
            cp(p, e, dst, src)
        elif e == "act":
            act(p, dst, src, AF.Copy, scale=scale)
            if mul is not None:
                ts(p, "dve", dst, dst, mul, ALU.mult)
        elif e == "dve":
            if mul is None:
                ts(p, "dve", dst, src, scale, ALU.mult)
            else:
                ts(p, "dve", dst, src, scale, ALU.mult, mul, ALU.mult)
        else:
            ts(p, "pool", dst, src, scale, ALU.mult, 1.0 if mul is None else mul, ALU.mult)

    def unit_pieces(l, u):
        Wl = W[l]
        nmx = lambda kb: nm[l][:, kb:kb + 1]
        nfx = lambda kb: nm[l][:, 8 + kb:9 + kb]
        nsx = lambda kb: nm[l][:, 16 + kb:17 + kb]
        out = []
        rows = lambda kb: slice(kb * 128, (kb + 1) * 128)
        if u in (U_BCB, U_BCC):
            c0 = C_B if u == U_BCB else C_C
            for kb in range(8):
                out.append((Wl["w_in"][rows(kb), c0:c0 + 512], kb * 512, 512, nmx(kb), None))
        elif 2 <= u < 14:
            h, k = divmod(u - 2, 3)
            for kb in range(8):
                if k == 0:
                    out.append((Wl["w_in"][rows(kb), C_Q + h * 256:C_Q + h * 256 + 256], kb * 512, 256, nmx(kb), None))
                    out.append((Wl["w_in"][rows(kb), C_K + h * 256:C_K + h * 256 + 256], kb * 512 + 256, 256, nmx(kb), 1.0 / 16))
                else:
                    c0 = (C_V if k == 1 else C_G) + h * 512
                    out.append((Wl["w_in"][rows(kb), c0:c0 + 512], kb * 512, 512, nmx(kb), None))
        elif 14 <= u < 22:
            g, k = divmod(u - 14, 2)
            c0 = (C_X if k == 0 else C_Z) + g * 512
            for kb in range(8):
                out.append((Wl["w_in"][rows(kb), c0:c0 + 512], kb * 512, 512, nmx(kb), None))
        elif 22 <= u < 34:
            uu, k = divmod(u - 22, 3)
            if k == 0:
                for kb in range(16):
                    out.append((Wl["ret_out"][rows(kb), uu * 256:uu * 256 + 256], kb * 256, 256, None, None))
            elif k == 1:
                for kb in range(16):
                    out.append((Wl["ssd_out"][rows(kb), uu * 256:uu * 256 + 256], kb * 256, 256, nsx(kb), None))
            else:
                for kb in range(8):
                    out.append((Wl["w_in"][rows(kb), C_GR + uu * 256:C_GR + uu * 256 + 256], kb * 512, 256, nmx(kb), None))
                    out.append((Wl["w_in"][rows(kb), C_GS + uu * 256:C_GS + uu * 256 + 256], kb * 512 + 256, 256, nmx(kb), None))
        elif u in (34, 35):
            uu = u - 34
            for kb in range(8):
                out.append((Wl["w_o"][rows(kb), uu * 512:uu * 512 + 512], kb * 512, 512, None, None))
        elif 36 <= u < 47:
            uu = u - 36
            for kb in range(8):
                out.append((Wl["w_gu"][rows(kb), uu * 256:uu * 256 + 256], kb * 512, 256, nfx(kb), None))
                out.append((Wl["w_gu"][rows(kb), DFF + uu * 256:DFF + uu * 256 + 256], kb * 512 + 256, 256, nfx(kb), None))
        else:
            f = u - 47
            for kb in range(NFF):
                out.append((Wl["w_down"][rows(kb), f * 128:f * 128 + 128], kb * 128, 128, None, None))
        return out

    stages = [(stage, "stage0"), (FS[:, 0:4096], "stage1"), (Rf[:].rearrange("p a b c -> p (a b c)"), "stage2")]

    def convert_layer(l, units):
        n = len(units)
        for i in range(n + 2):
            if i < n:
                st, key = stages[i % 3]
                for q, (src, dc, w, sc, mul) in enumerate(unit_pieces(l, units[i])):
                    p.dma(st[:, dc:dc + w], src, key, eng=("sp" if q % 2 == 0 else "act"))
            j = i - 2
            if 0 <= j < n:
                st, key = stages[j % 3]
                wb = wst[j % NW]
                pcs = unit_pieces(l, units[j])
                for src, dc, w, sc, mul in pcs:
                    convert_piece(wb[:, dc:dc + w], st[:, dc:dc + w], sc, mul)
                hi = max(dc + w for _, dc, w, _, _ in pcs)
                p.dma(W[l]["wu"][units[j], :, 0:hi], wb[:, 0:hi], "wst%d" % (j % NW))
        for kb in range(8):
            p.dma(stage[:, kb * 32:(kb + 1) * 32], W[l]["w_in"][kb * 128:(kb + 1) * 128, C_DT:C_DT + 32], "stage0")
        for kb in range(8):
            ts(p, "dve", wdt[l][:, kb, :], stage[:, kb * 32:(kb + 1) * 32], nm[l][:, kb:kb + 1], ALU.mult)

    P1_UNITS = [U_BCB] + [u for h in range(4) for u in (U_A(h), U_V(h))] + [U_X(g) for g in range(4)]
    P2_UNITS = ([U_BCB, U_BCC] + [u for h in range(4) for u in (U_A(h), U_V(h), U_G(h))]
                + [u for g in range(4) for u in (U_X(g), U_Z(g))]
                + [u for uu in range(4) for u in (U_RO(uu), U_SO(uu), U_GATE(uu))]
                + [U_WO(0), U_WO(1)] + [U_GU(i) for i in range(11)] + [U_WD(f) for f in range(8)])
    for l in layers:
        p.dma(stage[:, 0:256], W[l]["wdts"][:, :], "stage0")
        for kb in range(8):
            ts(p, "dve", wdt[l][:, kb, :], stage[:, kb * 32:(kb + 1) * 32], nm[l][:, kb:kb + 1], ALU.mult)

    class Stream:
        def __init__(self, l, seq):
            self.l, self.seq, self.i, self.slots = l, seq, 0, []

        def next(self, expect):
            while len(self.slots) < min(len(self.seq), self.i + NW - 1):
                u = self.seq[len(self.slots)]
                slot = wslot[0] % NW
                wslot[0] += 1
                p.dma(wst[slot][:], W[self.l]["wu"][u, :, :], "wst%d" % slot)
                self.slots.append(slot)
            assert self.seq[self.i] == expect, (self.seq[self.i], expect)
            w = wst[self.slots[self.i]]
            self.i += 1
            return w

    wslot = [0]

    def rmsnorm_to_hnT(l_unused=None):
        act(p, hnT[:], hT[:], AF.Square)
        for kb in range(NKB):
            mm(p, pM[:, 0:T], onesb[:], hnT[:, kb, :], kb == 0, kb == NKB - 1)
        act(p, lnv[:], pM[:, 0:T], AF.Ln, scale=1.0 / D, bias=EPS)
        act(p, rbc[:], lnv[:], AF.Exp, scale=-0.5)
        tt(p, "dve", hnT[:], hT[:], bc(rbc[:], 1, [128, NKB, T]), ALU.mult)

    def halo_norm(halo_src):
        p.dma(haloT[:], halo_src.rearrange("b p t -> p b t"), "haloT")
        act(p, hnTh[:], haloT[:], AF.Square)
        for kb in range(NKB):
            mm(p, pM[:, 256:260], onesb[:], hnTh[:, kb, :], kb == 0, kb == NKB - 1)
        act(p, small[:, 0:4], pM[:, 256:260], AF.Ln, scale=1.0 / D, bias=EPS)
        act(p, small[:, 4:8], small[:, 0:4], AF.Exp, scale=-0.5)
        tt(p, "dve", hnTh[:], haloT[:], bc(small[:, 4:8], 1, [128, NKB, 4]), ALU.mult)

    def proj_fm(w, c0, pa):
        for kb in range(NKB):
            mm(p, pa[:, 0:T], w[:, kb * 512 + c0:kb * 512 + c0 + 128], hnT[:, kb, :], kb == 0, kb == NKB - 1)

    def proj_tm(w, c, pa, width=512, c0=0):
        for kb in range(NKB):
            mm(p, pa[:, 0:width], hnT[:, kb, c * 128:(c + 1) * 128], w[:, kb * 512 + c0:kb * 512 + c0 + width],
               kb == 0, kb == NKB - 1)

    def conv_block(l, t, w, c0, blk, pa, out_bf):
        ra, rb, xraw = nxt()
        if t == 0:
            for kb in range(NKB):
                mm(p, pM[:, 260:264], w[:, kb * 512 + c0:kb * 512 + c0 + 128], hnTh[:, kb, :], kb == 0, kb == NKB - 1)
            cp(p, "act", xraw[:, 0:3], pM[:, 260:263])
        else:
            cp(p, "pool", xraw[:, 0:3], hist[:, blk, 0:3])
        cp(p, "act", xraw[:, 3:3 + T], pa[:, 0:T])
        cp(p, "pool", hist[:, blk, 0:3], xraw[:, T:T + 3])
        act(p, ra, pa[:, 0:T], AF.Identity, scale=cw[l][:, blk, 3:4], bias=cw[l][:, blk, 4:5])
        stt(p, rb, xraw[:, 0:T], cw[l][:, blk, 0:1], ra, ALU.mult, ALU.add)
        stt(p, ra, xraw[:, 1:1 + T], cw[l][:, blk, 1:2], rb, ALU.mult, ALU.add)
        stt(p, rb, xraw[:, 2:2 + T], cw[l][:, blk, 2:3], ra, ALU.mult, ALU.add)
        act(p, out_bf, rb, AF.Silu)

    def dt_prep(l, c, full):
        d = dtb[:, c, :]
        KD = int(os.environ.get('KDT', '63'))
        if KD & 1:
            for kb in range(NKB):
                mm(p, pM[:, 0:32], hnT[:, kb, c * 128:(c + 1) * 128], wdt[l][:, kb, :], kb == 0, kb == NKB - 1)
            tt(p, "dve", d[:, 0:32], pM[:, 0:32], hp[l][:, 0:32], ALU.add)
        if KD & 2:
            act(p, d[:, 32:64], d[:, 0:32], AF.Exp)
            act(p, d[:, 64:96], d[:, 32:64], AF.Ln, scale=1.0, bias=1.0)
            tt(p, "dve", d[:, 96:128], d[:, 64:96], atile[l][:], ALU.mult)
        if KD & 4:
            cp(p, "dve", lahl[:, c, 0, :], d[:, 96:128])
            tt(p, "dve", lahl[:, c, 1, :], d[:, 96:128], lahl[:, c, 0, :], ALU.subtract)
            for k, lt in enumerate((triTb, triUb, onesb)):
                mm(p, pM[:, 32 + 32 * k:64 + 32 * k], lt[:], lahl[:, c, 0, :], True, False)
                mm(p, pM[:, 32 + 32 * k:64 + 32 * k], lt[:], lahl[:, c, 1, :], False, True)
        if KD & 8:
            act(p, d[:, 128:224], pM[:, 32:128], AF.Exp)
        if KD & 32:
            tt(p, "dve", d[:, 224:256], d[:, 64:96], d[:, 160:192], ALU.mult)
        if (KD & 16) and not full:
            tt(p, "dve", LAt[:], LAt[:], pM[:, 96:128], ALU.add)

    def tile_body(l, t, full, src, dst, ws, last_layer):
        tsl = slice(t * T, (t + 1) * T)
        p.dma(hT[:], src[:, :, tsl].rearrange("b p t -> p b t"), "hT")
        p.dma(cossin[:], cs_d[:, :, tsl].rearrange("a p t -> p a t"), "cossin")
        rmsnorm_to_hnT()
        if KST < 5:
            raise _Stop()
        cosT, sinT = cossin[:, 0, :], cossin[:, 1, :]
        for c in range(NCH):
            dt_prep(l, c, full)
        if KST < 6:
            raise _Stop()
        w = ws.next(U_BCB)
        for g in range(4):
            proj_fm(w, g * 128, pA[g % 2])
            conv_block(l, t, w, g * 128, 16 + g, pA[g % 2], BT[:, g, :])
        if full:
            w = ws.next(U_BCC)
            for g in range(4):
                proj_fm(w, g * 128, pA[g % 2])
                conv_block(l, t, w, g * 128, 20 + g, pA[g % 2], CT[:, g, :])
        if KST < 7:
            raise _Stop()
        for h in range(NH):
            w = ws.next(U_A(h))
            for qk in ((0, 1) if full else (1,)):
                dstT = qT if qk == 0 else kT
                ra, rb, _ = nxt()
                proj_fm(w, qk * 256, pA[0])
                proj_fm(w, qk * 256 + 128, pA[1])
                tt(p, "dve", ra, pA[0][:, 0:T], cosT, ALU.mult)
                tt(p, "dve", rb, pA[1][:, 0:T], sinT, ALU.mult)
                tt(p, "pool", dstT[:, 0, :], ra, rb, ALU.subtract)
                tt(p, "dve", ra, pA[0][:, 0:T], sinT, ALU.mult)
                tt(p, "dve", rb, pA[1][:, 0:T], cosT, ALU.mult)
                tt(p, "pool", dstT[:, 1, :], ra, rb, ALU.add)
            w = ws.next(U_V(h))
            for c in range(NCH):
                proj_tm(w, c, pA[c % 2])
                cp(p, "act", vtok[:, c, :], pA[c % 2][:, :])
            if full:
                w = ws.next(U_G(h))
                for fb in range(4):
                    proj_fm(w, fb * 128, pA[fb % 2])
                    act(p, gT[:, fb, :], pA[fb % 2][:, 0:T], AF.Silu)
            for c in range(NCH):
                cs_ = slice(c * 128, (c + 1) * 128)
                if full:
                    for b in range(2):
                        mm(p, pM[:, 128:256], kT[:, b, cs_], qT[:, b, cs_], b == 0, b == 1)
                    tt(p, "dve", sT, pM[:, 128:256], intraT[:, h, :], ALU.mult)
                    tt(p, "pool", qTs, qT[:, :, cs_], bc(qdec[:, h, :], 1, [128, 2, 128]), ALU.mult)
                    mm(p, pY[:, :], sT, vtok[:, c, :], True, False)
                    mm(p, pY[:, :], qTs[:, 0, :], Rbf[:, h, 0, :], False, False)
                    mm(p, pY[:, :], qTs[:, 1, :], Rbf[:, h, 1, :], False, True)
                    act(p, ysn, pY[:, :], AF.Square, accum=small[:, 8:9])
                    act(p, small[:, 9:10], small[:, 8:9], AF.Ln, scale=1.0 / 512, bias=EPS)
                    act(p, small[:, 10:11], small[:, 9:10], AF.Exp, scale=-0.5)
                    ts(p, "dve", yn, pY[:, :], small[:, 10:11], ALU.mult)
                    for fb in range(4):
                        tr(p, pT[0][:, fb * 128:(fb + 1) * 128], yn[:, fb * 128:(fb + 1) * 128], identb[:], inc=(fb == 3))
                    tt(p, "dve", yrT[:, h * 4:(h + 1) * 4, cs_], pT[0][:, 0:512].rearrange("p (a b) -> p a b", a=4),
                       gT[:, :, cs_], ALU.mult)
                for b in range(2):
                    tr(p, pT[1][:, b * 128:(b + 1) * 128], kT[:, b, cs_], identb[:], inc=(b == 1))
                ts(p, "dve", ktok, pT[1][:, 0:256], kdec[:, h:h + 1], ALU.mult)
                for b in range(2):
                    mm(p, pS[b][:, :], ktok[:, b * 128:(b + 1) * 128], vtok[:, c, :], True, True)
                for b in range(2):
                    stt(p, Rf[:, h, b, :], Rf[:, h, b, :], GAMMA[h] ** 128, pS[b][:, :], ALU.mult, ALU.add)
                    if full:
                        cp(p, "act", Rbf[:, h, b, :], Rf[:, h, b, :])
        if KST < 8:
            raise _Stop()
        if full:
            out_proj(l, ws, first=True)
        if KST < 9:
            raise _Stop()
        for g in range(4):
            w = ws.next(U_X(g))
            for fb in range(4):
                proj_fm(w, fb * 128, pA[fb % 2])
                conv_block(l, t, w, fb * 128, g * 4 + fb, pA[fb % 2], xcT[:, fb, :])
            if full:
                w = ws.next(U_Z(g))
                for c in range(NCH):
                    proj_tm(w, c, pA[c % 2])
                    act(p, sz[:, c, :], pA[c % 2][:, :], AF.Silu)
            r3 = lambda ap: ap.rearrange("p (r q) -> p r q", r=8)

            def ssd_prep(c):
                cs_ = slice(c * 128, (c + 1) * 128)
                d = dtb[:, c, :]
                for fb in range(4):
                    tr(p, pT[0][:, fb * 128:(fb + 1) * 128], xcT[:, fb, cs_], identb[:], inc=(fb == 3))
                xps = pT[0][:, 0:512].rearrange("p (r q) -> p r q", r=8)
                tt(p, "dve", r3(xdts), xps, bc(d[:, 224 + 8 * g:232 + 8 * g], 2, [128, 8, 64]), ALU.mult)
                tr(p, pT[1][:, 256:384], BT[:, g, cs_], identb[:])
                cp(p, "act", Btok, pT[1][:, 256:384])
                if full:
                    cp(p, "act", xtok, pT[0][:, 0:512])
                    tt(p, "dve", r3(xdt), xps, bc(d[:, 64 + 8 * g:72 + 8 * g], 2, [128, 8, 64]), ALU.mult)
                    mm(p, pM[:, 256:384], BT[:, g, cs_], CT[:, g, cs_], True, True)
                    tt(p, "dve", cbm, pM[:, 256:384], triT, ALU.mult)
                    for k in range(2):
                        tt(p, "pool", rlab[:, k, :, :], bc(triTb[:], 1, [128, 8, 128]),
                           bc(lahl[:, c, k, 8 * g:8 * g + 8], 2, [128, 8, 128]), ALU.mult)
                    for hf in range(2):
                        for k in range(2):
                            mm(p, pA[hf][:, :], triUb[:], rlab[:, k, hf * 4:(hf + 1) * 4, :].rearrange("p r i -> p (r i)"), k == 0, k == 1)
                        act(p, esb[:, hf * 4:(hf + 1) * 4, :].rearrange("p r i -> p (r i)"), pA[hf][:, :], AF.Exp)
                    tt(p, "dve", MT, esb, bc(cbm, 1, [128, 8, 128]), ALU.mult)
                    tt(p, "pool", r3(xD[:]), r3(xtok), bc(hp[l][:, 64 + 8 * g:72 + 8 * g], 2, [128, 8, 64]), ALU.mult)

            def ssd_main(c):
                cs_ = slice(c * 128, (c + 1) * 128)
                d = dtb[:, c, :]
                if full:
                    mm(p, pY[:, :], identb[:], xD[:], True, False)
                    for r in range(8):
                        mm(p, pY[:, r * 64:(r + 1) * 64], MT[:, r, :], xdt[:, r * 64:(r + 1) * 64], False, r == 7)
                    mm(p, pS[1][:, :], CT[:, g, cs_], Sbf[:, g, :], True, True)
                mm(p, pS[0][:, :], Btok, xdts, True, True)
                tt(p, "dve", r3(Sf[:, g, :]), r3(Sf[:, g, :]), bc(d[:, 192 + 8 * g:200 + 8 * g], 2, [128, 8, 64]), ALU.mult)
                tt(p, "dve", Sf[:, g, :], Sf[:, g, :], pS[0][:, :], ALU.add)
                if full:
                    cp(p, "act", Sbf[:, g, :], Sf[:, g, :])

            def ssd_tail(c):
                cs_ = slice(c * 128, (c + 1) * 128)
                d = dtb[:, c, :]
                tt(p, "dve", r3(ubuf), r3(pS[1][:, :]), bc(d[:, 128 + 8 * g:136 + 8 * g], 2, [128, 8, 64]), ALU.mult)
                tt(p, "dve", ubuf, ubuf, pY[:, :], ALU.add)
                tt(p, "dve", ubuf, ubuf, sz[:, c, :], ALU.mult)
                act(p, ysn, ubuf, AF.Square, accum=small[:, 12:13])
                act(p, small[:, 13:14], small[:, 12:13], AF.Ln, scale=1.0 / 512, bias=EPS)
                act(p, small[:, 14:15], small[:, 13:14], AF.Exp, scale=-0.5)
                ts(p, "dve", ysn, ubuf, small[:, 14:15], ALU.mult)
                for fb in range(4):
                    tr(p, pT[1][:, 512 + fb * 128:512 + (fb + 1) * 128], ysn[:, fb * 128:(fb + 1) * 128], identb[:], inc=(fb == 3))
                cp(p, "act", ysT[:, g * 4:(g + 1) * 4, cs_], pT[1][:, 512:1024].rearrange("p (a b) -> p a b", a=4))

            ssd_prep(0)
            for c in range(NCH):
                ssd_main(c)
                if c + 1 < NCH:
                    ssd_prep(c + 1)
                if full:
                    ssd_tail(c)
        if not full:
            return
        if KST < 10:
            raise _Stop()
        out_proj(l, ws, first=False)
        if KST < 11:
            raise _Stop()
        for uu in range(2):
            w = ws.next(U_WO(uu))
            for f in range(4):
                fb = uu * 4 + f
                pa = pA[f % 2]
                for kb in range(NKB):
                    mm(p, pa[:, 0:T], w[:, kb * 512 + f * 128:kb * 512 + (f + 1) * 128], mTb[:, kb, :], kb == 0, kb == NKB - 1)
                tt(p, "dve", hT[:, fb, :], hT[:, fb, :], pa[:, 0:T], ALU.add)
        if KST < 12:
            raise _Stop()
        rmsnorm_to_hnT()
        for uu in range(11):
            w = ws.next(U_GU(uu))
            for j in range(2):
                proj_fm(w, j * 128, pA[0])
                proj_fm(w, 256 + j * 128, pA[1])
                ra, _, _ = nxt()
                act(p, ra, pA[0][:, 0:T], AF.Silu)
                tt(p, "dve", actT[:, 2 * uu + j, :], ra, pA[1][:, 0:T], ALU.mult)
        for fb in range(8):
            w = ws.next(U_WD(fb))
            pa = pA[fb % 2]
            for kb in range(NFF):
                mm(p, pa[:, 0:T], w[:, kb * 128:(kb + 1) * 128], actT[:, kb, :], kb == 0, kb == NFF - 1)
            tt(p, "dve", hT[:, fb, :], hT[:, fb, :], pa[:, 0:T], ALU.add)
        if KST < 13:
            raise _Stop()
        if last_layer:
            act(p, hnT[:], hT[:], AF.Square)
            for kb in range(NKB):
                mm(p, pM[:, 0:T], onesb[:], hnT[:, kb, :], kb == 0, kb == NKB - 1)
            act(p, lnv[:], pM[:, 0:T], AF.Ln, scale=1.0 / D, bias=EPS)
            act(p, rbc[:], lnv[:], AF.Exp, scale=-0.5)
            for kb in range(NKB):
                stt(p, hT[:, kb, :], hT[:, kb, :], fnw[:, kb:kb + 1], rbc[:], ALU.mult, ALU.mult)
        p.dma(dst[:, :, tsl].rearrange("b p t -> p b t"), hT[:], "hT")

    def out_proj(l, ws_, first):
        for uu in range(4):
            w = ws_.next(U_RO(uu) if first else U_SO(uu))
            wg = ws_.next(U_GATE(uu))
            for j in range(2):
                fb = uu * 2 + j
                for kb in range(16):
                    mm(p, pY[:, 0:T], w[:, kb * 256 + j * 128:kb * 256 + (j + 1) * 128], yrT[:, kb, :], kb == 0, kb == 15)
                proj_fm(wg, (0 if first else 256) + j * 128, pA[j])
                ra, rb, _ = nxt()
                act(p, ra, pA[j][:, 0:T], AF.Sigmoid)
                if first:
                    tt(p, "dve", mTb[:, fb, :], ra, pY[:, 0:T], ALU.mult)
                else:
                    tt(p, "dve", rb, ra, pY[:, 0:T], ALU.mult)
                    tt(p, "pool", mTb[:, fb, :], mTb[:, fb, :], rb, ALU.add)

    def p2_seq():
        s = [U_BCB, U_BCC]
        for h in range(4):
            s += [U_A(h), U_V(h), U_G(h)]
        for uu in range(4):
            s += [U_RO(uu), U_GATE(uu)]
        for g in range(4):
            s += [U_X(g), U_Z(g)]
        for uu in range(4):
            s += [U_SO(uu), U_GATE(uu)]
        s += [U_WO(0), U_WO(1)] + [U_GU(i) for i in range(11)] + [U_WD(f) for f in range(8)]
        return s

    def p1_seq():
        s = [U_BCB]
        for h in range(4):
            s += [U_A(h), U_V(h)]
        s += [U_X(g) for g in range(4)]
        return s

    try:
        for kind, l in phases:
            src = hin
            if KST < 3:
                break
            halo_norm(halo_in)
            if KST < 4:
                break
            if kind == "p1":
                p.op("pool", lambda e: e.memset(Rf[:].rearrange("p a b c -> p (a b c)"), 0.0), [], [Rf[:]])
                p.op("pool", lambda e: e.memset(Sf[:].rearrange("p a b -> p (a b)"), 0.0), [], [Sf[:]])
                p.op("pool", lambda e: e.memset(LAt[:], 0.0), [], [LAt[:]])
                ws = Stream(l, p1_seq() * NTILE)
                for t in range(NTILE):
                    tile_body(l, t, False, src, None, ws, False)
                for h in range(NH):
                    for b in range(2):
                        p.dma(stloc[h * 2 + b, :, :], Rf[:, h, b, :], "Rf")
                for g in range(4):
                    p.dma(stloc[8 + g, :, :], Sf[:, g, :], "Sf")
                p.op("pool", lambda e: e.memset(rbc[:, 0:512], 0.0), [], [rbc[:, 0:512]])
                cp(p, "dve", rbc[:, 0:32], LAt[:])
                p.dma(stloc[12, :, :], rbc[:, 0:512], "rbc")
            else:
                for blk in range(8):
                    h, b = divmod(blk, 2)
                    p.dma(FS[:, 0:2048].rearrange("p (j f) -> p j f", j=4), stall[:, blk, :, :].rearrange("j p f -> p j f"), "FS")
                    ts(p, "dve", Rf[:, h, b, :], FS[:, 0:512], cf[:, h * 4:h * 4 + 1], ALU.mult)
                    for j in range(1, 4):
                        stt(p, Rf[:, h, b, :], FS[:, j * 512:(j + 1) * 512], cf[:, h * 4 + j:h * 4 + j + 1], Rf[:, h, b, :], ALU.mult, ALU.add)
                    cp(p, "act", Rbf[:, h, b, :], Rf[:, h, b, :])
                LAa = FS[:, 2048:2048 + 128].rearrange("p (j f) -> p j f", j=4)
                p.dma(LAa, stall[:, 12, :, 0:32].rearrange("j p f -> p j f"), "FSla")
                Ej = FS[:, 2176:2176 + 128].rearrange("p (j f) -> p j f", j=4)
                for j in range(4):
                    ts(p, "dve", Ej[:, j, :], LAa[:, 0, :], cf[:, 20 + j * 4:21 + j * 4], ALU.mult)
                    for m in range(1, 4):
                        stt(p, Ej[:, j, :], LAa[:, m, :], cf[:, 20 + j * 4 + m:21 + j * 4 + m], Ej[:, j, :], ALU.mult, ALU.add)
                    act(p, Ej[:, j, :], Ej[:, j, :], AF.Exp)
                    ts(p, "dve", Ej[:, j, :], Ej[:, j, :], cf[:, 16 + j:17 + j], ALU.mult)
                for g in range(4):
                    p.dma(FS[:, 0:2048].rearrange("p (j f) -> p j f", j=4), stall[:, 8 + g, :, :].rearrange("j p f -> p j f"), "FS")
                    r3 = lambda ap: ap.rearrange("p (r q) -> p r q", r=8)
                    tt(p, "dve", r3(Sf[:, g, :]), r3(FS[:, 0:512]), bc(Ej[:, 0, 8 * g:8 * g + 8], 2, [128, 8, 64]), ALU.mult)
                    for j in range(1, 4):
                        tt(p, "dve", r3(FS[:, j * 512:(j + 1) * 512]), r3(FS[:, j * 512:(j + 1) * 512]),
                           bc(Ej[:, j, 8 * g:8 * g + 8], 2, [128, 8, 64]), ALU.mult)
                        tt(p, "dve", Sf[:, g, :], Sf[:, g, :], FS[:, j * 512:(j + 1) * 512], ALU.add)
                    cp(p, "act", Sbf[:, g, :], Sf[:, g, :])
                ws = Stream(l, p2_seq() * NTILE)
                for t in range(NTILE):
                    tile_body(l, t, True, src, hout, ws, (l == DEPTH - 1) and not globals().get('_NOFINAL', False))
    except _Stop:
        pass
    if p.log is not None:
        with open(os.environ['KDUMP'], 'w') as f:
            for r in p.log:
                f.write(repr(r) + '\n')
    p.finish()
    es.close()
    return nc


NSLOT = 14


def build_convert():
    nc = bass.Bass("TRN2", target_bir_lowering=False)
    es = ExitStack()
    p = Prog(nc, es)
    src = nc.dram_tensor("usrc", [NSLOT, 128, 4096], F32, kind="ExternalInput").ap()
    scd = nc.dram_tensor("usc", [128, NSLOT * 32], F32, kind="ExternalInput").ap()
    dst = nc.dram_tensor("wpart", [NSLOT, 128, 4096], BF16, kind="ExternalOutput").ap()
    sc = es.enter_context(nc.sbuf_tensor("sc_sb", [128, NSLOT, 32], F32))
    stg = [es.enter_context(nc.sbuf_tensor("stg%d" % i, [128, 32, 128], F32)) for i in range(3)]
    wb = [es.enter_context(nc.sbuf_tensor("wb%d" % i, [128, 32, 128], BF16)) for i in range(3)]
    p.dma(sc[:].rearrange("p a b -> p (a b)"), scd[:, :], "sc")
    for i in range(NSLOT + 2):
        if i < NSLOT:
            p.dma(stg[i % 3][:].rearrange("p a b -> p (a b)"), src[i, :, :], "stg%d" % (i % 3), eng=("sp" if i % 2 == 0 else "act"))
        j = i - 2
        if 0 <= j < NSLOT:
            st, o = stg[j % 3], wb[j % 3]
            tt(p, "dve", o[:, 0:16, :], st[:, 0:16, :], bc(sc[:, j, 0:16], 2, [128, 16, 128]), ALU.mult)
            tt(p, "pool", o[:, 16:32, :], st[:, 16:32, :], bc(sc[:, j, 16:32], 2, [128, 16, 128]), ALU.mult)
            if j == 0:
                kv = o[:].rearrange("p (a b) c -> p a b c", b=4)[:, :, 2:4, :]
                ts(p, "dve", kv, kv, 1.0 / 16, ALU.mult)
            p.dma(dst[j, :, :], o[:].rearrange("p a b -> p (a b)"), "wb%d" % (j % 3))
    p.finish()
    es.close()
    return nc


def _unit_src(l, u, P):
    data = np.zeros((128, 4096), np.float32)
    sc = np.ones((128, 32), np.float32)
    nmx = P["norm_mix_w"][l].reshape(8, 128).T
    nfx = P["norm_ffn_w"][l].reshape(8, 128).T
    nsx = P["ssd_norm_w"][l].reshape(16, 128).T

    def put(mat, kb, c0, w, dc, scv):
        data[:, dc:dc + w] = mat[kb * 128:(kb + 1) * 128, c0:c0 + w]
        if scv is not None:
            sc[:, dc // 128:(dc + w) // 128] = scv[:, kb:kb + 1]

    w_in = P["w_in"][l]
    if u in (U_BCB, U_BCC):
        c0 = C_B if u == U_BCB else C_C
        for kb in range(8):
            put(w_in, kb, c0, 512, kb * 512, nmx)
    elif 2 <= u < 14:
        h, k = divmod(u - 2, 3)
        for kb in range(8):
            if k == 0:
                put(w_in, kb, C_Q + h * 256, 256, kb * 512, nmx)
                put(w_in, kb, C_K + h * 256, 256, kb * 512 + 256, nmx)
            else:
                put(w_in, kb, (C_V if k == 1 else C_G) + h * 512, 512, kb * 512, nmx)
    elif 14 <= u < 22:
        g, k = divmod(u - 14, 2)
        for kb in range(8):
            put(w_in, kb, (C_X if k == 0 else C_Z) + g * 512, 512, kb * 512, nmx)
    elif 22 <= u < 34:
        uu, k = divmod(u - 22, 3)
        if k == 0:
            for kb in range(16):
                put(P["ret_out"][l], kb, uu * 256, 256, kb * 256, None)
        elif k == 1:
            for kb in range(16):
                put(P["ssd_out"][l], kb, uu * 256, 256, kb * 256, nsx)
        else:
            for kb in range(8):
                put(w_in, kb, C_GR + uu * 256, 256, kb * 512, nmx)
                put(w_in, kb, C_GS + uu * 256, 256, kb * 512 + 256, nmx)
    elif u in (34, 35):
        for kb in range(8):
            put(P["w_o"][l], kb, (u - 34) * 512, 512, kb * 512, None)
    elif 36 <= u < 47:
        uu = u - 36
        for kb in range(8):
            put(P["w_gate_up"][l], kb, uu * 256, 256, kb * 512, nfx)
            put(P["w_gate_up"][l], kb, DFF + uu * 256, 256, kb * 512 + 256, nfx)
    else:
        f = u - 47
        for kb in range(NFF):
            put(P["w_down"][l], kb, f * 128, 128, kb * 128, None)
    return data, sc


def _convert_weights(P):
    a_units = [(l, U_A(h)) for l in range(DEPTH) for h in range(NH)]
    rest = [(l, u) for l in range(DEPTH) for u in range(NUNIT) if (l, u) not in a_units]
    assign = []
    for c in range(8):
        mine = [a_units[c]] + rest[c * (NSLOT - 1):(c + 1) * (NSLOT - 1)]
        assign.append(mine)
    maps = []
    for c in range(8):
        usrc = np.zeros((NSLOT, 128, 4096), np.float32)
        usc = np.ones((128, NSLOT, 32), np.float32)
        for i, (l, u) in enumerate(assign[c]):
            usrc[i], usc[:, i, :] = _unit_src(l, u, P)
        maps.append({"usrc": usrc, "usc": np.ascontiguousarray(usc.reshape(128, NSLOT * 32))})
    res = run_bass_kernel_spmd(build_convert(), maps, core_ids=list(range(8)))
    first = np.asarray(res.results[0]["wpart"])
    wu = [np.zeros((NUNIT, 128, 4096), first.dtype) for _ in range(DEPTH)]
    for c in range(8):
        part = np.asarray(res.results[c]["wpart"])
        for i, (l, u) in enumerate(assign[c]):
            wu[l][u] = part[i]
    return wu


def _consts():
    j = np.arange(128)
    cst = np.zeros((128, 1540), np.float32)
    cst[:, 0:128] = np.eye(128)
    cst[:, 128:256] = (j[:, None] <= j[None, :])
    cst[:, 256:384] = (j[:, None] > j[None, :])
    cst[:, 384:512] = 1.0
    for h in range(NH):
        g = GAMMA[h]
        diff = j[None, :] - j[:, None]
        cst[:, 512 + h * 128:512 + (h + 1) * 128] = np.where(diff >= 0, g ** np.maximum(diff, 0).astype(np.float64), 0.0)
        cst[:, 1024 + h * 128:1024 + (h + 1) * 128] = (g ** (j + 1.0))[None, :]
        cst[:, 1536 + h] = g ** (127.0 - j)
    return cst


def _rope(pos):
    inv = np.float32(10000.0) ** (-(np.arange(128, dtype=np.float32) / np.float32(128)))
    ang = pos.astype(np.float32)[:, None] * inv[None, :].astype(np.float32)
    return np.stack([np.cos(ang).T, np.sin(ang).T]).astype(np.float32)


def _coef(s, NT):
    cf = np.zeros((128, 36), np.float32)
    for h in range(NH):
        for j in range(4):
            if j < s:
                cf[:, h * 4 + j] = GAMMA[h] ** (float(NT) * (s - 1 - j))
    for j in range(4):
        cf[:, 16 + j] = 1.0 if j < s else 0.0
        for m in range(4):
            cf[:, 20 + j * 4 + m] = 1.0 if (j < m < s) else 0.0
    return cf


def _fm(a):
    return np.ascontiguousarray(a.T.reshape(NKB, 128, a.shape[0]))


def _layer_inputs(l, P):
    nmv = np.concatenate([P["norm_mix_w"][l].reshape(8, 128).T, P["norm_ffn_w"][l].reshape(8, 128).T,
                          P["ssd_norm_w"][l].reshape(16, 128).T], axis=1)
    cwv = np.zeros((128, 24, 5), np.float32)
    cwv[:, :, 0:4] = P["conv_w"][l].reshape(4, 24, 128).transpose(2, 1, 0)
    cwv[:, :, 4] = P["conv_b"][l].reshape(24, 128).T
    hpv = np.concatenate([np.broadcast_to(P[k][l][None, :], (128, 32)) for k in ("dt_bias", "a_log", "d_skip")], axis=1)
    wdts = P["w_in"][l][:, C_DT:C_DT + 32].reshape(8, 128, 32).transpose(1, 0, 2).reshape(128, 256)
    return {
        "nm%d" % l: np.ascontiguousarray(nmv, np.float32), "cw%d" % l: np.ascontiguousarray(cwv.reshape(128, 120)),
        "hp%d" % l: np.ascontiguousarray(hpv, np.float32), "wdts%d" % l: np.ascontiguousarray(wdts, np.float32),
    }


_T_DEFAULT = 512


def kernel(x, norm_mix_w, w_in, ret_out, conv_w, conv_b, dt_bias, a_log, d_skip, ssd_norm_w,
           ssd_out, w_o, norm_ffn_w, w_gate_up, w_down, final_norm_w, _T=None):
    P = dict(norm_mix_w=norm_mix_w, w_in=w_in, ret_out=ret_out, conv_w=conv_w, conv_b=conv_b, dt_bias=dt_bias,
             a_log=a_log, d_skip=d_skip, ssd_norm_w=ssd_norm_w, ssd_out=ssd_out, w_o=w_o, norm_ffn_w=norm_ffn_w,
             w_gate_up=w_gate_up, w_down=w_down)
    P = {k: np.asarray(v, np.float32) for k, v in P.items()}
    x = np.asarray(x, np.float32)
    B, S, _ = x.shape
    NSEG = 4
    NT = S // NSEG
    T = _T or min(_T_DEFAULT, NT)
    ncore = B * NSEG
    assert ncore == 8
    cst = _consts()
    fnw = np.ascontiguousarray(np.asarray(final_norm_w, np.float32).reshape(8, 128).T)
    common = []
    for c in range(ncore):
        b, s = divmod(c, NSEG)
        common.append({"cst": cst, "cs": _rope(np.arange(s * NT, (s + 1) * NT)), "cf": _coef(s, NT), "fnw": fnw})

    def halo_of(hfm_prev):
        h = np.zeros((NKB, 128, 4), np.float32)
        if hfm_prev is not None:
            h[:, :, 0:3] = hfm_prev[:, :, -3:]
        return h

    hcur = [_fm(x[c // NSEG, (c % NSEG) * NT:((c % NSEG) + 1) * NT, :]) for c in range(ncore)]
    wu = _convert_weights(P)
    for l in range(DEPTH):
        li = _layer_inputs(l, P)
        li["wu%d" % l] = wu[l]
        halos = [halo_of(hcur[c - 1] if c % NSEG else None) for c in range(ncore)]
        nc1 = build(NT, T, [("p1", l)])
        maps = [dict(common[c], hin=hcur[c], halo=halos[c], **li) for c in range(ncore)]
        r1 = run_bass_kernel_spmd(nc1, maps, core_ids=list(range(ncore)))
        st = [np.asarray(r["stloc"]) for r in r1.results]
        nc2 = build(NT, T, [("p2", l)])
        maps = []
        for c in range(ncore):
            b = c // NSEG
            stall = np.stack([st[b * NSEG + j] for j in range(NSEG)])
            maps.append(dict(common[c], hin=hcur[c], halo=halos[c], stall=stall, **li))
        r2 = run_bass_kernel_spmd(nc2, maps, core_ids=list(range(ncore)))
        hcur = [np.asarray(r["hout"]) for r in r2.results]
    out = np.empty((B, S, D), np.float32)
    for c in range(ncore):
        b, s = divmod(c, NSEG)
        out[b, s * NT:(s + 1) * NT, :] = hcur[c].reshape(D, NT).T
    return out
```

```python
import os
import numpy as np
from contextlib import ExitStack
import concourse.bass as bass
import concourse.mybir as mybir
from concourse.bass_utils import run_bass_kernel_spmd

F32, BF16 = mybir.dt.float32, mybir.dt.bfloat16
AF = mybir.ActivationFunctionType
ALU = mybir.AluOpType

D = 1024
NKB = 8
DEPTH = 2
EPS = 1e-6
NH = 4
DFF = 2816
NFF = 22
DIN = 13344
C_Q, C_K, C_V, C_G, C_Z, C_X, C_B, C_C, C_DT, C_GR, C_GS = 0, 1024, 2048, 4096, 6144, 8192, 10240, 10752, 11264, 11296, 12320
NUNIT = 55
U_BCB, U_BCC = 0, 1
def U_A(h): return 2 + 3 * h
def U_V(h): return 3 + 3 * h
def U_G(h): return 4 + 3 * h
def U_X(g): return 14 + 2 * g
def U_Z(g): return 15 + 2 * g
def U_RO(u): return 22 + 3 * u
def U_SO(u): return 23 + 3 * u
def U_GATE(u): return 24 + 3 * u
def U_WO(u): return 34 + u
def U_GU(u): return 36 + u
def U_WD(f): return 47 + f
GAMMA = [1.0 - 2.0 ** (-5.0 - h) for h in range(NH)]
NST = 13

ENG = dict(pe="tensor", act="scalar", dve="vector", pool="gpsimd", sp="sync")


def _prod(xs):
    r = 1
    for v in xs:
        r *= int(v)
    return r


def _iv(ap):
    t = ap.tensor
    dims = [(int(s), int(c)) for s, c in ap.ap]
    off = int(ap.offset)
    if "DRAM" in str(ap.space).upper():
        hi = off + sum(s * (c - 1) for s, c in dims if s > 0) + 1
        return (t.name, 0, 1, off, hi)
    if "PSUM" in str(ap.space).upper():
        return (t.name, 0, 128, 0, 1 << 30)
    rows = _prod(list(t.shape)[1:])
    p0, f0 = off // rows, off % rows
    fhi = f0 + sum(s * (c - 1) for s, c in dims[1:] if s > 0) + 1
    return (t.name, p0, p0 + dims[0][1], f0, fhi)


class _Stop(Exception):
    pass


class Prog:
    def __init__(self, nc, es):
        self.nc, self.es = nc, es
        self.ops = {e: [] for e in ENG}
        self.cnt = {e: 0 for e in ENG}
        self.sems = {}
        self.semval = {}
        self.waited = {e: {} for e in ENG}
        self.acc = {}
        self.nsem = 0
        self.log = [] if os.environ.get('KDUMP') else None

    def sem(self, key):
        if key not in self.sems:
            self.nsem += 1
            self.sems[key] = self.es.enter_context(self.nc.semaphore("s%d" % self.nsem))
        return self.sems[key]

    def _need(self, eng, reads, writes):
        need = {}

        def add(tok):
            k, v = tok
            if k == "pe" and eng == "pe":
                return
            if v > need.get(k, 0):
                need[k] = v

        for ap in reads:
            n, p0, p1, f0, f1 = _iv(ap)
            for r in self.acc.get(n, ()):
                if r[0] == "w" and r[1] < p1 and p0 < r[2] and r[3] < f1 and f0 < r[4]:
                    add(r[5])
        for ap in writes:
            n, p0, p1, f0, f1 = _iv(ap)
            for r in self.acc.get(n, ()):
                if r[1] < p1 and p0 < r[2] and r[3] < f1 and f0 < r[4]:
                    add(r[5])
        out = []
        for k, v in need.items():
            if k.startswith("d:"):
                v = max(v, self.semval[k])
            if self.waited[eng].get(k, 0) < v:
                self.waited[eng][k] = v
                out.append((k, v))
        return out

    def _rec(self, kind, ap, tok):
        n, p0, p1, f0, f1 = _iv(ap)
        L = self.acc.setdefault(n, [])
        if kind == "w":
            L[:] = [r for r in L if not (p0 <= r[1] and r[2] <= p1 and f0 <= r[3] and r[4] <= f1)]
        else:
            L[:] = [r for r in L if not (r[0] == "r" and r[5][0] == tok[0] and p0 <= r[1] and r[2] <= p1
                                         and f0 <= r[3] and r[4] <= f1)]
        L.append((kind, p0, p1, f0, f1, tok))

    def op(self, eng, fn, reads=(), writes=(), inc=True):
        self.sem(eng)
        psr = [a for a in reads if "PSUM" in str(a.space).upper()]
        if psr:
            reads = [a for a in reads if "PSUM" not in str(a.space).upper()]
            writes = list(writes) + psr
        waits = self._need(eng, reads, writes)
        idx = self.cnt[eng] + 1
        if inc:
            self.cnt[eng] = idx
        tok = (eng, idx)
        for ap in reads:
            self._rec("r", ap, tok)
        for ap in writes:
            self._rec("w", ap, tok)
        self.ops[eng].append((waits, fn, (eng, 1) if inc else None))
        if self.log is not None:
            self.log.append((eng, idx, inc, waits, [_iv(a) for a in reads], [_iv(a) for a in writes]))

    def dma(self, out, in_, key, eng="sp"):
        k = "d:" + key
        self.sem(k)
        waits = self._need(eng, [in_], [out])
        v = self.semval.get(k, 0) + 16
        self.semval[k] = v
        tok = (k, v)
        self._rec("r", in_, tok)
        self._rec("w", out, tok)
        self.ops[eng].append((waits, lambda e: e.dma_start(out=out, in_=in_), (k, 16)))
        if self.log is not None:
            self.log.append((eng, tok, True, waits, [_iv(in_)], [_iv(out)]))

    def finish(self):
        for k, v in self.semval.items():
            if self.waited["sp"].get(k, 0) < v:
                self.ops["sp"].append(([(k, v)], None, None))
        block = self.es.enter_context(self.nc.Block())
        for eng, attr in ENG.items():
            ops = self.ops[eng]
            if not ops:
                continue

            def body(e, ops=ops):
                for waits, fn, inc in ops:
                    for k, v in waits:
                        e.wait_ge(self.sems[k], v)
                    if fn is None:
                        continue
                    ins = fn(e)
                    if inc is not None:
                        ins.then_inc(self.sems[inc[0]], inc[1])

            getattr(block, attr)(body)


def mm(p, out, lhsT, rhs, start, stop, inc=None):
    if inc is None:
        inc = stop
    p.op("pe", lambda e: e.matmul(out, lhsT=lhsT, rhs=rhs, start=start, stop=stop), [lhsT, rhs], [out], inc=inc)


def tr(p, out, in_, ident, inc=True):
    p.op("pe", lambda e: e.transpose(out, in_, ident), [in_, ident], [out], inc=inc)


def act(p, out, in_, func, bias=None, scale=None, accum=None):
    kw = {}
    rd = [in_]
    if bias is not None:
        kw["bias"] = bias
        if not isinstance(bias, (int, float)):
            rd.append(bias)
    if scale is not None:
        kw["scale"] = scale
        if not isinstance(scale, (int, float)):
            rd.append(scale)
    wr = [out]
    if accum is not None:
        kw["accum_out"] = accum
        wr.append(accum)
    p.op("act", lambda e: e.activation(out=out, in_=in_, func=func, **kw), rd, wr)


def tt(p, eng, out, a, b, op):
    p.op(eng, lambda e: e.tensor_tensor(out=out, in0=a, in1=b, op=op), [a, b], [out])


def ts(p, eng, out, a, s1, op0, s2=None, op1=None):
    rd = [a] + [s for s in (s1, s2) if s is not None and not isinstance(s, (int, float))]
    if op1 is None:
        p.op(eng, lambda e: e.tensor_scalar(out=out, in0=a, scalar1=s1, scalar2=None, op0=op0), rd, [out])
    else:
        p.op(eng, lambda e: e.tensor_scalar(out=out, in0=a, scalar1=s1, scalar2=s2, op0=op0, op1=op1), rd, [out])


def stt(p, out, in0, scalar, in1, op0, op1):
    rd = [in0, in1] + ([] if isinstance(scalar, (int, float)) else [scalar])
    p.op("dve", lambda e: e.scalar_tensor_tensor(out=out, in0=in0, scalar=scalar, in1=in1, op0=op0, op1=op1), rd, [out])


def cp(p, eng, out, in_):
    if eng == "act":
        p.op("act", lambda e: e.copy(out=out, in_=in_), [in_], [out])
    else:
        p.op(eng, lambda e: e.tensor_copy(out=out, in_=in_), [in_], [out])


def bc(ap, axis, shape):
    return ap.unsqueeze(axis).to_broadcast(list(shape))


def build(NT, T, phases, fused=False):
    NCH = T // 128
    NTILE = NT // T
    assert T % 128 == 0 and NT % T == 0
    nc = bass.Bass("TRN2", target_bir_lowering=False)
    es = ExitStack()
    p = Prog(nc, es)
    layers = sorted(set(l for _, l in phases))

    def din(name, shape, dt=F32):
        return nc.dram_tensor(name, list(shape), dt, kind="ExternalInput").ap()

    def dout(name, shape, dt=F32):
        return nc.dram_tensor(name, list(shape), dt, kind="ExternalOutput").ap()

    def dint(name, shape, dt=F32):
        return nc.dram_tensor(name, list(shape), dt).ap()

    hin = din("hin", [NKB, 128, NT])
    halo_in = din("halo", [NKB, 128, 4])
    cst_d = din("cst", [128, 1540])
    cs_d = din("cs", [2, 128, NT])
    cf_d = din("cf", [128, 36])
    fnw_d = din("fnw", [128, 8])
    W = {}
    for l in layers:
        W[l] = dict(nm=din("nm%d" % l, [128, 32]), cw=din("cw%d" % l, [128, 120]), hp=din("hp%d" % l, [128, 96]),
                    wu=din("wu%d" % l, [NUNIT, 128, 4096], BF16), wdts=din("wdts%d" % l, [128, 256]))
    if not fused:
        (kind, lay), = phases
        if kind == "p1":
            stloc = dout("stloc", [NST, 128, 512])
        else:
            stall = din("stall", [4, NST, 128, 512])
            hout = dout("hout", [NKB, 128, NT])

    sb = lambda name, shape, dt=F32: es.enter_context(nc.sbuf_tensor(name, list(shape), dt))
    ps = lambda name, shape, dt=F32: es.enter_context(nc.psum_tensor(name, list(shape), dt))

    cst = sb("cst_sb", [128, 1540])
    identb = sb("identb", [128, 128], BF16)
    onesb = sb("onesb", [128, 128], BF16)
    triTb = sb("triTb", [128, 128], BF16)
    triUb = sb("triUb", [128, 128], BF16)
    lahl = sb("lahl", [128, NCH, 2, 32], BF16)
    rlab = sb("rlab", [128, 2, 8, 128], BF16)
    xD = sb("xD", [128, 512], BF16)
    cossin = sb("cossin", [128, 2, T])
    cf = sb("cf_sb", [128, 36])
    fnw = sb("fnw_sb", [128, 8])
    nm = {l: sb("nm_sb%d" % l, [128, 32]) for l in layers}
    cw = {l: sb("cw_sb%d" % l, [128, 24, 5]) for l in layers}
    hp = {l: sb("hp_sb%d" % l, [128, 96]) for l in layers}
    atile = {l: sb("atile%d" % l, [128, 32]) for l in layers}
    wdt = {l: sb("wdt%d" % l, [128, NKB, 32], BF16) for l in layers}
    hT = sb("hT", [128, NKB, T])
    hnT = sb("hnT", [128, NKB, T], BF16)
    rbc = sb("rbc", [128, T])
    lnv = sb("lnv", [128, T])
    hnTh = sb("hnTh", [128, NKB, 4], BF16)
    haloT = sb("haloT", [128, NKB, 4])
    small = sb("small", [128, 64])
    NW = 4
    wst = [sb("wst%d" % i, [128, 4096], BF16) for i in range(NW)]
    FS = sb("FS", [128, 3 * T + 4 + NCH * 256 + 128 + 1024 + 1024 + 3 * T + 4])
    BS = sb("BS", [128, 8 * T + 4 * T + NCH * 512 + 1536 + 128 + 2048 + 512 + 64], BF16)
    U = sb("U", [128, 24 * T], BF16)
    hist = sb("hist", [128, 24, 4])
    Rf = sb("Rf", [128, NH, 2, 512])
    Rbf = sb("Rbf", [128, NH, 2, 512], BF16)
    Sf = sb("Sf", [128, 4, 512])
    Sbf = sb("Sbf", [128, 4, 512], BF16)
    LAt = sb("LAt", [128, 32])
    pA = [ps("pA0", [128, 512]), ps("pA1", [128, 512])]
    pY = ps("pY", [128, 512])
    pS = [ps("pS0", [128, 512]), ps("pS1", [128, 512])]
    pM = ps("pM", [128, 512])
    pT = [ps("pT0", [128, 1024], BF16), ps("pT1", [128, 1024], BF16)]

    identf = cst[:, 0:128]
    triT = cst[:, 128:256]
    triU = cst[:, 256:384]
    onesf = cst[:, 384:512]
    intraT = cst[:, 512:1024].rearrange("p (h i) -> p h i", h=4)
    qdec = cst[:, 1024:1536].rearrange("p (h i) -> p h i", h=4)
    kdec = cst[:, 1536:1540]
    _o2 = 3 * T + 4 + NCH * 256 + 128 + 1024 + 1024
    _tmp = [(FS[:, 0:T], FS[:, T:2 * T], FS[:, 2 * T:3 * T + 4]),
            (FS[:, _o2:_o2 + T], FS[:, _o2 + T:_o2 + 2 * T], FS[:, _o2 + 2 * T:_o2 + 3 * T + 4])]
    _tsel = [0]

    def nxt():
        _tsel[0] ^= 1
        return _tmp[_tsel[0]]
    ra, rb, xraw = _tmp[0]
    o = 3 * T + 4
    dtb = FS[:, o:o + NCH * 256].rearrange("p (c f) -> p c f", c=NCH)
    o += NCH * 256
    cbm = FS[:, o:o + 128]
    o += 128
    rla = FS[:, o:o + 1024].rearrange("p (r i) -> p r i", r=8)
    o += 1024
    ubuf = FS[:, o:o + 512]
    vbuf = FS[:, o + 512:o + 1024]
    BT = BS[:, 0:4 * T].rearrange("p (g t) -> p g t", g=4)
    CT = BS[:, 4 * T:8 * T].rearrange("p (g t) -> p g t", g=4)
    o = 8 * T
    qT = BS[:, o:o + 2 * T].rearrange("p (b t) -> p b t", b=2)
    kT = BS[:, o + 2 * T:o + 4 * T].rearrange("p (b t) -> p b t", b=2)
    r2 = o + 4 * T
    vtok = BS[:, r2:r2 + NCH * 512].rearrange("p (c f) -> p c f", c=NCH)
    r2 += NCH * 512
    sT = BS[:, r2:r2 + 128]
    qTs = BS[:, r2 + 128:r2 + 384].rearrange("p (b i) -> p b i", b=2)
    yn = BS[:, r2 + 384:r2 + 896]
    ktok = BS[:, r2 + 896:r2 + 1152]
    gT = U[:, 16 * T:20 * T].rearrange("p (b t) -> p b t", b=4)
    xcT = BS[:, o:o + 4 * T].rearrange("p (b t) -> p b t", b=4)
    s2 = o + 4 * T
    sz = BS[:, s2:s2 + NCH * 512].rearrange("p (c f) -> p c f", c=NCH)
    s2 += NCH * 512
    xtok = BS[:, s2:s2 + 512]
    xdt = BS[:, s2 + 512:s2 + 1024]
    xdts = BS[:, s2 + 1024:s2 + 1536]
    s2 += 1536
    Btok = BS[:, s2:s2 + 128]
    s2 += 128
    esb = BS[:, s2:s2 + 1024].rearrange("p (r i) -> p r i", r=8)
    MT = BS[:, s2 + 1024:s2 + 2048].rearrange("p (r i) -> p r i", r=8)
    s2 += 2048
    ysn = BS[:, s2:s2 + 512]
    yrT = U[:, 0:16 * T].rearrange("p (b t) -> p b t", b=16)
    ysT = yrT
    mTb = U[:, 16 * T:24 * T].rearrange("p (b t) -> p b t", b=8)
    actT = U[:, 0:22 * T].rearrange("p (b t) -> p b t", b=22)
    stage = hT[:, :, :].rearrange("p a b -> p (a b)")
    assert NKB * T >= 4096

    p.dma(cst[:], cst_d[:, :], "cst")
    p.dma(cf[:], cf_d[:, :], "cf")
    p.dma(fnw[:], fnw_d[:, :], "fnw")
    cp(p, "dve", identb[:], identf)
    cp(p, "dve", onesb[:], onesf)
    cp(p, "dve", triTb[:], triT)
    cp(p, "dve", triUb[:], triU)
    for l in layers:
        p.dma(nm[l][:], W[l]["nm"][:, :], "nm%d" % l)
        p.dma(cw[l][:].rearrange("p a b -> p (a b)"), W[l]["cw"][:, :], "cw%d" % l)
        p.dma(hp[l][:], W[l]["hp"][:, :], "hp%d" % l)
        act(p, atile[l][:], hp[l][:, 32:64], AF.Exp)
        ts(p, "dve", atile[l][:], atile[l][:], -1.0, ALU.mult)

    KST = int(os.environ.get('KSTAGE', '99'))
    conv_rr = [0]

    def convert_piece(dst, src, scale, mul=None):
        e = ("dve", "act", "pool")[conv_rr[0] % 3]
        conv_rr[0] += 1
        if scale is None:
            cp(p, e, dst, src)
        elif e == "act":
            act(p, dst, src, AF.Copy, scale=scale)
            if mul is not None:
                ts(p, "dve", dst, dst, mul, ALU.mult)
        elif e == "dve":
            if mul is None:
                ts(p, "dve", dst, src, scale, ALU.mult)
            else:
                ts(p, "dve", dst, src, scale, ALU.mult, mul, ALU.mult)
        else:
            ts(p, "pool", dst, src, scale, ALU.mult, 1.0 if mul is None else mul, ALU.mult)

    def unit_pieces(l, u):
        Wl = W[l]
        nmx = lambda kb: nm[l][:, kb:kb + 1]
        nfx = lambda kb: nm[l][:, 8 + kb:9 + kb]
        nsx = lambda kb: nm[l][:, 16 + kb:17 + kb]
        out = []
        rows = lambda kb: slice(kb * 128, (kb + 1) * 128)
        if u in (U_BCB, U_BCC):
            c0 = C_B if u == U_BCB else C_C
            for kb in range(8):
                out.append((Wl["w_in"][rows(kb), c0:c0 + 512], kb * 512, 512, nmx(kb), None))
        elif 2 <= u < 14:
            h, k = divmod(u - 2, 3)
            for kb in range(8):
                if k == 0:
                    out.append((Wl["w_in"][rows(kb), C_Q + h * 256:C_Q + h * 256 + 256], kb * 512, 256, nmx(kb), None))
                    out.append((Wl["w_in"][rows(kb), C_K + h * 256:C_K + h * 256 + 256], kb * 512 + 256, 256, nmx(kb), 1.0 / 16))
                else:
                    c0 = (C_V if k == 1 else C_G) + h * 512
                    out.append((Wl["w_in"][rows(kb), c0:c0 + 512], kb * 512, 512, nmx(kb), None))
        elif 14 <= u < 22:
            g, k = divmod(u - 14, 2)
            c0 = (C_X if k == 0 else C_Z) + g * 512
            for kb in range(8):
                out.append((Wl["w_in"][rows(kb), c0:c0 + 512], kb * 512, 512, nmx(kb), None))
        elif 22 <= u < 34:
            uu, k = divmod(u - 22, 3)
            if k == 0:
                for kb in range(16):
                    out.append((Wl["ret_out"][rows(kb), uu * 256:uu * 256 + 256], kb * 256, 256, None, None))
            elif k == 1:
                for kb in range(16):
                    out.append((Wl["ssd_out"][rows(kb), uu * 256:uu * 256 + 256], kb * 256, 256, nsx(kb), None))
            else:
                for kb in range(8):
                    out.append((Wl["w_in"][rows(kb), C_GR + uu * 256:C_GR + uu * 256 + 256], kb * 512, 256, nmx(kb), None))
                    out.append((Wl["w_in"][rows(kb), C_GS + uu * 256:C_GS + uu * 256 + 256], kb * 512 + 256, 256, nmx(kb), None))
        elif u in (34, 35):
            uu = u - 34
            for kb in range(8):
                out.append((Wl["w_o"][rows(kb), uu * 512:uu * 512 + 512], kb * 512, 512, None, None))
        elif 36 <= u < 47:
            uu = u - 36
            for kb in range(8):
                out.append((Wl["w_gu"][rows(kb), uu * 256:uu * 256 + 256], kb * 512, 256, nfx(kb), None))
                out.append((Wl["w_gu"][rows(kb), DFF + uu * 256:DFF + uu * 256 + 256], kb * 512 + 256, 256, nfx(kb), None))
        else:
            f = u - 47
            for kb in range(NFF):
                out.append((Wl["w_down"][rows(kb), f * 128:f * 128 + 128], kb * 128, 128, None, None))
        return out

    stages = [(stage, "stage0"), (FS[:, 0:4096], "stage1"), (Rf[:].rearrange("p a b c -> p (a b c)"), "stage2")]

    def convert_layer(l, units):
        n = len(units)
        for i in range(n + 2):
            if i < n:
                st, key = stages[i % 3]
                for q, (src, dc, w, sc, mul) in enumerate(unit_pieces(l, units[i])):
                    p.dma(st[:, dc:dc + w], src, key, eng=("sp" if q % 2 == 0 else "act"))
            j = i - 2
            if 0 <= j < n:
                st, key = stages[j % 3]
                wb = wst[j % NW]
                pcs = unit_pieces(l, units[j])
                for src, dc, w, sc, mul in pcs:
                    convert_piece(wb[:, dc:dc + w], st[:, dc:dc + w], sc, mul)
                hi = max(dc + w for _, dc, w, _, _ in pcs)
                p.dma(W[l]["wu"][units[j], :, 0:hi], wb[:, 0:hi], "wst%d" % (j % NW))
        for kb in range(8):
            p.dma(stage[:, kb * 32:(kb + 1) * 32], W[l]["w_in"][kb * 128:(kb + 1) * 128, C_DT:C_DT + 32], "stage0")
        for kb in range(8):
            ts(p, "dve", wdt[l][:, kb, :], stage[:, kb * 32:(kb + 1) * 32], nm[l][:, kb:kb + 1], ALU.mult)

    P1_UNITS = [U_BCB] + [u for h in range(4) for u in (U_A(h), U_V(h))] + [U_X(g) for g in range(4)]
    P2_UNITS = ([U_BCB, U_BCC] + [u for h in range(4) for u in (U_A(h), U_V(h), U_G(h))]
                + [u for g in range(4) for u in (U_X(g), U_Z(g))]
                + [u for uu in range(4) for u in (U_RO(uu), U_SO(uu), U_GATE(uu))]
                + [U_WO(0), U_WO(1)] + [U_GU(i) for i in range(11)] + [U_WD(f) for f in range(8)])
    for l in layers:
        p.dma(stage[:, 0:256], W[l]["wdts"][:, :], "stage0")
        for kb in range(8):
            ts(p, "dve", wdt[l][:, kb, :], stage[:, kb * 32:(kb + 1) * 32], nm[l][:, kb:kb + 1], ALU.mult)

    class Stream:
        def __init__(self, l, seq):
            self.l, self.seq, self.i, self.slots = l, seq, 0, []

        def next(self, expect):
            while len(self.slots) < min(len(self.seq), self.i + NW - 1):
                u = self.seq[len(self.slots)]
                slot = wslot[0] % NW
                wslot[0] += 1
                p.dma(wst[slot][:], W[self.l]["wu"][u, :, :], "wst%d" % slot)
                self.slots.append(slot)
            assert self.seq[self.i] == expect, (self.seq[self.i], expect)
            w = wst[self.slots[self.i]]
            self.i += 1
            return w

    wslot = [0]

    def rmsnorm_to_hnT(l_unused=None):
        act(p, hnT[:], hT[:], AF.Square)
        for kb in range(NKB):
            mm(p, pM[:, 0:T], onesb[:], hnT[:, kb, :], kb == 0, kb == NKB - 1)
        act(p, lnv[:], pM[:, 0:T], AF.Ln, scale=1.0 / D, bias=EPS)
        act(p, rbc[:], lnv[:], AF.Exp, scale=-0.5)
        tt(p, "dve", hnT[:], hT[:], bc(rbc[:], 1, [128, NKB, T]), ALU.mult)

    def halo_norm(halo_src):
        p.dma(haloT[:], halo_src.rearrange("b p t -> p b t"), "haloT")
        act(p, hnTh[:], haloT[:], AF.Square)
        for kb in range(NKB):
            mm(p, pM[:, 256:260], onesb[:], hnTh[:, kb, :], kb == 0, kb == NKB - 1)
        act(p, small[:, 0:4], pM[:, 256:260], AF.Ln, scale=1.0 / D, bias=EPS)
        act(p, small[:, 4:8], small[:, 0:4], AF.Exp, scale=-0.5)
        tt(p, "dve", hnTh[:], haloT[:], bc(small[:, 4:8], 1, [128, NKB, 4]), ALU.mult)

    def proj_fm(w, c0, pa):
        for kb in range(NKB):
            mm(p, pa[:, 0:T], w[:, kb * 512 + c0:kb * 512 + c0 + 128], hnT[:, kb, :], kb == 0, kb == NKB - 1)

    def proj_tm(w, c, pa, width=512, c0=0):
        for kb in range(NKB):
            mm(p, pa[:, 0:width], hnT[:, kb, c * 128:(c + 1) * 128], w[:, kb * 512 + c0:kb * 512 + c0 + width],
               kb == 0, kb == NKB - 1)

    def conv_block(l, t, w, c0, blk, pa, out_bf):
        ra, rb, xraw = nxt()
        if t == 0:
            for kb in range(NKB):
                mm(p, pM[:, 260:264], w[:, kb * 512 + c0:kb * 512 + c0 + 128], hnTh[:, kb, :], kb == 0, kb == NKB - 1)
            cp(p, "act", xraw[:, 0:3], pM[:, 260:263])
        else:
            cp(p, "pool", xraw[:, 0:3], hist[:, blk, 0:3])
        cp(p, "act", xraw[:, 3:3 + T], pa[:, 0:T])
        cp(p, "pool", hist[:, blk, 0:3], xraw[:, T:T + 3])
        act(p, ra, pa[:, 0:T], AF.Identity, scale=cw[l][:, blk, 3:4], bias=cw[l][:, blk, 4:5])
        stt(p, rb, xraw[:, 0:T], cw[l][:, blk, 0:1], ra, ALU.mult, ALU.add)
        stt(p, ra, xraw[:, 1:1 + T], cw[l][:, blk, 1:2], rb, ALU.mult, ALU.add)
        stt(p, rb, xraw[:, 2:2 + T], cw[l][:, blk, 2:3], ra, ALU.mult, ALU.add)
        act(p, out_bf, rb, AF.Silu)

    def dt_prep(l, c, full):
        d = dtb[:, c, :]
        KD = int(os.environ.get('KDT', '63'))
        if KD & 1:
            for kb in range(NKB):
                mm(p, pM[:, 0:32], hnT[:, kb, c * 128:(c + 1) * 128], wdt[l][:, kb, :], kb == 0, kb == NKB - 1)
            tt(p, "dve", d[:, 0:32], pM[:, 0:32], hp[l][:, 0:32], ALU.add)
        if KD & 2:
            act(p, d[:, 32:64], d[:, 0:32], AF.Exp)
            act(p, d[:, 64:96], d[:, 32:64], AF.Ln, scale=1.0, bias=1.0)
            tt(p, "dve", d[:, 96:128], d[:, 64:96], atile[l][:], ALU.mult)
        if KD & 4:
            cp(p, "dve", lahl[:, c, 0, :], d[:, 96:128])
            tt(p, "dve", lahl[:, c, 1, :], d[:, 96:128], lahl[:, c, 0, :], ALU.subtract)
            for k, lt in enumerate((triTb, triUb, onesb)):
                mm(p, pM[:, 32 + 32 * k:64 + 32 * k], lt[:], lahl[:, c, 0, :], True, False)
                mm(p, pM[:, 32 + 32 * k:64 + 32 * k], lt[:], lahl[:, c, 1, :], False, True)
        if KD & 8:
            act(p, d[:, 128:224], pM[:, 32:128], AF.Exp)
        if KD & 32:
            tt(p, "dve", d[:, 224:256], d[:, 64:96], d[:, 160:192], ALU.mult)
        if (KD & 16) and not full:
            tt(p, "dve", LAt[:], LAt[:], pM[:, 96:128], ALU.add)

    def tile_body(l, t, full, src, dst, ws, last_layer):
        tsl = slice(t * T, (t + 1) * T)
        p.dma(hT[:], src[:, :, tsl].rearrange("b p t -> p b t"), "hT")
        p.dma(cossin[:], cs_d[:, :, tsl].rearrange("a p t -> p a t"), "cossin")
        rmsnorm_to_hnT()
        if KST < 5:
            raise _Stop()
        cosT, sinT = cossin[:, 0, :], cossin[:, 1, :]
        for c in range(NCH):
            dt_prep(l, c, full)
        if KST < 6:
            raise _Stop()
        w = ws.next(U_BCB)
        for g in range(4):
            proj_fm(w, g * 128, pA[g % 2])
            conv_block(l, t, w, g * 128, 16 + g, pA[g % 2], BT[:, g, :])
        if full:
            w = ws.next(U_BCC)
            for g in range(4):
                proj_fm(w, g * 128, pA[g % 2])
                conv_block(l, t, w, g * 128, 20 + g, pA[g % 2], CT[:, g, :])
        if not full:
            kTv = U[:, 0:2 * T].rearrange("p (b t) -> p b t", b=2)
            vtv = U[:, 2 * T:2 * T + NCH * 512].rearrange("p (c f) -> p c f", c=NCH)
            ktv = U[:, 2 * T + NCH * 512:2 * T + NCH * 512 + 256]
            r3 = lambda ap: ap.rearrange("p (r q) -> p r q", r=8)
            for i in range(4):
                w = ws.next(U_A(i))
                proj_fm(w, 256, pA[0])
                proj_fm(w, 384, pA[1])
                ra, rb, _ = nxt()
                tt(p, "dve", ra, pA[0][:, 0:T], cosT, ALU.mult)
                tt(p, "dve", rb, pA[1][:, 0:T], sinT, ALU.mult)
                tt(p, "pool", kTv[:, 0, :], ra, rb, ALU.subtract)
                ra, rb, _ = nxt()
                tt(p, "dve", ra, pA[0][:, 0:T], sinT, ALU.mult)
                tt(p, "dve", rb, pA[1][:, 0:T], cosT, ALU.mult)
                tt(p, "pool", kTv[:, 1, :], ra, rb, ALU.add)
                w = ws.next(U_V(i))
                for c in range(NCH):
                    proj_tm(w, c, pA[c % 2])
                    cp(p, "act", vtv[:, c, :], pA[c % 2][:, :])
                w = ws.next(U_X(i))
                for fb in range(4):
                    proj_fm(w, fb * 128, pA[fb % 2])
                    conv_block(l, t, w, fb * 128, i * 4 + fb, pA[fb % 2], xcT[:, fb, :])
                for c in range(NCH):
                    cs_ = slice(c * 128, (c + 1) * 128)
                    d = dtb[:, c, :]
                    for b in range(2):
                        tr(p, pT[1][:, b * 128:(b + 1) * 128], kTv[:, b, cs_], identb[:], inc=(b == 1))
                    for fb in range(4):
                        tr(p, pT[0][:, fb * 128:(fb + 1) * 128], xcT[:, fb, cs_], identb[:], inc=False)
                    tr(p, pT[0][:, 512:640], BT[:, i, cs_], identb[:])
                    ts(p, "dve", ktv, pT[1][:, 0:256], kdec[:, i:i + 1], ALU.mult)
                    tt(p, "dve", r3(xdts), pT[0][:, 0:512].rearrange("p (r q) -> p r q", r=8),
                       bc(d[:, 224 + 8 * i:232 + 8 * i], 2, [128, 8, 64]), ALU.mult)
                    cp(p, "act", Btok, pT[0][:, 512:640])
                    for b in range(2):
                        mm(p, pS[b][:, :], ktv[:, b * 128:(b + 1) * 128], vtv[:, c, :], True, True)
                    mm(p, pY[:, :], Btok, xdts, True, True)
                    for b in range(2):
                        stt(p, Rf[:, i, b, :], Rf[:, i, b, :], GAMMA[i] ** 128, pS[b][:, :], ALU.mult, ALU.add)
                    tt(p, "pool", r3(Sf[:, i, :]), r3(Sf[:, i, :]), bc(d[:, 192 + 8 * i:200 + 8 * i], 2, [128, 8, 64]), ALU.mult)
                    tt(p, "dve", Sf[:, i, :], Sf[:, i, :], pY[:, :], ALU.add)
            return
        if KST < 7:
            raise _Stop()
        for h in range(NH):
            w = ws.next(U_A(h))
            for qk in ((0, 1) if full else (1,)):
                dstT = qT if qk == 0 else kT
                ra, rb, _ = nxt()
                proj_fm(w, qk * 256, pA[0])
                proj_fm(w, qk * 256 + 128, pA[1])
                tt(p, "dve", ra, pA[0][:, 0:T], cosT, ALU.mult)
                tt(p, "dve", rb, pA[1][:, 0:T], sinT, ALU.mult)
                tt(p, "pool", dstT[:, 0, :], ra, rb, ALU.subtract)
                tt(p, "dve", ra, pA[0][:, 0:T], sinT, ALU.mult)
                tt(p, "dve", rb, pA[1][:, 0:T], cosT, ALU.mult)
                tt(p, "pool", dstT[:, 1, :], ra, rb, ALU.add)
            w = ws.next(U_V(h))
            for c in range(NCH):
                proj_tm(w, c, pA[c % 2])
                cp(p, "act", vtok[:, c, :], pA[c % 2][:, :])
            if full:
                w = ws.next(U_G(h))
                for fb in range(4):
                    proj_fm(w, fb * 128, pA[fb % 2])
                    act(p, gT[:, fb, :], pA[fb % 2][:, 0:T], AF.Silu)
            for c in range(NCH):
                cs_ = slice(c * 128, (c + 1) * 128)
                if full:
                    for b in range(2):
                        mm(p, pM[:, 128:256], kT[:, b, cs_], qT[:, b, cs_], b == 0, b == 1)
                    tt(p, "dve", sT, pM[:, 128:256], intraT[:, h, :], ALU.mult)
                    tt(p, "pool", qTs, qT[:, :, cs_], bc(qdec[:, h, :], 1, [128, 2, 128]), ALU.mult)
                    mm(p, pY[:, :], sT, vtok[:, c, :], True, False)
                    mm(p, pY[:, :], qTs[:, 0, :], Rbf[:, h, 0, :], False, False)
                    mm(p, pY[:, :], qTs[:, 1, :], Rbf[:, h, 1, :], False, True)
                    act(p, ysn, pY[:, :], AF.Square, accum=small[:, 8:9])
                    act(p, small[:, 9:10], small[:, 8:9], AF.Ln, scale=1.0 / 512, bias=EPS)
                    act(p, small[:, 10:11], small[:, 9:10], AF.Exp, scale=-0.5)
                    ts(p, "dve", yn, pY[:, :], small[:, 10:11], ALU.mult)
                    for fb in range(4):
                        tr(p, pT[0][:, fb * 128:(fb + 1) * 128], yn[:, fb * 128:(fb + 1) * 128], identb[:], inc=(fb == 3))
                    tt(p, "dve", yrT[:, h * 4:(h + 1) * 4, cs_], pT[0][:, 0:512].rearrange("p (a b) -> p a b", a=4),
                       gT[:, :, cs_], ALU.mult)
                for b in range(2):
                    tr(p, pT[1][:, b * 128:(b + 1) * 128], kT[:, b, cs_], identb[:], inc=(b == 1))
                ts(p, "dve", ktok, pT[1][:, 0:256], kdec[:, h:h + 1], ALU.mult)
                for b in range(2):
                    mm(p, pS[b][:, :], ktok[:, b * 128:(b + 1) * 128], vtok[:, c, :], True, True)
                for b in range(2):
                    stt(p, Rf[:, h, b, :], Rf[:, h, b, :], GAMMA[h] ** 128, pS[b][:, :], ALU.mult, ALU.add)
                    if full:
                        cp(p, "act", Rbf[:, h, b, :], Rf[:, h, b, :])
        if KST < 8:
            raise _Stop()
        if full:
            out_proj(l, ws, first=True)
        if KST < 9:
            raise _Stop()
        for g in range(4):
            w = ws.next(U_X(g))
            for fb in range(4):
                proj_fm(w, fb * 128, pA[fb % 2])
                conv_block(l, t, w, fb * 128, g * 4 + fb, pA[fb % 2], xcT[:, fb, :])
            if full:
                w = ws.next(U_Z(g))
                for c in range(NCH):
                    proj_tm(w, c, pA[c % 2])
                    act(p, sz[:, c, :], pA[c % 2][:, :], AF.Silu)
            hs = slice(8 * g, 8 * g + 8)
            for c in range(NCH):
                cs_ = slice(c * 128, (c + 1) * 128)
                d = dtb[:, c, :]
                for fb in range(4):
                    tr(p, pT[0][:, fb * 128:(fb + 1) * 128], xcT[:, fb, cs_], identb[:], inc=(fb == 3))
                xps = pT[0][:, 0:512].rearrange("p (r q) -> p r q", r=8)
                r3 = lambda ap: ap.rearrange("p (r q) -> p r q", r=8)
                tt(p, "dve", r3(xdts), xps, bc(d[:, 224 + 8 * g:232 + 8 * g], 2, [128, 8, 64]), ALU.mult)
                tr(p, pT[1][:, 256:384], BT[:, g, cs_], identb[:])
                cp(p, "act", Btok, pT[1][:, 256:384])
                if full:
                    cp(p, "act", xtok, pT[0][:, 0:512])
                    tt(p, "dve", r3(xdt), xps, bc(d[:, 64 + 8 * g:72 + 8 * g], 2, [128, 8, 64]), ALU.mult)
                    mm(p, pM[:, 256:384], BT[:, g, cs_], CT[:, g, cs_], True, True)
                    tt(p, "dve", cbm, pM[:, 256:384], triT, ALU.mult)
                    for k in range(2):
                        tt(p, "pool", rlab[:, k, :, :], bc(triTb[:], 1, [128, 8, 128]),
                           bc(lahl[:, c, k, 8 * g:8 * g + 8], 2, [128, 8, 128]), ALU.mult)
                    for hf in range(2):
                        for k in range(2):
                            mm(p, pA[hf][:, :], triUb[:], rlab[:, k, hf * 4:(hf + 1) * 4, :].rearrange("p r i -> p (r i)"), k == 0, k == 1)
                        act(p, esb[:, hf * 4:(hf + 1) * 4, :].rearrange("p r i -> p (r i)"), pA[hf][:, :], AF.Exp)
                    tt(p, "dve", MT, esb, bc(cbm, 1, [128, 8, 128]), ALU.mult)
                    tt(p, "pool", r3(xD[:]), r3(xtok), bc(hp[l][:, 64 + 8 * g:72 + 8 * g], 2, [128, 8, 64]), ALU.mult)
                    mm(p, pY[:, :], identb[:], xD[:], True, False)
                    for r in range(8):
                        mm(p, pY[:, r * 64:(r + 1) * 64], MT[:, r, :], xdt[:, r * 64:(r + 1) * 64], False, r == 7)
                    mm(p, pS[1][:, :], CT[:, g, cs_], Sbf[:, g, :], True, True)
                    tt(p, "dve", r3(ubuf), r3(pS[1][:, :]), bc(d[:, 128 + 8 * g:136 + 8 * g], 2, [128, 8, 64]), ALU.mult)
                    tt(p, "dve", ubuf, ubuf, pY[:, :], ALU.add)
                    tt(p, "dve", ubuf, ubuf, sz[:, c, :], ALU.mult)
                    act(p, ysn, ubuf, AF.Square, accum=small[:, 12:13])
                    act(p, small[:, 13:14], small[:, 12:13], AF.Ln, scale=1.0 / 512, bias=EPS)
                    act(p, small[:, 14:15], small[:, 13:14], AF.Exp, scale=-0.5)
                    ts(p, "dve", ysn, ubuf, small[:, 14:15], ALU.mult)
                    for fb in range(4):
                        tr(p, pT[1][:, 512 + fb * 128:512 + (fb + 1) * 128], ysn[:, fb * 128:(fb + 1) * 128], identb[:], inc=(fb == 3))
                    cp(p, "act", ysT[:, g * 4:(g + 1) * 4, cs_], pT[1][:, 512:1024].rearrange("p (a b) -> p a b", a=4))
                mm(p, pS[0][:, :], Btok, xdts, True, True)
                tt(p, "dve", r3(Sf[:, g, :]), r3(Sf[:, g, :]), bc(d[:, 192 + 8 * g:200 + 8 * g], 2, [128, 8, 64]), ALU.mult)
                tt(p, "dve", Sf[:, g, :], Sf[:, g, :], pS[0][:, :], ALU.add)
                if full:
                    cp(p, "act", Sbf[:, g, :], Sf[:, g, :])
        if not full:
            return
        if KST < 10:
            raise _Stop()
        out_proj(l, ws, first=False)
        if KST < 11:
            raise _Stop()
        for uu in range(2):
            w = ws.next(U_WO(uu))
            for f in range(4):
                fb = uu * 4 + f
                pa = pA[f % 2]
                for kb in range(NKB):
                    mm(p, pa[:, 0:T], w[:, kb * 512 + f * 128:kb * 512 + (f + 1) * 128], mTb[:, kb, :], kb == 0, kb == NKB - 1)
                tt(p, "dve", hT[:, fb, :], hT[:, fb, :], pa[:, 0:T], ALU.add)
        if KST < 12:
            raise _Stop()
        rmsnorm_to_hnT()
        for uu in range(11):
            w = ws.next(U_GU(uu))
            for j in range(2):
                proj_fm(w, j * 128, pA[0])
                proj_fm(w, 256 + j * 128, pA[1])
                ra, _, _ = nxt()
                act(p, ra, pA[0][:, 0:T], AF.Silu)
                tt(p, "dve", actT[:, 2 * uu + j, :], ra, pA[1][:, 0:T], ALU.mult)
        for fb in range(8):
            w = ws.next(U_WD(fb))
            pa = pA[fb % 2]
            for kb in range(NFF):
                mm(p, pa[:, 0:T], w[:, kb * 128:(kb + 1) * 128], actT[:, kb, :], kb == 0, kb == NFF - 1)
            tt(p, "dve", hT[:, fb, :], hT[:, fb, :], pa[:, 0:T], ALU.add)
        if KST < 13:
            raise _Stop()
        if last_layer:
            act(p, hnT[:], hT[:], AF.Square)
            for kb in range(NKB):
                mm(p, pM[:, 0:T], onesb[:], hnT[:, kb, :], kb == 0, kb == NKB - 1)
            act(p, lnv[:], pM[:, 0:T], AF.Ln, scale=1.0 / D, bias=EPS)
            act(p, rbc[:], lnv[:], AF.Exp, scale=-0.5)
            for kb in range(NKB):
                stt(p, hT[:, kb, :], hT[:, kb, :], fnw[:, kb:kb + 1], rbc[:], ALU.mult, ALU.mult)
        p.dma(dst[:, :, tsl].rearrange("b p t -> p b t"), hT[:], "hT")

    def out_proj(l, ws_, first):
        for uu in range(4):
            w = ws_.next(U_RO(uu) if first else U_SO(uu))
            wg = ws_.next(U_GATE(uu))
            for j in range(2):
                fb = uu * 2 + j
                for kb in range(16):
                    mm(p, pY[:, 0:T], w[:, kb * 256 + j * 128:kb * 256 + (j + 1) * 128], yrT[:, kb, :], kb == 0, kb == 15)
                proj_fm(wg, (0 if first else 256) + j * 128, pA[j])
                ra, rb, _ = nxt()
                act(p, ra, pA[j][:, 0:T], AF.Sigmoid)
                if first:
                    tt(p, "dve", mTb[:, fb, :], ra, pY[:, 0:T], ALU.mult)
                else:
                    tt(p, "dve", rb, ra, pY[:, 0:T], ALU.mult)
                    tt(p, "pool", mTb[:, fb, :], mTb[:, fb, :], rb, ALU.add)

    def p2_seq():
        s = [U_BCB, U_BCC]
        for h in range(4):
            s += [U_A(h), U_V(h), U_G(h)]
        for uu in range(4):
            s += [U_RO(uu), U_GATE(uu)]
        for g in range(4):
            s += [U_X(g), U_Z(g)]
        for uu in range(4):
            s += [U_SO(uu), U_GATE(uu)]
        s += [U_WO(0), U_WO(1)] + [U_GU(i) for i in range(11)] + [U_WD(f) for f in range(8)]
        return s

    def p1_seq():
        s = [U_BCB]
        for h in range(4):
            s += [U_A(h), U_V(h), U_X(h)]
        return s

    try:
        for kind, l in phases:
            src = hin
            if KST < 3:
                break
            halo_norm(halo_in)
            if KST < 4:
                break
            if kind == "p1":
                p.op("pool", lambda e: e.memset(Rf[:].rearrange("p a b c -> p (a b c)"), 0.0), [], [Rf[:]])
                p.op("pool", lambda e: e.memset(Sf[:].rearrange("p a b -> p (a b)"), 0.0), [], [Sf[:]])
                p.op("pool", lambda e: e.memset(LAt[:], 0.0), [], [LAt[:]])
                ws = Stream(l, p1_seq() * NTILE)
                for t in range(NTILE):
                    tile_body(l, t, False, src, None, ws, False)
                for h in range(NH):
                    for b in range(2):
                        p.dma(stloc[h * 2 + b, :, :], Rf[:, h, b, :], "Rf")
                for g in range(4):
                    p.dma(stloc[8 + g, :, :], Sf[:, g, :], "Sf")
                p.op("pool", lambda e: e.memset(rbc[:, 0:512], 0.0), [], [rbc[:, 0:512]])
                cp(p, "dve", rbc[:, 0:32], LAt[:])
                p.dma(stloc[12, :, :], rbc[:, 0:512], "rbc")
            else:
                for blk in range(8):
                    h, b = divmod(blk, 2)
                    p.dma(FS[:, 0:2048].rearrange("p (j f) -> p j f", j=4), stall[:, blk, :, :].rearrange("j p f -> p j f"), "FS")
                    ts(p, "dve", Rf[:, h, b, :], FS[:, 0:512], cf[:, h * 4:h * 4 + 1], ALU.mult)
                    for j in range(1, 4):
                        stt(p, Rf[:, h, b, :], FS[:, j * 512:(j + 1) * 512], cf[:, h * 4 + j:h * 4 + j + 1], Rf[:, h, b, :], ALU.mult, ALU.add)
                    cp(p, "act", Rbf[:, h, b, :], Rf[:, h, b, :])
                LAa = FS[:, 2048:2048 + 128].rearrange("p (j f) -> p j f", j=4)
                p.dma(LAa, stall[:, 12, :, 0:32].rearrange("j p f -> p j f"), "FSla")
                Ej = FS[:, 2176:2176 + 128].rearrange("p (j f) -> p j f", j=4)
                for j in range(4):
                    ts(p, "dve", Ej[:, j, :], LAa[:, 0, :], cf[:, 20 + j * 4:21 + j * 4], ALU.mult)
                    for m in range(1, 4):
                        stt(p, Ej[:, j, :], LAa[:, m, :], cf[:, 20 + j * 4 + m:21 + j * 4 + m], Ej[:, j, :], ALU.mult, ALU.add)
                    act(p, Ej[:, j, :], Ej[:, j, :], AF.Exp)
                    ts(p, "dve", Ej[:, j, :], Ej[:, j, :], cf[:, 16 + j:17 + j], ALU.mult)
                for g in range(4):
                    p.dma(FS[:, 0:2048].rearrange("p (j f) -> p j f", j=4), stall[:, 8 + g, :, :].rearrange("j p f -> p j f"), "FS")
                    r3 = lambda ap: ap.rearrange("p (r q) -> p r q", r=8)
                    tt(p, "dve", r3(Sf[:, g, :]), r3(FS[:, 0:512]), bc(Ej[:, 0, 8 * g:8 * g + 8], 2, [128, 8, 64]), ALU.mult)
                    for j in range(1, 4):
                        tt(p, "dve", r3(FS[:, j * 512:(j + 1) * 512]), r3(FS[:, j * 512:(j + 1) * 512]),
                           bc(Ej[:, j, 8 * g:8 * g + 8], 2, [128, 8, 64]), ALU.mult)
                        tt(p, "dve", Sf[:, g, :], Sf[:, g, :], FS[:, j * 512:(j + 1) * 512], ALU.add)
                    cp(p, "act", Sbf[:, g, :], Sf[:, g, :])
                ws = Stream(l, p2_seq() * NTILE)
                for t in range(NTILE):
                    tile_body(l, t, True, src, hout, ws, (l == DEPTH - 1) and not globals().get('_NOFINAL', False))
    except _Stop:
        pass
    if p.log is not None:
        with open(os.environ['KDUMP'], 'w') as f:
            for r in p.log:
                f.write(repr(r) + '\n')
    p.finish()
    es.close()
    return nc


NSLOT = 14


def build_convert():
    nc = bass.Bass("TRN2", target_bir_lowering=False)
    es = ExitStack()
    p = Prog(nc, es)
    src = nc.dram_tensor("usrc", [NSLOT, 128, 4096], F32, kind="ExternalInput").ap()
    scd = nc.dram_tensor("usc", [128, NSLOT * 32], F32, kind="ExternalInput").ap()
    dst = nc.dram_tensor("wpart", [NSLOT, 128, 4096], BF16, kind="ExternalOutput").ap()
    sc = es.enter_context(nc.sbuf_tensor("sc_sb", [128, NSLOT, 32], F32))
    stg = [es.enter_context(nc.sbuf_tensor("stg%d" % i, [128, 32, 128], F32)) for i in range(3)]
    wb = [es.enter_context(nc.sbuf_tensor("wb%d" % i, [128, 32, 128], BF16)) for i in range(3)]
    p.dma(sc[:].rearrange("p a b -> p (a b)"), scd[:, :], "sc")
    for i in range(NSLOT + 2):
        if i < NSLOT:
            p.dma(stg[i % 3][:].rearrange("p a b -> p (a b)"), src[i, :, :], "stg%d" % (i % 3), eng=("sp" if i % 2 == 0 else "act"))
        j = i - 2
        if 0 <= j < NSLOT:
            st, o = stg[j % 3], wb[j % 3]
            tt(p, "dve", o[:, 0:16, :], st[:, 0:16, :], bc(sc[:, j, 0:16], 2, [128, 16, 128]), ALU.mult)
            tt(p, "pool", o[:, 16:32, :], st[:, 16:32, :], bc(sc[:, j, 16:32], 2, [128, 16, 128]), ALU.mult)
            if j == 0:
                kv = o[:].rearrange("p (a b) c -> p a b c", b=4)[:, :, 2:4, :]
                ts(p, "dve", kv, kv, 1.0 / 16, ALU.mult)
            p.dma(dst[j, :, :], o[:].rearrange("p a b -> p (a b)"), "wb%d" % (j % 3))
    p.finish()
    es.close()
    return nc


def _unit_src(l, u, P):
    data = np.zeros((128, 4096), np.float32)
    sc = np.ones((128, 32), np.float32)
    nmx = P["norm_mix_w"][l].reshape(8, 128).T
    nfx = P["norm_ffn_w"][l].reshape(8, 128).T
    nsx = P["ssd_norm_w"][l].reshape(16, 128).T

    def put(mat, kb, c0, w, dc, scv):
        data[:, dc:dc + w] = mat[kb * 128:(kb + 1) * 128, c0:c0 + w]
        if scv is not None:
            sc[:, dc // 128:(dc + w) // 128] = scv[:, kb:kb + 1]

    w_in = P["w_in"][l]
    if u in (U_BCB, U_BCC):
        c0 = C_B if u == U_BCB else C_C
        for kb in range(8):
            put(w_in, kb, c0, 512, kb * 512, nmx)
    elif 2 <= u < 14:
        h, k = divmod(u - 2, 3)
        for kb in range(8):
            if k == 0:
                put(w_in, kb, C_Q + h * 256, 256, kb * 512, nmx)
                put(w_in, kb, C_K + h * 256, 256, kb * 512 + 256, nmx)
            else:
                put(w_in, kb, (C_V if k == 1 else C_G) + h * 512, 512, kb * 512, nmx)
    elif 14 <= u < 22:
        g, k = divmod(u - 14, 2)
        for kb in range(8):
            put(w_in, kb, (C_X if k == 0 else C_Z) + g * 512, 512, kb * 512, nmx)
    elif 22 <= u < 34:
        uu, k = divmod(u - 22, 3)
        if k == 0:
            for kb in range(16):
                put(P["ret_out"][l], kb, uu * 256, 256, kb * 256, None)
        elif k == 1:
            for kb in range(16):
                put(P["ssd_out"][l], kb, uu * 256, 256, kb * 256, nsx)
        else:
            for kb in range(8):
                put(w_in, kb, C_GR + uu * 256, 256, kb * 512, nmx)
                put(w_in, kb, C_GS + uu * 256, 256, kb * 512 + 256, nmx)
    elif u in (34, 35):
        for kb in range(8):
            put(P["w_o"][l], kb, (u - 34) * 512, 512, kb * 512, None)
    elif 36 <= u < 47:
        uu = u - 36
        for kb in range(8):
            put(P["w_gate_up"][l], kb, uu * 256, 256, kb * 512, nfx)
            put(P["w_gate_up"][l], kb, DFF + uu * 256, 256, kb * 512 + 256, nfx)
    else:
        f = u - 47
        for kb in range(NFF):
            put(P["w_down"][l], kb, f * 128, 128, kb * 128, None)
    return data, sc


def _convert_weights(P):
    a_units = [(l, U_A(h)) for l in range(DEPTH) for h in range(NH)]
    rest = [(l, u) for l in range(DEPTH) for u in range(NUNIT) if (l, u) not in a_units]
    assign = []
    for c in range(8):
        mine = [a_units[c]] + rest[c * (NSLOT - 1):(c + 1) * (NSLOT - 1)]
        assign.append(mine)
    maps = []
    for c in range(8):
        usrc = np.zeros((NSLOT, 128, 4096), np.float32)
        usc = np.ones((128, NSLOT, 32), np.float32)
        for i, (l, u) in enumerate(assign[c]):
            usrc[i], usc[:, i, :] = _unit_src(l, u, P)
        maps.append({"usrc": usrc, "usc": np.ascontiguousarray(usc.reshape(128, NSLOT * 32))})
    res = run_bass_kernel_spmd(build_convert(), maps, core_ids=list(range(8)))
    first = np.asarray(res.results[0]["wpart"])
    wu = [np.zeros((NUNIT, 128, 4096), first.dtype) for _ in range(DEPTH)]
    for c in range(8):
        part = np.asarray(res.results[c]["wpart"])
        for i, (l, u) in enumerate(assign[c]):
            wu[l][u] = part[i]
    return wu


def _consts():
    j = np.arange(128)
    cst = np.zeros((128, 1540), np.float32)
    cst[:, 0:128] = np.eye(128)
    cst[:, 128:256] = (j[:, None] <= j[None, :])
    cst[:, 256:384] = (j[:, None] > j[None, :])
    cst[:, 384:512] = 1.0
    for h in range(NH):
        g = GAMMA[h]
        diff = j[None, :] - j[:, None]
        cst[:, 512 + h * 128:512 + (h + 1) * 128] = np.where(diff >= 0, g ** np.maximum(diff, 0).astype(np.float64), 0.0)
        cst[:, 1024 + h * 128:1024 + (h + 1) * 128] = (g ** (j + 1.0))[None, :]
        cst[:, 1536 + h] = g ** (127.0 - j)
    return cst


def _rope(pos):
    inv = np.float32(10000.0) ** (-(np.arange(128, dtype=np.float32) / np.float32(128)))
    ang = pos.astype(np.float32)[:, None] * inv[None, :].astype(np.float32)
    return np.stack([np.cos(ang).T, np.sin(ang).T]).astype(np.float32)


def _coef(s, NT):
    cf = np.zeros((128, 36), np.float32)
    for h in range(NH):
        for j in range(4):
            if j < s:
                cf[:, h * 4 + j] = GAMMA[h] ** (float(NT) * (s - 1 - j))
    for j in range(4):
        cf[:, 16 + j] = 1.0 if j < s else 0.0
        for m in range(4):
            cf[:, 20 + j * 4 + m] = 1.0 if (j < m < s) else 0.0
    return cf


def _fm(a):
    return np.ascontiguousarray(a.T.reshape(NKB, 128, a.shape[0]))


def _layer_inputs(l, P):
    nmv = np.concatenate([P["norm_mix_w"][l].reshape(8, 128).T, P["norm_ffn_w"][l].reshape(8, 128).T,
                          P["ssd_norm_w"][l].reshape(16, 128).T], axis=1)
    cwv = np.zeros((128, 24, 5), np.float32)
    cwv[:, :, 0:4] = P["conv_w"][l].reshape(4, 24, 128).transpose(2, 1, 0)
    cwv[:, :, 4] = P["conv_b"][l].reshape(24, 128).T
    hpv = np.concatenate([np.broadcast_to(P[k][l][None, :], (128, 32)) for k in ("dt_bias", "a_log", "d_skip")], axis=1)
    wdts = P["w_in"][l][:, C_DT:C_DT + 32].reshape(8, 128, 32).transpose(1, 0, 2).reshape(128, 256)
    return {
        "nm%d" % l: np.ascontiguousarray(nmv, np.float32), "cw%d" % l: np.ascontiguousarray(cwv.reshape(128, 120)),
        "hp%d" % l: np.ascontiguousarray(hpv, np.float32), "wdts%d" % l: np.ascontiguousarray(wdts, np.float32),
    }


_T_DEFAULT = 512


def kernel(x, norm_mix_w, w_in, ret_out, conv_w, conv_b, dt_bias, a_log, d_skip, ssd_norm_w,
           ssd_out, w_o, norm_ffn_w, w_gate_up, w_down, final_norm_w, _T=None):
    P = dict(norm_mix_w=norm_mix_w, w_in=w_in, ret_out=ret_out, conv_w=conv_w, conv_b=conv_b, dt_bias=dt_bias,
             a_log=a_log, d_skip=d_skip, ssd_norm_w=ssd_norm_w, ssd_out=ssd_out, w_o=w_o, norm_ffn_w=norm_ffn_w,
             w_gate_up=w_gate_up, w_down=w_down)
    P = {k: np.asarray(v, np.float32) for k, v in P.items()}
    x = np.asarray(x, np.float32)
    B, S, _ = x.shape
    NSEG = 4
    NT = S // NSEG
    T = _T or min(_T_DEFAULT, NT)
    ncore = B * NSEG
    assert ncore == 8
    cst = _consts()
    fnw = np.ascontiguousarray(np.asarray(final_norm_w, np.float32).reshape(8, 128).T)
    common = []
    for c in range(ncore):
        b, s = divmod(c, NSEG)
        common.append({"cst": cst, "cs": _rope(np.arange(s * NT, (s + 1) * NT)), "cf": _coef(s, NT), "fnw": fnw})

    def halo_of(hfm_prev):
        h = np.zeros((NKB, 128, 4), np.float32)
        if hfm_prev is not None:
            h[:, :, 0:3] = hfm_prev[:, :, -3:]
        return h

    hcur = [_fm(x[c // NSEG, (c % NSEG) * NT:((c % NSEG) + 1) * NT, :]) for c in range(ncore)]
    wu = _convert_weights(P)
    for l in range(DEPTH):
        li = _layer_inputs(l, P)
        li["wu%d" % l] = wu[l]
        halos = [halo_of(hcur[c - 1] if c % NSEG else None) for c in range(ncore)]
        nc1 = build(NT, T, [("p1", l)])
        maps = [dict(common[c], hin=hcur[c], halo=halos[c], **li) for c in range(ncore)]
        r1 = run_bass_kernel_spmd(nc1, maps, core_ids=list(range(ncore)))
        st = [np.asarray(r["stloc"]) for r in r1.results]
        nc2 = build(NT, T, [("p2", l)])
        maps = []
        for c in range(ncore):
            b = c // NSEG
            stall = np.stack([st[b * NSEG + j] for j in range(NSEG)])
            maps.append(dict(common[c], hin=hcur[c], halo=halos[c], stall=stall, **li))
        r2 = run_bass_kernel_spmd(nc2, maps, core_ids=list(range(ncore)))
        hcur = [np.asarray(r["hout"]) for r in r2.results]
    out = np.empty((B, S, D), np.float32)
    for c in range(ncore):
        b, s = divmod(c, NSEG)
        out[b, s * NT:(s + 1) * NT, :] = hcur[c].reshape(D, NT).T
    return out
```
